# Optimizing a Trainium2 kernel written in Bass

```python
import jax, jax.numpy as jnp
from jax import lax
import numpy as np

D_MODEL = 1024
BATCH = 32
SEQ = 2048
DEPTH = 4
DEC_BATCH = 16
DEC_SEQ = 64
PAST_LEN = 2048

CHUNK = 64
HEAD_DIM = 64
A_HEADS = D_MODEL // (2 * HEAD_DIM)
B_HEADS = D_MODEL // (2 * HEAD_DIM)
A_WIDTH = A_HEADS * HEAD_DIM
B_WIDTH = B_HEADS * HEAD_DIM
MIX_WIDTH = A_WIDTH + B_WIDTH
A_PAST_CHUNKS = 8
A_PAST = A_PAST_CHUNKS * CHUNK
A_BAND = A_PAST + CHUNK
REL_CLIP = 128
N_REL = 2 * REL_CLIP + 1
IDX_HEADS = 8
IDX_DIM = 64
TOPK_MAX = 256
Q_BLOCK = CHUNK
ROPE_THETA = 10000.0
D_FF = 4 * D_MODEL
LN_EPS = 1e-5
DN_ALPHA = (2 * DEPTH) ** 0.25
DN_BETA = (8 * DEPTH) ** -0.25
IN_SIZES = (A_WIDTH, A_WIDTH, A_WIDTH, B_WIDTH, B_WIDTH, B_WIDTH, IDX_HEADS * IDX_DIM, IDX_DIM, IDX_HEADS)
IN_WIDTH = 3 * A_WIDTH + 3 * B_WIDTH + IDX_HEADS * IDX_DIM + IDX_DIM + IDX_HEADS

kernel_name = 'hybrid_chunkband_dsa_streaming_encoder_step'


def layer_norm(x, g, b):
    xf = x.astype(jnp.float32)
    mu = jnp.mean(xf, -1, keepdims=True)
    var = jnp.mean(jnp.square(xf - mu), -1, keepdims=True)
    y = (xf - mu) * lax.rsqrt(var + LN_EPS)
    return (y * g.astype(jnp.float32) + b.astype(jnp.float32)).astype(x.dtype)


def rope(x, pos):
    half = x.shape[-1] // 2
    inv = ROPE_THETA ** (-jnp.arange(half, dtype=jnp.float32) / half)
    ang = pos.astype(jnp.float32)[:, None] * inv[None, :]
    shp = (pos.shape[0],) + (1,) * (x.ndim - 3) + (half,)
    cos = jnp.cos(ang).reshape(shp)
    sin = jnp.sin(ang).reshape(shp)
    xf = x.astype(jnp.float32)
    x1, x2 = xf[..., :half], xf[..., half:]
    return jnp.concatenate([x1 * cos - x2 * sin, x2 * cos + x1 * sin], -1).astype(x.dtype)


def project(x, w_in, pos):
    bsz, t = x.shape[0], x.shape[1]
    h = jnp.einsum('btd,de->bte', x, w_in)
    cuts = np.cumsum(IN_SIZES)[:-1].tolist()
    qa, ka, va, qb, kb, vb, qi, ki, wi = jnp.split(h, cuts, axis=-1)
    qa = qa.reshape(bsz, t, A_HEADS, HEAD_DIM)
    ka = ka.reshape(bsz, t, A_HEADS, HEAD_DIM)
    va = va.reshape(bsz, t, A_HEADS, HEAD_DIM)
    qb = rope(qb.reshape(bsz, t, B_HEADS, HEAD_DIM), pos)
    kb = rope(kb.reshape(bsz, t, B_HEADS, HEAD_DIM), pos)
    vb = vb.reshape(bsz, t, B_HEADS, HEAD_DIM)
    qi = rope(qi.reshape(bsz, t, IDX_HEADS, IDX_DIM), pos)
    ki = rope(ki, pos)
    wi = wi * (IDX_HEADS * IDX_DIM) ** -0.5
    return qa, ka, va, qb, kb, vb, qi, ki, wi


def band_bias(table, n_q, n_past):
    dist = jnp.arange(n_q)[:, None] + n_past - jnp.arange(n_past + n_q)[None, :]
    idx = jnp.clip(dist, -REL_CLIP, REL_CLIP) + REL_CLIP
    return table[:, idx].astype(jnp.float32)


def band_attention_prompt(q, k, v, table):
    bsz, s_len, nh, hd = q.shape
    nc = s_len // CHUNK
    pad = jnp.zeros((bsz, A_PAST, nh, hd), k.dtype)
    kp = jnp.concatenate([pad, k], 1)
    vp = jnp.concatenate([pad, v], 1)
    bias = band_bias(table, CHUNK, A_PAST)
    qc = q.reshape(bsz, nc, CHUNK, nh, hd).transpose(1, 0, 2, 3, 4)
    scale = hd ** -0.5

    def one_chunk(args):
        c, qblk = args
        start = c * CHUNK
        kb = lax.dynamic_slice_in_dim(kp, start, A_BAND, axis=1)
        vb = lax.dynamic_slice_in_dim(vp, start, A_BAND, axis=1)
        s = jnp.einsum('bqhd,bkhd->bhqk', qblk, kb, preferred_element_type=jnp.float32) * scale + bias
        kpos = start - A_PAST + jnp.arange(A_BAND)
        s = jnp.where(kpos[None, None, None, :] >= 0, s, -jnp.inf)
        p = jax.nn.softmax(s, axis=-1).astype(vb.dtype)
        return jnp.einsum('bhqk,bkhd->bqhd', p, vb)

    o = lax.map(one_chunk, (jnp.arange(nc), qc))
    return o.transpose(1, 0, 2, 3, 4).reshape(bsz, s_len, nh * hd)


def band_attention_step(q, k_new, v_new, k_cache, v_cache, table):
    bsz, t, nh, hd = q.shape
    p_len = k_cache.shape[1]
    kk = jnp.concatenate([k_cache, k_new], 1)
    vv = jnp.concatenate([v_cache, v_new], 1)
    bias = band_bias(table, t, p_len)
    s = jnp.einsum('bqhd,bkhd->bhqk', q, kk, preferred_element_type=jnp.float32) * hd ** -0.5 + bias
    p = jax.nn.softmax(s, axis=-1).astype(vv.dtype)
    return jnp.einsum('bhqk,bkhd->bqhd', p, vv).reshape(bsz, t, nh * hd)


def sparse_attend(q, k, v, qi, w, ki, q_pos, k_pos, topk):
    adm = (k_pos[None, :] // CHUNK) <= (q_pos[:, None] // CHUNK)
    dots = jnp.einsum('qhd,ld->qhl', qi, ki, preferred_element_type=jnp.float32)
    score = jnp.einsum('qh,qhl->ql', w.astype(jnp.float32), jax.nn.relu(dots))
    score = jnp.where(adm, score, -jnp.inf)
    _, idx = lax.top_k(score, topk)
    valid = jnp.take_along_axis(adm, idx, axis=1)
    kg = k[idx]
    vg = v[idx]
    s = jnp.einsum('qhd,qkhd->qhk', q, kg, preferred_element_type=jnp.float32) * q.shape[-1] ** -0.5
    s = jnp.where(valid[:, None, :], s, -jnp.inf)
    p = jax.nn.softmax(s, axis=-1).astype(vg.dtype)
    return jnp.einsum('qhk,qkhd->qhd', p, vg)


def sparse_attention_prompt(q, k, v, qi, w, ki, pos, topk):
    bsz, s_len, nh, hd = q.shape
    nb = s_len // Q_BLOCK
    pos_blk = pos.reshape(nb, Q_BLOCK)

    def per_seq(args):
        qs, ks, vs, qis, ws, kis = args
        def per_block(a):
            bq, bqi, bw, bpos = a
            return sparse_attend(bq, ks, vs, bqi, bw, kis, bpos, pos, topk)
        o = lax.map(per_block, (qs.reshape(nb, Q_BLOCK, nh, hd),
                                qis.reshape(nb, Q_BLOCK, IDX_HEADS, IDX_DIM),
                                ws.reshape(nb, Q_BLOCK, IDX_HEADS), pos_blk))
        return o.reshape(s_len, nh * hd)

    return lax.map(per_seq, (q, k, v, qi, w, ki))


def sparse_attention_step(q, k_new, v_new, qi, w, ki_new, k_cache, v_cache, ki_cache, topk):
    bsz, t, nh, hd = q.shape
    p_len = k_cache.shape[1]
    kk = jnp.concatenate([k_cache, k_new], 1)
    vv = jnp.concatenate([v_cache, v_new], 1)
    kki = jnp.concatenate([ki_cache, ki_new], 1)
    q_pos = p_len + jnp.arange(t)
    k_pos = jnp.arange(p_len + t)

    def per_seq(args):
        qs, ks, vs, qis, ws, kis = args
        return sparse_attend(qs, ks, vs, qis, ws, kis, q_pos, k_pos, topk)

    o = lax.map(per_seq, (q, kk, vv, qi, w, kki))
    return o.reshape(bsz, t, nh * hd)


def post_sublayers(x, oa, ob, w_out, ln1_g, ln1_b, w_up, w_down, ln2_g, ln2_b):
    mix = jnp.einsum('bte,ed->btd', jnp.concatenate([oa, ob], -1), w_out)
    x = layer_norm(DN_ALPHA * x + mix, ln1_g, ln1_b)
    hid = jnp.square(jax.nn.relu(jnp.einsum('btd,df->btf', x, w_up)))
    ff = jnp.einsum('btf,fd->btd', hid, w_down)
    return layer_norm(DN_ALPHA * x + ff, ln2_g, ln2_b)


def setup_inputs(seed: int = 0) -> dict:
    key = jax.random.key(seed)
    ks = jax.random.split(key, 20)
    f32 = jnp.float32
    a_keep = min(A_PAST, PAST_LEN)
    x_prompt = jax.random.normal(ks[0], (BATCH, SEQ, D_MODEL), f32)
    x_sample = jax.random.normal(ks[1], (DEC_BATCH, DEC_SEQ, D_MODEL), f32)
    cache_a_k = jax.random.normal(ks[2], (DEPTH, DEC_BATCH, a_keep, A_HEADS, HEAD_DIM), f32)
    cache_a_v = jax.random.normal(ks[3], (DEPTH, DEC_BATCH, a_keep, A_HEADS, HEAD_DIM), f32) * DN_BETA
    cache_b_k = jax.random.normal(ks[4], (DEPTH, DEC_BATCH, PAST_LEN, B_HEADS, HEAD_DIM), f32)
    cache_b_v = jax.random.normal(ks[5], (DEPTH, DEC_BATCH, PAST_LEN, B_HEADS, HEAD_DIM), f32) * DN_BETA
    cache_b_kidx = jax.random.normal(ks[6], (DEPTH, DEC_BATCH, PAST_LEN, IDX_DIM), f32)
    v_a0 = 2 * A_WIDTH
    v_b0 = 3 * A_WIDTH + 2 * B_WIDTH
    col_scale = jnp.ones((IN_WIDTH,), f32).at[v_a0:v_a0 + A_WIDTH].set(DN_BETA).at[v_b0:v_b0 + B_WIDTH].set(DN_BETA)
    w_in = jax.random.normal(ks[7], (DEPTH, D_MODEL, IN_WIDTH), f32) * D_MODEL ** -0.5 * col_scale
    rel_bias = jax.random.normal(ks[8], (DEPTH, A_HEADS, N_REL), f32) * 0.5
    w_out = jax.random.normal(ks[9], (DEPTH, MIX_WIDTH, D_MODEL), f32) * MIX_WIDTH ** -0.5 * DN_BETA
    ln1_g = 1.0 + 0.05 * jax.random.normal(ks[10], (DEPTH, D_MODEL), f32)
    ln1_b = 0.02 * jax.random.normal(ks[11], (DEPTH, D_MODEL), f32)
    w_up = jax.random.normal(ks[12], (DEPTH, D_MODEL, D_FF), f32) * D_MODEL ** -0.5 * DN_BETA
    w_down = jax.random.normal(ks[13], (DEPTH, D_FF, D_MODEL), f32) * D_FF ** -0.5 * DN_BETA
    ln2_g = 1.0 + 0.05 * jax.random.normal(ks[14], (DEPTH, D_MODEL), f32)
    ln2_b = 0.02 * jax.random.normal(ks[15], (DEPTH, D_MODEL), f32)
    return {'x_prompt': x_prompt, 'x_sample': x_sample,
            'cache_a_k': cache_a_k, 'cache_a_v': cache_a_v,
            'cache_b_k': cache_b_k, 'cache_b_v': cache_b_v, 'cache_b_kidx': cache_b_kidx,
            'w_in': w_in, 'rel_bias': rel_bias, 'w_out': w_out,
            'ln1_g': ln1_g, 'ln1_b': ln1_b, 'w_up': w_up, 'w_down': w_down,
            'ln2_g': ln2_g, 'ln2_b': ln2_b}


def reference(x_prompt, x_sample, cache_a_k, cache_a_v, cache_b_k, cache_b_v, cache_b_kidx,
              w_in, rel_bias, w_out, ln1_g, ln1_b, w_up, w_down, ln2_g, ln2_b):
    s_len = x_prompt.shape[1]
    t_len = x_sample.shape[1]
    p_len = cache_b_k.shape[2]
    pos_p = jnp.arange(s_len)
    pos_s = p_len + jnp.arange(t_len)
    topk_p = min(TOPK_MAX, s_len // 4)
    topk_s = min(TOPK_MAX, (p_len + t_len) // 4)
    keep_a = min(A_PAST, s_len)

    xp, xs = x_prompt, x_sample
    ak_p, av_p, bk_p, bv_p, bi_p = [], [], [], [], []
    ak_s, av_s, bk_s, bv_s, bi_s = [], [], [], [], []
    for l in range(DEPTH):
        qa, ka, va, qb, kb, vb, qi, ki, wi = project(xp, w_in[l], pos_p)
        oa = band_attention_prompt(qa, ka, va, rel_bias[l])
        ob = sparse_attention_prompt(qb, kb, vb, qi, wi, ki, pos_p, topk_p)
        xp = post_sublayers(xp, oa, ob, w_out[l], ln1_g[l], ln1_b[l], w_up[l], w_down[l], ln2_g[l], ln2_b[l])
        ak_p.append(ka[:, s_len - keep_a:])
        av_p.append(va[:, s_len - keep_a:])
        bk_p.append(kb)
        bv_p.append(vb)
        bi_p.append(ki)
        qa, ka, va, qb, kb, vb, qi, ki, wi = project(xs, w_in[l], pos_s)
        oa = band_attention_step(qa, ka, va, cache_a_k[l], cache_a_v[l], rel_bias[l])
        ob = sparse_attention_step(qb, kb, vb, qi, wi, ki, cache_b_k[l], cache_b_v[l], cache_b_kidx[l], topk_s)
        xs = post_sublayers(xs, oa, ob, w_out[l], ln1_g[l], ln1_b[l], w_up[l], w_down[l], ln2_g[l], ln2_b[l])
        ak_s.append(ka)
        av_s.append(va)
        bk_s.append(kb)
        bv_s.append(vb)
        bi_s.append(ki)

    return (xp, xs,
            jnp.stack(ak_p), jnp.stack(av_p), jnp.stack(bk_p), jnp.stack(bv_p), jnp.stack(bi_p),
            jnp.stack(ak_s), jnp.stack(av_s), jnp.stack(bk_s), jnp.stack(bv_s), jnp.stack(bi_s))
```

```python
import numpy as np
import concourse.bass as bass
import concourse.mybir as mybir
from concourse.bass_utils import run_bass_kernel_spmd
from contextlib import ExitStack

F32 = mybir.dt.float32
BF = mybir.dt.bfloat16
ALU = mybir.AluOpType
AF = mybir.ActivationFunctionType
AX = mybir.AxisListType.X

D = 1024
NL_FULL = 4
EW = 3656
DFF = 4096
SEQ = 2048
NT = 16
NHIST = 17
ALPHA = float(8 ** 0.25)
EPS = 1e-5
NEG = -30000.0
NEGBIG = -1.0e30
NIT = 14
TOPK = 256.0
CW = 256
WI_SCALE = float(512 ** -0.5)


import os as _os
STOP = int(_os.environ.get("KSTOP", "99"))


class StopBuild(Exception):
    pass


class Buf:
    __slots__ = ("name", "w", "r")

    def __init__(self, name=""):
        self.name = name
        self.w = {}
        self.r = {}


class Sch:
    ENG = ("pe", "act", "dve", "pool", "sp")

    def __init__(self, nc, es, ndma=12, nepoch=8):
        self.nc = nc
        self.eng = {"pe": nc.tensor, "act": nc.scalar, "dve": nc.vector, "pool": nc.gpsimd, "sp": nc.sync}
        self.cnt = {e: 0 for e in self.ENG}
        self.semh = {}
        self.epoch = 0
        for ep in range(nepoch):
            for e in self.ENG:
                if e == "sp":
                    continue
                self.semh[(e, ep)] = es.enter_context(nc.semaphore(f"s_{e}_{ep}"))
        self.dq = {}
        for q in ("sp", "pool"):
            sems = []
            for i in range(ndma):
                k = ("d", q, i)
                self.semh[k] = es.enter_context(nc.semaphore(f"d_{q}_{i}"))
                sems.append(k)
            self.dq[q] = {"sems": sems, "uses": [0] * ndma, "next": 0}
        self.waited = {e: {} for e in self.ENG}
        self.same_win = 3
        self.ninstr = 0

    def _wait(self, e, k, v):
        if self.waited[e].get(k, 0) >= v:
            return
        self.eng[e].wait_ge(self.semh[k], v)
        self.waited[e][k] = v
        self.ninstr += 1

    def _deps(self, e, reads, writes):
        need = {}
        for b in reads:
            for k, v in b.w.items():
                if need.get(k, 0) < v:
                    need[k] = v
        for b in writes:
            for k, v in b.w.items():
                if need.get(k, 0) < v:
                    need[k] = v
            for k, v in b.r.items():
                if need.get(k, 0) < v:
                    need[k] = v
        for k, v in need.items():
            if k[0] != "d":
                if k[1] != self.epoch:
                    continue
                if k[0] == e:
                    if e == "pe":
                        continue
                    if v <= self.cnt[e] - self.same_win:
                        continue
            self._wait(e, k, v)

    def _mark(self, tok, reads, writes, partial):
        k, v = tok
        for b in reads:
            if b.r.get(k, 0) < v:
                b.r[k] = v
        for b in writes:
            if partial:
                if b.w.get(k, 0) < v:
                    b.w[k] = v
            else:
                b.w = {k: v}
                b.r = {}

    def op(self, e, reads, writes, emit, partial=False):
        self._deps(e, reads, writes)
        ins = emit(self.eng[e])
        self.cnt[e] += 1
        self.ninstr += 1
        k = (e, self.epoch)
        ins.then_inc(self.semh[k], 1)
        self._mark((k, self.cnt[e]), reads, writes, partial)
        return ins

    def dma(self, q, out, in_, reads, writes, partial=False):
        self._deps(q, reads, writes)
        d = self.dq[q]
        i = d["next"]
        d["next"] = (i + 1) % len(d["sems"])
        k = d["sems"][i]
        if d["uses"][i] > 0:
            self._wait(q, k, 16 * d["uses"][i])
        ins = self.eng[q].dma_start(out=out, in_=in_)
        ins.then_inc(self.semh[k], 16)
        self.ninstr += 1
        d["uses"][i] += 1
        self._mark((k, 16 * d["uses"][i]), reads, writes, partial)
        return ins

    def barrier(self, engines):
        for e in engines:
            for k in ("pe", "act", "dve", "pool"):
                if k == e:
                    continue
                if self.cnt[k] > 0:
                    self._wait(e, (k, self.epoch), self.cnt[k])

    def wait_all_dma(self, e):
        for q, d in self.dq.items():
            for i, k in enumerate(d["sems"]):
                if d["uses"][i] > 0:
                    self._wait(e, k, 16 * d["uses"][i])

    def new_epoch(self):
        self.barrier(self.ENG)
        for e in self.ENG:
            self.wait_all_dma(e)
        self.epoch += 1
        for e in self.ENG:
            self.cnt[e] = 0


class Builder:
    def __init__(self, npj=4, nsj=2, nl=4):
        self.npj, self.nsj, self.nl = npj, nsj, nl
        nc = self.nc = bass.Bass("TRN2", target_bir_lowering=False)
        dt = nc.dram_tensor
        I, O, N = "ExternalInput", "ExternalOutput", "Internal"
        nsj_ = max(nsj, 1)
        npj_ = max(npj, 1)
        self.xp = dt("xp", [npj_, SEQ, D], F32, kind=I)
        self.xs = dt("xs", [nsj_, 64, D], F32, kind=I)
        self.cak = dt("cak", [NL_FULL, nsj_, 512, 512], F32, kind=I)
        self.cav = dt("cav", [NL_FULL, nsj_, 512, 512], F32, kind=I)
        self.cbk = dt("cbk", [NL_FULL, nsj_, SEQ, 512], F32, kind=I)
        self.cbv = dt("cbv", [NL_FULL, nsj_, SEQ, 512], F32, kind=I)
        self.cbi = dt("cbi", [NL_FULL, nsj_, SEQ, 64], F32, kind=I)
        self.w_in = dt("w_in", [NL_FULL, D, EW], F32, kind=I)
        self.w_out = dt("w_out", [NL_FULL, D, D], F32, kind=I)
        self.w_up = dt("w_up", [NL_FULL, D, DFF], F32, kind=I)
        self.w_down = dt("w_down", [NL_FULL, DFF, D], F32, kind=I)
        self.relb = dt("relb", [NL_FULL * 8, 257], F32, kind=I)
        self.lnp_d = dt("lnp", [NL_FULL, 4, D], F32, kind=I)
        self.cs_d = dt("cs_tab", [NHIST, 128, 64], F32, kind=I)
        self.yp = dt("yp", [npj_, SEQ, D], F32, kind=O)
        self.ys = dt("ys", [nsj_, 64, D], F32, kind=O)
        self.akp = dt("akp", [NL_FULL, npj_, 512, 512], F32, kind=O)
        self.avp = dt("avp", [NL_FULL, npj_, 512, 512], F32, kind=O)
        self.bkp = dt("bkp", [NL_FULL, npj_, SEQ, 512], F32, kind=O)
        self.bvp = dt("bvp", [NL_FULL, npj_, SEQ, 512], F32, kind=O)
        self.bip = dt("bip", [NL_FULL, npj_, SEQ, 64], F32, kind=O)
        self.aks = dt("aks", [NL_FULL, nsj_, 64, 512], F32, kind=O)
        self.avs = dt("avs", [NL_FULL, nsj_, 64, 512], F32, kind=O)
        self.bks = dt("bks", [NL_FULL, nsj_, 64, 512], F32, kind=O)
        self.bvs = dt("bvs", [NL_FULL, nsj_, 64, 512], F32, kind=O)
        self.bis = dt("bis", [NL_FULL, nsj_, 64, 64], F32, kind=O)
        self.w_in16 = dt("w_in16", [NL_FULL, D, EW], BF, kind=N)
        self.w_out16 = dt("w_out16", [NL_FULL, D, D], BF, kind=N)
        self.w_up16 = dt("w_up16", [NL_FULL, D, DFF], BF, kind=N)
        self.w_down16 = dt("w_down16", [NL_FULL, DFF, D], BF, kind=N)
        self.ext_d = dt("ext_tab", [NL_FULL * 8, 768], F32, kind=N)

    def T(self, name, shape, dtype):
        return self.es.enter_context(self.nc.sbuf_tensor(name, shape, dtype))

    def build(self):
        nc = self.nc
        with ExitStack() as es:
            self.es = es
            S = self.S = Sch(nc, es)
            T = self.T
            self.x = T("x", [128, NT, D], F32)
            self.xB = [Buf(f"x{t}") for t in range(NT)]
            self.kbT = T("kbT", [128, 4, NHIST * 128], BF)
            self.vbA = T("vbA", [128, NHIST, 8, 65], BF)
            self.kiT = T("kiT", [128, NHIST * 128], BF)
            self.hB = [Buf(f"hist{t}") for t in range(NHIST)]
            self.kaT = T("kaT", [128, 4, 8 * 128], BF)
            self.vaA = T("vaA", [128, 8, 8, 65], BF)
            self.aB = [Buf(f"band{t}") for t in range(8)]
            self.BT = T("BT", [128, 8, 5, 128], BF)
            self.BTb = Buf("BT")
            self.cs = T("cs", [128, NHIST, 64], F32)
            self.csb = Buf("cs")
            self.ident = T("ident", [128, 128], BF)
            self.flipJ = T("flipJ", [128, 128], BF)
            self.negI = T("negI", [128, 128], BF)
            self.constb = Buf("const")
            self.pow2 = T("pow2", [128, NIT], F32)
            self.cbias = T("cbias", [128, 2], F32)
            self.wbuf = [T(f"wbuf{i}", [128, 8, CW], BF) for i in range(2)]
            self.wB = [Buf(f"wbuf{i}") for i in range(2)]
            self.wuse = 0
            self.T8 = T("T8", [128, 8, 512], BF)
            self.T8b = Buf("T8")
            self.lng = T("lng", [128, 2, D], F32)
            self.lngB = Buf("lng")
            self.stage = [T(f"stage{i}", [128, CW], F32) for i in range(2)]
            self.stB = [Buf(f"stage{i}") for i in range(2)]
            self.stuse = 0
            self.hank = [T(f"hank{i}", [128, 5, 128], BF) for i in range(1)]
            self.hkB = [Buf(f"hank{i}") for i in range(1)]
            self.xb16 = T("xb16", [128, D], BF)
            self.xb16B = Buf("xb16")
            self.small = T("small", [128, 64], F32)
            self.smB = Buf("small")
            P = lambda n, sh, d_: es.enter_context(nc.psum_tensor(n, sh, d_))
            self.psA = [P(f"psA{i}", [128, 512], F32) for i in range(2)]
            self.psT = [P(f"psT{i}", [128, 1024], BF) for i in range(2)]
            self.psS = [P(f"psS{i}", [128, 512], F32) for i in range(2)]
            self.psO = [P(f"psO{i}", [128, 512], F32) for i in range(2)]
            self.pAB = [Buf(f"psA{i}") for i in range(2)]
            self.pTB = [Buf(f"psT{i}") for i in range(2)]
            self.pSB = [Buf(f"psS{i}") for i in range(2)]
            self.pOB = [Buf(f"psO{i}") for i in range(2)]
            self.ua = self.ut = self.us = 0
            self.uid = 0
            self.njob = 0
            self.wcB = {(w, l): Buf(f"wc_{w}{l}") for w in ("in", "out", "up", "down") for l in range(NL_FULL)}
            self.extB = Buf("ext")
            self.outB = Buf("out")

            try:
                self.prologue()
                self.stopflag = False
                for j in range(self.npj):
                    if not self.stopflag:
                        self.job("p", j)
                for j in range(self.nsj):
                    if not self.stopflag:
                        self.job("s", j)
            except StopBuild:
                pass
            S.barrier(["sp"])
            S.wait_all_dma("sp")
        return nc

    def prologue(self):
        S, nc = self.S, self.nc
        cb = self.constb
        tmpf, tb = self.stage[0][:, 0:128], self.stB[0]
        t32, t32b = self.x[0:32, 1, 0:257], self.xB[1]
        e32, e32b = self.x[0:32, 0, 0:768], self.xB[0]
        S.op("pool", [], [tb], lambda e: e.memset(tmpf, 1.0))
        S.op("pool", [tb], [tb], lambda e: e.affine_select(out=tmpf, in_=tmpf, pattern=[[-1, 128]],
                                                           compare_op=ALU.is_equal, fill=0.0, base=0,
                                                           channel_multiplier=1))
        S.op("dve", [tb], [cb], lambda e: e.tensor_copy(self.ident[:], tmpf), partial=True)
        S.op("dve", [tb], [cb], lambda e: e.tensor_scalar(out=self.negI[:], in0=tmpf, scalar1=NEG, scalar2=None,
                                                          op0=ALU.mult), partial=True)
        S.op("pool", [tb], [tb], lambda e: e.memset(tmpf, 1.0))
        S.op("pool", [tb], [tb], lambda e: e.affine_select(out=tmpf, in_=tmpf, pattern=[[1, 128]],
                                                           compare_op=ALU.is_equal, fill=0.0, base=-127,
                                                           channel_multiplier=1))
        S.op("dve", [tb], [cb], lambda e: e.tensor_copy(self.flipJ[:], tmpf), partial=True)
        for i in range(NIT):
            S.op("pool", [], [cb], lambda e, i=i: e.memset(self.pow2[:, i:i + 1], float(2.0 ** -(i + 1))),
                 partial=True)
        S.op("pool", [], [cb], lambda e: e.memset(self.cbias[:, 0:1], EPS), partial=True)
        S.dma("sp", self.cs[:], self.cs_d.ap().rearrange("t p c -> p t c"), [], [self.csb])
        S.dma("sp", t32, self.relb.ap(), [], [t32b])
        S.op("dve", [t32b], [e32b], lambda e: e.tensor_copy(e32[:, 0:256], t32[:, 1:257]), partial=True)
        S.op("dve", [t32b], [e32b], lambda e: e.tensor_copy(e32[:, 256:768],
                                                            t32[:, 256:257].to_broadcast([32, 512])), partial=True)
        S.dma("sp", self.ext_d.ap(), e32, [e32b], [self.extB])
        for l in range(self.nl):
            for name, src, dst, rows in (("in", self.w_in, self.w_in16, D), ("out", self.w_out, self.w_out16, D),
                                         ("up", self.w_up, self.w_up16, D),
                                         ("down", self.w_down, self.w_down16, DFF)):
                for r0 in range(0, rows, 128):
                    S.dma("pool", dst.ap()[l, r0:r0 + 128, :], src.ap()[l, r0:r0 + 128, :], [],
                          [self.wcB[(name, l)]], partial=True)

    def wchunk(self, name, l, a, b_=0, width=CW):
        i = self.wuse % 2
        self.wuse += 1
        wt, wb = self.wbuf[i], self.wB[i]
        if name == "in":
            src = bass.AP(self.w_in16, l * D * EW + a * CW, [[EW, 128], [128 * EW, 8], [1, width]])
        elif name == "out":
            src = bass.AP(self.w_out16, l * D * D + a * CW, [[D, 128], [128 * D, 8], [1, width]])
        elif name == "up":
            src = bass.AP(self.w_up16, l * D * DFF + a * CW, [[DFF, 128], [128 * DFF, 8], [1, width]])
        else:
            src = bass.AP(self.w_down16, l * DFF * D + b_ * 8 * 128 * D + a * CW, [[D, 128], [128 * D, 8], [1, width]])
        self.S.dma("sp", wt[:, :, 0:width], src, [self.wcB[(name, l)]], [wb])
        return wt, wb

    def nxt(self, which):
        if which == "A":
            i = self.ua % 2; self.ua += 1
            return self.psA[i], self.pAB[i]
        if which == "T":
            i = self.ut % 2; self.ut += 1
            return self.psT[i], self.pTB[i]
        i = self.us % 2; self.us += 1
        return self.psS[i], self.pSB[i]

    def nstage(self):
        i = self.stuse % 2
        self.stuse += 1
        return self.stage[i], self.stB[i]

    def job(self, kind, j):
        S, nc = self.S, self.nc
        prompt = kind == "p"
        if self.njob > 0:
            S.new_epoch()
        self.njob += 1
        ntile = NT if prompt else 1
        if prompt:
            for t in range(NT):
                S.dma("sp", self.x[:, t, :], self.xp.ap()[j, t * 128:(t + 1) * 128, :], [], [self.xB[t]])
        else:
            S.op("pool", [], [self.xB[0]], lambda e: e.memset(self.x[:, 0, :], 0.0))
            S.dma("sp", self.x[0:64, 0, :], self.xs.ap()[j, :, :], [], [self.xB[0]], partial=True)
        for l in range(self.nl):
            if STOP <= 1:
                self.stopflag = True
                return
            self.build_bias(l)
            if STOP <= 2:
                self.stopflag = True
                return
            if not prompt:
                self.load_caches(l, j)
            groups = [list(range(g * 4, g * 4 + 4)) for g in range(4)] if prompt else [[0]]
            for tiles in groups:
                self.phaseA(kind, j, l, tiles)
                S.barrier(["pe", "act", "dve", "pool"])
                if self.stopflag:
                    return
                self.phaseB(kind, j, l, tiles)
                S.barrier(["pe", "act", "dve", "pool"])
                if self.stopflag:
                    return
        if prompt:
            for t in range(NT):
                S.dma("sp", self.yp.ap()[j, t * 128:(t + 1) * 128, :], self.x[:, t, :], [self.xB[t]], [self.outB],
                      partial=True)
        else:
            S.dma("sp", self.ys.ap()[j, :, :], self.x[0:64, 0, :], [self.xB[0]], [self.outB], partial=True)

    def build_bias(self, l):
        S = self.S
        BB = int(_os.environ.get("KBB", "9"))
        for h in range(8):
            hk, hb = self.hank[0], self.hkB[0]
            src = bass.AP(self.ext_d, (l * 8 + h) * 768, [[1, 128], [128, 5], [1, 128]])
            S.dma("pool", hk[:], src, [self.extB], [hb])
            if BB <= 1:
                continue
            ps, pb = self.nxt("A")
            for rp in range(4):
                S.op("pe", [hb, self.constb], [pb],
                     lambda e, rp=rp: e.matmul(ps[:, rp * 128:(rp + 1) * 128], self.flipJ[:], hk[:, rp, :],
                                               start=True, stop=True))
            ps2, pb2 = self.nxt("A")
            S.op("pe", [hb, self.constb], [pb2],
                 lambda e: e.matmul(ps2[:, 0:128], self.flipJ[:], hk[:, 4, :], start=True, stop=True))
            if BB <= 2:
                continue
            for rp in range(4):
                S.op("act", [pb], [self.BTb],
                     lambda e, rp=rp: e.activation(out=self.BT[:, h, 4 - rp, :], in_=ps[:, rp * 128:(rp + 1) * 128],
                                                   func=AF.Copy), partial=True)
            S.op("act", [pb2], [self.BTb],
                 lambda e: e.activation(out=self.BT[:, h, 0, :], in_=ps2[:, 0:128], func=AF.Copy), partial=True)
        if BB <= 3:
            return
        S.op("dve", [self.BTb], [self.BTb], lambda e: e.memset(self.BT[64:128, :, 4, 0:64], NEG), partial=True)
        S.op("dve", [self.BTb], [self.BTb], lambda e: e.memset(self.BT[0:64, :, 0, 64:128], NEG), partial=True)

    def transposes_to(self, src_ap_fn, nblk, src_bufs, dst_fn, dst_bufs, evac_eng="act"):
        S = self.S
        i = 0
        while i < nblk:
            n = min(8, nblk - i)
            ps, pb = self.nxt("T")
            for q in range(n):
                S.op("pe", src_bufs + [self.constb], [pb],
                     lambda e, q=q, i=i: e.transpose(ps[:, q * 128:(q + 1) * 128], src_ap_fn(i + q), self.ident[:]))
            dst_fn(i, n, ps, pb)
            i += n

    def load_caches(self, l, s):
        S = self.S
        self.uid += 1
        with self.nc.sbuf_tensor(f"cst_{self.uid}", [128, 2, 512], BF) as cst, \
                self.nc.sbuf_tensor(f"cst2_{self.uid}", [128, 2, 128], BF) as cst2:
            cB = [Buf("cst0"), Buf("cst1")]
            c2B = [Buf("cst20"), Buf("cst21")]
            u = 0
            for t in range(16):
                rows = slice(t * 128, (t + 1) * 128)
                i = u % 2; u += 1
                hb = self.hB[t]
                S.dma("pool", cst[:, i, :], self.cbk.ap()[l, s, rows, :], [], [cB[i]])

                def dst(i0, n, ps, pb, t=t, hb=hb):
                    S.op("act", [pb], [hb], lambda e: e.activation(
                        out=self.kbT[:, :, t * 128:(t + 1) * 128],
                        in_=ps[:, 0:512].rearrange("p (a b) -> p a b", a=4), func=AF.Copy), partial=True)
                self.transposes_to(lambda q, i=i: cst[:, i, q * 128:(q + 1) * 128], 4, [cB[i]], dst, [hb])
                S.dma("pool", self.vbA[:, t, :, 0:64], self.cbv.ap()[l, s, rows, :].rearrange("p (h d) -> p h d", h=8),
                      [], [hb], partial=True)
                S.op("pool", [], [hb], lambda e, t=t: e.memset(self.vbA[:, t, :, 64:65], 1.0), partial=True)
                S.dma("pool", cst2[:, i, 0:64], self.cbi.ap()[l, s, rows, :], [], [c2B[i]], partial=True)
                S.dma("pool", cst2[:, i, 64:128], self.cbi.ap()[l, s, rows, :], [], [c2B[i]], partial=True)

                def dst2(i0, n, ps, pb, t=t, hb=hb):
                    S.op("act", [pb], [hb], lambda e: e.activation(
                        out=self.kiT[:, t * 128:(t + 1) * 128], in_=ps[:, 0:128], func=AF.Copy), partial=True)
                self.transposes_to(lambda q, i=i: cst2[:, i, :], 1, [c2B[i]], dst2, [hb])
            for t in range(4):
                rows = slice(t * 128, (t + 1) * 128)
                i = u % 2; u += 1
                ab = self.aB[t]
                S.dma("pool", cst[:, i, :], self.cak.ap()[l, s, rows, :], [], [cB[i]])

                def dst3(i0, n, ps, pb, t=t, ab=ab):
                    S.op("act", [pb], [ab], lambda e: e.activation(
                        out=self.kaT[:, :, t * 128:(t + 1) * 128],
                        in_=ps[:, 0:512].rearrange("p (a b) -> p a b", a=4), func=AF.Copy), partial=True)
                self.transposes_to(lambda q, i=i: cst[:, i, q * 128:(q + 1) * 128], 4, [cB[i]], dst3, [ab])
                S.dma("pool", self.vaA[:, t, :, 0:64], self.cav.ap()[l, s, rows, :].rearrange("p (h d) -> p h d", h=8),
                      [], [ab], partial=True)
                S.op("pool", [], [ab], lambda e, t=t: e.memset(self.vaA[:, t, :, 64:65], 1.0), partial=True)
            S.barrier(["pe", "act", "dve", "pool"])
            S.wait_all_dma("pool")

    def make_xT(self, tiles):
        S = self.S
        for ti, t in enumerate(tiles):
            S.op("act", [self.xB[t]], [self.xb16B], lambda e, t=t: e.activation(out=self.xb16[:], in_=self.x[:, t, :],
                                                                                 func=AF.Copy))

            def dst(i0, n, ps, pb, ti=ti):
                S.op("dve", [pb], [self.T8b], lambda e: e.tensor_copy(
                    self.T8[:, :, ti * 128:(ti + 1) * 128], ps[:, 0:1024].rearrange("p (a b) -> p a b", a=8)),
                    partial=True)
            self.transposes_to(lambda q: self.xb16[:, q * 128:(q + 1) * 128], 8, [self.xb16B], dst, [self.T8b])

    def phaseA(self, kind, j, l, tiles):
        S, nc = self.S, self.nc
        prompt = kind == "p"
        ng = len(tiles)
        ntok = ng * 128
        with ExitStack() as es:
            self.uid += 1
            A = lambda n, sh, d_: es.enter_context(nc.sbuf_tensor(f"{n}_{self.uid}", sh, d_))
            qaT = A("qaT", [128, 4, 512], BF); qbT = A("qbT", [128, 4, 512], BF); qiT = A("qiT", [128, 4, 512], BF)
            qB = Buf("q")
            wsb = A("wsb", [128, 4, 8], F32); wsbB = Buf("wsb")
            hb16 = [A(f"hb16_{i}", [128, CW], BF) for i in range(2)]
            hbB = [Buf(f"hb16_{i}") for i in range(2)]
            t1 = A("t1", [128, CW], F32); t2 = A("t2", [128, CW], F32); t12B = Buf("t12")
            score = A("score", [128, NHIST * 128], F32); scB = Buf("score")
            rsb = [A(f"rsb{i}", [128, 512], F32) for i in range(2)]
            rB = [Buf(f"rsb{i}") for i in range(2)]
            maskb = A("maskb", [128, NHIST * 128], BF); mkB = Buf("maskb")
            mbT = A("mbT", [128, NHIST, 128], BF); mtB = Buf("mbT")
            expT = [A(f"expT{i}", [128, 640], BF) for i in range(2)]
            eB = [Buf(f"expT{i}") for i in range(2)]
            otok = A("otok", [128, 512], BF); otB = Buf("otok")
            st = A("st", [128, 48], F32); stB_ = Buf("st")
            steps = A("steps", [128, NIT], F32)

            self.make_xT(tiles)
            T8 = self.T8
            if STOP <= 3:
                self.stopflag = True
                return

            hu = 0
            for c in range(int(_os.environ.get("KNC", "15"))):
                width = CW if c < 14 else 72
                wt, wb = self.wchunk("in", l, c, width=width)
                seg = c // 2
                hh = c % 2
                for ti, t in enumerate(tiles):
                    hidx = t if prompt else 16
                    aslot = (t % 8) if prompt else 4
                    ps, pb = self.nxt("A")
                    for k in range(8):
                        S.op("pe", [self.T8b, wb], [pb],
                             lambda e, k=k, ti=ti: e.matmul(ps[:, 0:width], T8[:, k, ti * 128:(ti + 1) * 128],
                                                            wt[:, k, 0:width], start=(k == 0), stop=(k == 7)))
                    cs_c = self.cs[:, hidx, 0:32]
                    cs_s = self.cs[:, hidx, 32:64]
                    nrow = 128 if prompt else 64

                    def rope(src, srcB, nh):
                        v = lambda a: a.rearrange("p (h two d) -> p h two d", h=nh, two=2)
                        bc = lambda a: a.unsqueeze(1).to_broadcast([128, nh, 32])
                        S.op("pool", [srcB, self.csb], [t12B], lambda e: e.tensor_tensor(
                            out=v(t1[:, 0:nh * 64]), in0=v(src),
                            in1=cs_c.unsqueeze(1).unsqueeze(1).to_broadcast([128, nh, 2, 32]), op=ALU.mult))
                        S.op("pool", [srcB, self.csb], [t12B], lambda e: e.tensor_tensor(
                            out=v(t2[:, 0:nh * 64])[:, :, 0, :], in0=v(src)[:, :, 1, :], in1=bc(cs_s), op=ALU.mult),
                            partial=True)
                        S.op("pool", [srcB, self.csb], [t12B], lambda e: e.tensor_tensor(
                            out=v(t2[:, 0:nh * 64])[:, :, 1, :], in0=v(src)[:, :, 0, :], in1=bc(cs_s), op=ALU.mult),
                            partial=True)
                        S.op("pool", [t12B], [t12B], lambda e: e.tensor_tensor(
                            out=v(t1[:, 0:nh * 64])[:, :, 0, :], in0=v(t1[:, 0:nh * 64])[:, :, 0, :],
                            in1=v(t2[:, 0:nh * 64])[:, :, 0, :], op=ALU.subtract), partial=True)
                        S.op("pool", [t12B], [t12B], lambda e: e.tensor_tensor(
                            out=v(t1[:, 0:nh * 64])[:, :, 1, :], in0=v(t1[:, 0:nh * 64])[:, :, 1, :],
                            in1=v(t2[:, 0:nh * 64])[:, :, 1, :], op=ALU.add), partial=True)

                    def out_dma(dram, col0, ncol, src_t, src_b):
                        S.dma("sp", dram[0:nrow, col0:col0 + ncol], src_t[0:nrow, 0:ncol], [src_b], [self.outB],
                              partial=True)

                    def tr_to(dstT, hbt, hbb, ti=ti, hh=hh):
                        def dst(i0, n, ps_, pb_):
                            S.op("act", [pb_], [qB], lambda e: e.activation(
                                out=dstT[:, 2 * hh:2 * hh + 2, ti * 128:(ti + 1) * 128],
                                in_=ps_[:, 0:256].rearrange("p (a b) -> p a b", a=2), func=AF.Copy), partial=True)
                        self.transposes_to(lambda q: hbt[:, q * 128:(q + 1) * 128], 2, [hbb], dst, [qB])

                    hi_ = hu % 2
                    if seg == 0:
                        hu += 1
                        S.op("act", [pb], [hbB[hi_]], lambda e: e.activation(out=hb16[hi_][:], in_=ps[:, 0:CW],
                                                                             func=AF.Copy, scale=0.125))
                        tr_to(qaT, hb16[hi_], hbB[hi_])
                    elif seg == 1:
                        hu += 1
                        ab = self.aB[aslot]
                        S.op("act", [pb], [hbB[hi_]], lambda e: e.activation(out=hb16[hi_][:], in_=ps[:, 0:CW],
                                                                             func=AF.Copy))
                        if (prompt and t >= 12) or not prompt:
                            sg, sgb = self.nstage()
                            S.op("act", [pb], [sgb], lambda e: e.activation(out=sg[:, 0:CW], in_=ps[:, 0:CW], func=AF.Copy))
                            dram = (self.akp.ap()[l, j, (t - 12) * 128:(t - 11) * 128, :] if prompt
                                    else self.aks.ap()[l, j, :, :])
                            out_dma(dram, hh * CW, CW, sg, sgb)

                        def dstk(i0, n, ps_, pb_, aslot=aslot, ab=ab, hh=hh):
                            S.op("act", [pb_], [ab], lambda e: e.activation(
                                out=self.kaT[:, 2 * hh:2 * hh + 2, aslot * 128:(aslot + 1) * 128],
                                in_=ps_[:, 0:256].rearrange("p (a b) -> p a b", a=2), func=AF.Copy), partial=True)
                        self.transposes_to(lambda q, hi_=hi_: hb16[hi_][:, q * 128:(q + 1) * 128], 2, [hbB[hi_]], dstk, [ab])
                    elif seg == 2:
                        ab = self.aB[aslot]
                        S.op("act", [pb], [ab], lambda e: e.activation(
                            out=self.vaA[:, aslot, 4 * hh:4 * hh + 4, 0:64],
                            in_=ps[:, 0:CW].rearrange("p (h d) -> p h d", h=4), func=AF.Copy), partial=True)
                        if hh == 0:
                            S.op("pool", [], [ab], lambda e: e.memset(self.vaA[:, aslot, :, 64:65], 1.0), partial=True)
                        if (prompt and t >= 12) or not prompt:
                            sg, sgb = self.nstage()
                            S.op("act", [pb], [sgb], lambda e: e.activation(out=sg[:, 0:CW], in_=ps[:, 0:CW], func=AF.Copy))
                            dram = (self.avp.ap()[l, j, (t - 12) * 128:(t - 11) * 128, :] if prompt
                                    else self.avs.ap()[l, j, :, :])
                            out_dma(dram, hh * CW, CW, sg, sgb)
                    elif seg in (3, 4, 6):
                        hu += 1
                        sg, sgb = self.nstage()
                        S.op("act", [pb], [sgb], lambda e: e.activation(out=sg[:, 0:CW], in_=ps[:, 0:CW], func=AF.Copy))
                        rope(sg[:, 0:CW], sgb, 4)
                        if seg == 4:
                            S.op("pool", [t12B], [sgb], lambda e: e.tensor_copy(sg[:, 0:CW], t1[:, 0:CW]))
                            dram = (self.bkp.ap()[l, j, t * 128:(t + 1) * 128, :] if prompt else self.bks.ap()[l, j, :, :])
                            out_dma(dram, hh * CW, CW, sg, sgb)
                            S.op("pool", [sgb], [hbB[hi_]], lambda e: e.tensor_copy(hb16[hi_][:], sg[:, 0:CW]))
                            hb_ = self.hB[hidx]

                            def dstk(i0, n, ps_, pb_, hidx=hidx, hb_=hb_, hh=hh):
                                S.op("act", [pb_], [hb_], lambda e: e.activation(
                                    out=self.kbT[:, 2 * hh:2 * hh + 2, hidx * 128:(hidx + 1) * 128],
                                    in_=ps_[:, 0:256].rearrange("p (a b) -> p a b", a=2), func=AF.Copy), partial=True)
                            self.transposes_to(lambda q, hi_=hi_: hb16[hi_][:, q * 128:(q + 1) * 128], 2, [hbB[hi_]],
                                               dstk, [hb_])
                        else:
                            sc = 0.125 if seg == 3 else 1.0
                            S.op("pool", [t12B], [hbB[hi_]], lambda e: e.tensor_scalar(
                                out=hb16[hi_][:], in0=t1[:, 0:CW], scalar1=sc, scalar2=None, op0=ALU.mult))
                            tr_to(qbT if seg == 3 else qiT, hb16[hi_], hbB[hi_])
                    elif seg == 5:
                        hb_ = self.hB[hidx]
                        S.op("act", [pb], [hb_], lambda e: e.activation(
                            out=self.vbA[:, hidx, 4 * hh:4 * hh + 4, 0:64],
                            in_=ps[:, 0:CW].rearrange("p (h d) -> p h d", h=4), func=AF.Copy), partial=True)
                        if hh == 0:
                            S.op("pool", [], [hb_], lambda e: e.memset(self.vbA[:, hidx, :, 64:65], 1.0), partial=True)
                        sg, sgb = self.nstage()
                        S.op("act", [pb], [sgb], lambda e: e.activation(out=sg[:, 0:CW], in_=ps[:, 0:CW], func=AF.Copy))
                        dram = (self.bvp.ap()[l, j, t * 128:(t + 1) * 128, :] if prompt else self.bvs.ap()[l, j, :, :])
                        out_dma(dram, hh * CW, CW, sg, sgb)
                    else:
                        hu += 1
                        sg, sgb = self.nstage()
                        S.op("act", [pb], [sgb], lambda e: e.activation(out=sg[:, 0:72], in_=ps[:, 0:72], func=AF.Copy))
                        rope(sg[:, 0:64], sgb, 1)
                        S.op("pool", [sgb], [wsbB], lambda e: e.tensor_scalar(
                            out=wsb[:, ti, :], in0=sg[:, 64:72], scalar1=WI_SCALE, scalar2=None, op0=ALU.mult),
                            partial=True)
                        S.op("pool", [t12B], [sgb], lambda e: e.tensor_copy(sg[:, 0:64], t1[:, 0:64]))
                        dram = (self.bip.ap()[l, j, t * 128:(t + 1) * 128, :] if prompt else self.bis.ap()[l, j, :, :])
                        out_dma(dram, 0, 64, sg, sgb)
                        S.op("pool", [sgb], [hbB[hi_]], lambda e: e.tensor_copy(hb16[hi_][:, 0:64], sg[:, 0:64]))
                        S.op("pool", [sgb], [hbB[hi_]], lambda e: e.tensor_copy(hb16[hi_][:, 64:128], sg[:, 0:64]),
                             partial=True)
                        hb_ = self.hB[hidx]

                        def dstk(i0, n, ps_, pb_, hidx=hidx, hb_=hb_):
                            S.op("act", [pb_], [hb_], lambda e: e.activation(
                                out=self.kiT[:, hidx * 128:(hidx + 1) * 128], in_=ps_[:, 0:128], func=AF.Copy),
                                partial=True)
                        self.transposes_to(lambda q, hi_=hi_: hb16[hi_][:, 0:128], 1, [hbB[hi_]], dstk, [hb_])

            if STOP <= 4:
                self.stopflag = True
                return
            for ti, t in enumerate(tiles):
                qc = slice(ti * 128, (ti + 1) * 128)
                if prompt:
                    win = [(tt, tt - (t - 4), tt % 8) for tt in range(max(0, t - 4), t + 1)]
                else:
                    win = [(r, r, r) for r in range(5)]
                for h in range(8):
                    pr, hf = h // 2, (h % 2) * 64
                    psS, pSb = self.nxt("S")
                    psX, pXb = self.psA[h % 2], self.pAB[h % 2]
                    for wi_, (tt, r, slot) in enumerate(win):
                        o_ps, o_b = (psS, pSb) if wi_ < 4 else (psX, pXb)
                        oc = slice((wi_ % 4) * 128, (wi_ % 4 + 1) * 128)
                        S.op("pe", [self.aB[slot], qB], [o_b], lambda e, o_ps=o_ps, oc=oc, slot=slot: e.matmul(
                            o_ps[:, oc], self.kaT[hf:hf + 64, pr, slot * 128:(slot + 1) * 128],
                            qaT[hf:hf + 64, pr, qc], start=True, stop=False))
                        S.op("pe", [self.BTb, self.constb], [o_b], lambda e, o_ps=o_ps, oc=oc, r=r: e.matmul(
                            o_ps[:, oc], self.ident[:], self.BT[:, h, r, :], start=False, stop=True))
                    ei = h % 2
                    n1 = min(4, len(win))
                    S.op("act", [pSb], [eB[ei]], lambda e, n1=n1, ei=ei: e.activation(
                        out=expT[ei][:, 0:n1 * 128], in_=psS[:, 0:n1 * 128], func=AF.Exp))
                    if len(win) == 5:
                        S.op("act", [pXb], [eB[ei]], lambda e, ei=ei: e.activation(
                            out=expT[ei][:, 512:640], in_=psX[:, 0:128], func=AF.Exp), partial=True)
                    po, pob = self.psO[h // 4], self.pOB[h // 4]
                    for wi_, (tt, r, slot) in enumerate(win):
                        S.op("pe", [eB[ei], self.aB[slot]], [pob], lambda e, wi_=wi_, slot=slot, ei=ei, po=po: e.matmul(
                            po[:, (h % 4) * 65:(h % 4) * 65 + 65], expT[ei][:, wi_ * 128:(wi_ + 1) * 128],
                            self.vaA[:, slot, h, :], start=(wi_ == 0), stop=(wi_ == len(win) - 1)))
                self.finish_attn(st, stB_, otok, otB, 0, ti)
                if STOP <= 5:
                    self.stopflag = True
                    return

                nkt = (t + 1) if prompt else NHIST
                N = nkt * 128
                hist = list(range(nkt))
                for h in range(8):
                    pr, hf = h // 2, (h % 2) * 64
                    for k0 in range(0, N, 512):
                        kn = min(512, N - k0)
                        psS, pSb = self.nxt("S")
                        S.op("pe", [qB] + [self.hB[x_] for x_ in range(k0 // 128, (k0 + kn) // 128)], [pSb],
                             lambda e, k0=k0, kn=kn, psS=psS: e.matmul(psS[:, 0:kn], qiT[hf:hf + 64, pr, qc],
                                                                       self.kiT[hf:hf + 64, k0:k0 + kn],
                                                                       start=True, stop=True))
                        ri = self.us % 2
                        S.op("act", [pSb], [rB[ri]], lambda e, kn=kn, ri=ri, psS=psS: e.activation(
                            out=rsb[ri][:, 0:kn], in_=psS[:, 0:kn], func=AF.Relu))
                        if h == 0:
                            S.op("pool", [rB[ri], wsbB], [scB], lambda e, k0=k0, kn=kn, ri=ri: e.tensor_scalar(
                                out=score[:, k0:k0 + kn], in0=rsb[ri][:, 0:kn], scalar1=wsb[:, ti, 0:1], scalar2=None,
                                op0=ALU.mult), partial=True)
                        else:
                            S.op("pool", [rB[ri], wsbB], [rB[ri]], lambda e, kn=kn, ri=ri: e.tensor_scalar(
                                out=rsb[ri][:, 0:kn], in0=rsb[ri][:, 0:kn], scalar1=wsb[:, ti, h:h + 1], scalar2=None,
                                op0=ALU.mult))
                            S.op("pool", [rB[ri], scB], [scB], lambda e, k0=k0, kn=kn, ri=ri: e.tensor_tensor(
                                out=score[:, k0:k0 + kn], in0=score[:, k0:k0 + kn], in1=rsb[ri][:, 0:kn],
                                op=ALU.add), partial=True)
                if prompt:
                    S.op("dve", [scB], [scB], lambda e: e.memset(score[0:64, N - 64:N], NEGBIG), partial=True)
                else:
                    S.op("pool", [scB], [scB], lambda e: e.memset(score[:, N - 64:N], NEGBIG), partial=True)
                need_topk = (not prompt) or t >= 2
                if need_topk:
                    S.op("dve", [scB], [stB_], lambda e: e.tensor_reduce(out=st[:, 0:1], in_=score[:, 0:N], axis=AX,
                                                                         op=ALU.max), partial=True)
                    if prompt:
                        S.op("dve", [scB], [stB_], lambda e: e.tensor_reduce(out=st[0:64, 1:2], in_=score[0:64, 0:N - 64],
                                                                             axis=AX, op=ALU.min), partial=True)
                        S.op("dve", [scB], [stB_], lambda e: e.tensor_reduce(out=st[64:128, 1:2], in_=score[64:128, 0:N],
                                                                             axis=AX, op=ALU.min), partial=True)
                    else:
                        S.op("dve", [scB], [stB_], lambda e: e.tensor_reduce(out=st[:, 1:2], in_=score[:, 0:N - 64],
                                                                             axis=AX, op=ALU.min), partial=True)
                    S.op("dve", [stB_], [stB_], lambda e: e.tensor_tensor(out=st[:, 2:3], in0=st[:, 0:1], in1=st[:, 1:2],
                                                                          op=ALU.subtract))
                    S.op("dve", [stB_, self.constb], [stB_], lambda e: e.tensor_scalar(
                        out=steps[:], in0=self.pow2[:], scalar1=st[:, 2:3], scalar2=None, op0=ALU.mult))
                    S.op("dve", [stB_], [stB_], lambda e: e.tensor_copy(st[:, 3:4], st[:, 1:2]))
                    for it in range(NIT):
                        S.op("dve", [stB_], [stB_], lambda e, it=it: e.tensor_tensor(
                            out=st[:, 4:5], in0=st[:, 3:4], in1=steps[:, it:it + 1], op=ALU.add))
                        S.op("dve", [stB_, scB], [stB_, mkB], lambda e: e.tensor_scalar(
                            out=maskb[:, 0:N], in0=score[:, 0:N], scalar1=st[:, 4:5], scalar2=None, op0=ALU.is_ge,
                            op1=ALU.add, accum_out=st[:, 5:6]))
                        S.op("dve", [stB_], [stB_], lambda e: e.tensor_scalar(
                            out=st[:, 6:7], in0=st[:, 5:6], scalar1=TOPK, scalar2=None, op0=ALU.is_ge))
                        S.op("dve", [stB_], [stB_], lambda e, it=it: e.scalar_tensor_tensor(
                            out=st[:, 3:4], in0=st[:, 6:7], scalar=steps[:, it:it + 1], in1=st[:, 3:4],
                            op0=ALU.mult, op1=ALU.add))
                else:
                    S.op("dve", [stB_], [stB_], lambda e: e.memset(st[:, 3:4], -1.0e29))
                S.op("dve", [stB_, scB], [mkB], lambda e: e.tensor_scalar(
                    out=maskb[:, 0:N], in0=score[:, 0:N], scalar1=st[:, 3:4], scalar2=None, op0=ALU.is_lt))

                def dstm(i0, n, ps_, pb_):
                    S.op("act", [pb_], [mtB], lambda e: e.activation(
                        out=mbT[:, i0:i0 + n, :], in_=ps_[:, 0:n * 128].rearrange("p (a b) -> p a b", a=n),
                        func=AF.Copy), partial=True)
                self.transposes_to(lambda q: maskb[:, q * 128:(q + 1) * 128], nkt, [mkB], dstm, [mtB])
                for h in range(8):
                    pr, hf = h // 2, (h % 2) * 64
                    po, pob = self.psO[h // 4], self.pOB[h // 4]
                    for k0 in range(0, nkt, 4):
                        kn = min(4, nkt - k0)
                        psS, pSb = self.nxt("S")
                        for q in range(kn):
                            kt = k0 + q
                            S.op("pe", [self.hB[kt], qB], [pSb], lambda e, q=q, kt=kt, psS=psS: e.matmul(
                                psS[:, q * 128:(q + 1) * 128], self.kbT[hf:hf + 64, pr, kt * 128:(kt + 1) * 128],
                                qbT[hf:hf + 64, pr, qc], start=True, stop=False))
                            S.op("pe", [mtB, self.constb], [pSb], lambda e, q=q, kt=kt, psS=psS: e.matmul(
                                psS[:, q * 128:(q + 1) * 128], self.negI[:], mbT[:, kt, :], start=False, stop=True))
                        ei = self.us % 2
                        S.op("act", [pSb], [eB[ei]], lambda e, kn=kn, ei=ei, psS=psS: e.activation(
                            out=expT[ei][:, 0:kn * 128], in_=psS[:, 0:kn * 128], func=AF.Exp))
                        for q in range(kn):
                            kt = k0 + q
                            S.op("pe", [eB[ei], self.hB[kt]], [pob], lambda e, q=q, kt=kt, ei=ei, po=po: e.matmul(
                                po[:, (h % 4) * 65:(h % 4) * 65 + 65], expT[ei][:, q * 128:(q + 1) * 128],
                                self.vbA[:, kt, h, :], start=(kt == 0), stop=(kt == nkt - 1)))
                self.finish_attn(st, stB_, otok, otB, 4, ti)
                if STOP <= 6:
                    self.stopflag = True
                    return

    def finish_attn(self, st, stB_, otok, otB, e0, ti):
        S = self.S
        for half in range(2):
            po, pob = self.psO[half], self.pOB[half]
            pv = po[:, 0:260].rearrange("p (h d) -> p h d", h=4)
            S.op("dve", [pob], [stB_], lambda e, pv=pv, half=half: e.reciprocal(
                st[:, 8 + half * 4:12 + half * 4].unsqueeze(2), pv[:, :, 64:65]), partial=True)
            S.op("dve", [pob, stB_], [otB], lambda e, pv=pv, half=half: e.tensor_tensor(
                out=otok[:, half * 256:(half + 1) * 256].rearrange("p (h d) -> p h d", h=4), in0=pv[:, :, 0:64],
                in1=st[:, 8 + half * 4:12 + half * 4].unsqueeze(2).to_broadcast([128, 4, 64]), op=ALU.mult),
                partial=True)

        def dst(i0, n, ps_, pb_):
            S.op("act", [pb_], [self.T8b], lambda e: e.activation(
                out=self.T8[:, e0:e0 + 4, ti * 128:(ti + 1) * 128],
                in_=ps_[:, 0:512].rearrange("p (a b) -> p a b", a=4), func=AF.Copy), partial=True)
        self.transposes_to(lambda q: otok[:, q * 128:(q + 1) * 128], 4, [otB], dst, [self.T8b])

    def layer_norm(self, t, which, ytmp_unused=None):
        S = self.S
        xb = self.xB[t]
        sm, smB = self.small, self.smB
        xt = self.x[:, t, :]
        S.op("dve", [xb], [smB], lambda e: e.bn_stats(sm[:, 0:6], self.x[:, t, 0:512]), partial=True)
        S.op("dve", [xb], [smB], lambda e: e.bn_stats(sm[:, 6:12], self.x[:, t, 512:1024]), partial=True)
        S.op("dve", [smB], [smB], lambda e: e.bn_aggr(sm[:, 12:14], sm[:, 0:12]))
        S.op("act", [smB, self.constb], [smB], lambda e: e.activation(out=sm[:, 14:15], in_=sm[:, 13:14], func=AF.Sqrt,
                                                                      bias=self.cbias[:, 0:1], scale=1.0))
        S.op("dve", [smB], [smB], lambda e: e.reciprocal(sm[:, 15:16], sm[:, 14:15]))
        S.op("dve", [smB, xb], [xb], lambda e: e.tensor_scalar(out=xt, in0=xt, scalar1=sm[:, 12:13], scalar2=sm[:, 15:16],
                                                              op0=ALU.subtract, op1=ALU.mult))
        S.op("pool", [xb, self.lngB], [xb], lambda e: e.tensor_tensor(out=xt, in0=xt, in1=self.lng[:, 0, :], op=ALU.mult))
        S.op("pool", [xb, self.lngB], [xb], lambda e: e.tensor_tensor(out=xt, in0=xt, in1=self.lng[:, 1, :], op=ALU.add))

    def load_ln(self, l, which):
        src = bass.AP(self.lnp_d, (l * 4 + which * 2) * D, [[0, 128], [D, 2], [1, D]])
        self.S.dma("sp", self.lng[:], src, [], [self.lngB])

    def phaseB(self, kind, j, l, tiles):
        S, nc = self.S, self.nc
        ng = len(tiles)
        ntok = ng * 128
        T8 = self.T8
        with ExitStack() as es:
            self.uid += 1
            A = lambda n, sh, d_: es.enter_context(nc.sbuf_tensor(f"{n}_{self.uid}", sh, d_))
            hidT = A("hidT", [128, 32, 512], BF)
            hidB = Buf("hidT")
            rsb = [A(f"rf{i}", [128, 512], F32) for i in range(2)]
            rB = [Buf(f"rf{i}") for i in range(2)]
            self.load_ln(l, 0)
            for b in range(4):
                wt, wb = self.wchunk("out", l, b)
                for ti, t in enumerate(tiles):
                    ps, pb = self.nxt("A")
                    for k in range(8):
                        S.op("pe", [self.T8b, wb], [pb], lambda e, k=k, ti=ti: e.matmul(
                            ps[:, 0:CW], T8[:, k, ti * 128:(ti + 1) * 128], wt[:, k, :], start=(k == 0), stop=(k == 7)))
                    xs_ = self.x[:, t, b * CW:(b + 1) * CW]
                    S.op("dve", [pb, self.xB[t]], [self.xB[t]], lambda e, xs_=xs_: e.scalar_tensor_tensor(
                        out=xs_, in0=xs_, scalar=ALPHA, in1=ps[:, 0:CW], op0=ALU.mult, op1=ALU.add), partial=True)
            for t in tiles:
                self.layer_norm(t, 0)
            if STOP <= 7:
                self.stopflag = True
                return
            self.make_xT(tiles)
            self.load_ln(l, 1)
            ri = 0
            for c in range(16):
                wt, wb = self.wchunk("up", l, c)
                for fc in range(2):
                    ps, pb = self.nxt("A")
                    for k in range(8):
                        S.op("pe", [self.T8b, wb], [pb], lambda e, k=k, fc=fc: e.matmul(
                            ps[:, 0:ntok], wt[:, k, fc * 128:(fc + 1) * 128], T8[:, k, 0:ntok], start=(k == 0),
                            stop=(k == 7)))
                    r_, rb_ = rsb[ri % 2], rB[ri % 2]
                    ri += 1
                    S.op("act", [pb], [rb_], lambda e, r_=r_: e.activation(out=r_[:, 0:ntok], in_=ps[:, 0:ntok],
                                                                           func=AF.Relu))
                    S.op("pool", [rb_], [hidB], lambda e, r_=r_, c=c, fc=fc: e.tensor_tensor(
                        out=hidT[:, c * 2 + fc, 0:ntok], in0=r_[:, 0:ntok], in1=r_[:, 0:ntok], op=ALU.mult),
                        partial=True)
            accs = [(self.psO[0], self.pOB[0]), (self.psO[1], self.pOB[1]), (self.psS[0], self.pSB[0]),
                    (self.psS[1], self.pSB[1])]
            for b in range(4):
                for fg in range(4):
                    wt, wb = self.wchunk("down", l, b, fg)
                    for jj in range(8):
                        f = fg * 8 + jj
                        for ti, t in enumerate(tiles):
                            ps, pb = accs[ti]
                            S.op("pe", [hidB, wb], [pb], lambda e, ps=ps, jj=jj, f=f, ti=ti: e.matmul(
                                ps[:, 0:CW], hidT[:, f, ti * 128:(ti + 1) * 128], wt[:, jj, :], start=(f == 0),
                                stop=(f == 31)))
                for ti, t in enumerate(tiles):
                    ps, pb = accs[ti]
                    xs_ = self.x[:, t, b * CW:(b + 1) * CW]
                    S.op("dve", [pb, self.xB[t]], [self.xB[t]], lambda e, xs_=xs_, ps=ps: e.scalar_tensor_tensor(
                        out=xs_, in0=xs_, scalar=ALPHA, in1=ps[:, 0:CW], op0=ALU.mult, op1=ALU.add), partial=True)
            for t in tiles:
                self.layer_norm(t, 1)


def rope_tables():
    half = 32
    inv = (np.float32(10000.0) ** (-np.arange(half, dtype=np.float32) / np.float32(half))).astype(np.float32)
    tab = np.zeros((NHIST, 128, 64), np.float32)
    for t in range(NHIST):
        if t < 16:
            pos = (t * 128 + np.arange(128)).astype(np.float32)
        else:
            pos = (2048 + np.arange(128)).astype(np.float32)
        ang = pos[:, None] * inv[None, :]
        tab[t, :, 0:32] = np.cos(ang)
        tab[t, :, 32:64] = np.sin(ang)
    return tab


_NC_CACHE = {}


def get_nc(npj, nsj, nl):
    key = (npj, nsj, nl)
    if key not in _NC_CACHE:
        _NC_CACHE[key] = Builder(npj, nsj, nl).build()
    return _NC_CACHE[key]


def make_in_maps(inp, ncores, npj, nsj):
    cs = rope_tables()
    lnp = np.ascontiguousarray(np.stack([inp["ln1_g"], inp["ln1_b"], inp["ln2_g"], inp["ln2_b"]], axis=1))
    relb = np.ascontiguousarray(inp["rel_bias"].reshape(32, 257))
    maps = []
    f = lambda a: np.ascontiguousarray(a, dtype=np.float32)
    for c in range(ncores):
        ps = slice(c * npj, (c + 1) * npj) if npj else slice(0, 1)
        ss = slice(c * nsj, (c + 1) * nsj) if nsj else slice(0, 1)
        maps.append({
            "xp": f(inp["x_prompt"][ps]), "xs": f(inp["x_sample"][ss]),
            "cak": f(inp["cache_a_k"][:, ss].reshape(4, -1, 512, 512)),
            "cav": f(inp["cache_a_v"][:, ss].reshape(4, -1, 512, 512)),
            "cbk": f(inp["cache_b_k"][:, ss].reshape(4, -1, 2048, 512)),
            "cbv": f(inp["cache_b_v"][:, ss].reshape(4, -1, 2048, 512)),
            "cbi": f(inp["cache_b_kidx"][:, ss]),
            "w_in": f(inp["w_in"]), "w_out": f(inp["w_out"]), "w_up": f(inp["w_up"]), "w_down": f(inp["w_down"]),
            "relb": relb, "lnp": lnp, "cs_tab": cs,
        })
    return maps


def kernel(x_prompt, x_sample, cache_a_k, cache_a_v, cache_b_k, cache_b_v, cache_b_kidx,
           w_in, rel_bias, w_out, ln1_g, ln1_b, w_up, w_down, ln2_g, ln2_b):
    inp = dict(x_prompt=np.asarray(x_prompt), x_sample=np.asarray(x_sample), cache_a_k=np.asarray(cache_a_k),
               cache_a_v=np.asarray(cache_a_v), cache_b_k=np.asarray(cache_b_k), cache_b_v=np.asarray(cache_b_v),
               cache_b_kidx=np.asarray(cache_b_kidx), w_in=np.asarray(w_in), rel_bias=np.asarray(rel_bias),
               w_out=np.asarray(w_out), ln1_g=np.asarray(ln1_g), ln1_b=np.asarray(ln1_b), w_up=np.asarray(w_up),
               w_down=np.asarray(w_down), ln2_g=np.asarray(ln2_g), ln2_b=np.asarray(ln2_b))
    ncores, npj, nsj = 8, 4, 2
    nc = get_nc(npj, nsj, 4)
    maps = make_in_maps(inp, ncores, npj, nsj)
    res = run_bass_kernel_spmd(nc, maps, core_ids=list(range(ncores)))
    R = res.results
    cat = lambda name, axis: np.concatenate([np.asarray(r[name]) for r in R], axis=axis)
    y_p = cat("yp", 0)
    y_s = cat("ys", 0)
    akp = cat("akp", 1).reshape(4, 32, 512, 8, 64)
    avp = cat("avp", 1).reshape(4, 32, 512, 8, 64)
    bkp = cat("bkp", 1).reshape(4, 32, 2048, 8, 64)
    bvp = cat("bvp", 1).reshape(4, 32, 2048, 8, 64)
    bip = cat("bip", 1)
    aks = cat("aks", 1).reshape(4, 16, 64, 8, 64)
    avs = cat("avs", 1).reshape(4, 16, 64, 8, 64)
    bks = cat("bks", 1).reshape(4, 16, 64, 8, 64)
    bvs = cat("bvs", 1).reshape(4, 16, 64, 8, 64)
    bis = cat("bis", 1)
    return (y_p, y_s, akp, avp, bkp, bvp, bip, aks, avs, bks, bvs, bis)
```

```python
import numpy as np
import concourse.bass as bass
import concourse.mybir as mybir
from concourse.bass_utils import run_bass_kernel_spmd
from contextlib import ExitStack
from functools import partial as _partial

F32 = mybir.dt.float32
BF = mybir.dt.bfloat16
ALU = mybir.AluOpType
AF = mybir.ActivationFunctionType
AX = mybir.AxisListType.X

D = 1024
NL_FULL = 4
EW = 3656
DFF = 4096
SEQ = 2048
NT = 16
NHIST = 17
ALPHA = float(8 ** 0.25)
EPS = 1e-5
NEG = -30000.0
NEGBIG = -1.0e30
NIT = 14
TOPK = 256.0
CW = 256
WCH = 512
WI_SCALE = float(512 ** -0.5)


import os as _os
STOP = int(_os.environ.get("KSTOP", "99"))


class StopBuild(Exception):
    pass


class Buf:
    __slots__ = ("name", "w", "r")

    def __init__(self, name=""):
        self.name = name
        self.w = {}
        self.r = {}


class Sch:
    ENG = ("pe", "act", "dve", "pool", "sp")

    def __init__(self, nc, es, ndma=12, nepoch=8):
        self.nc = nc
        self.eng = {"pe": nc.tensor, "act": nc.scalar, "dve": nc.vector, "pool": nc.gpsimd, "sp": nc.sync}
        self.cnt = {e: 0 for e in self.ENG}
        self.semh = {}
        self.epoch = 0
        for ep in range(nepoch):
            for e in self.ENG:
                if e == "sp":
                    continue
                self.semh[(e, ep)] = es.enter_context(nc.semaphore(f"s_{e}_{ep}"))
        self.dq = {}
        for q in ("sp", "pool"):
            sems = []
            for i in range(ndma):
                k = ("d", q, i)
                self.semh[k] = es.enter_context(nc.semaphore(f"d_{q}_{i}"))
                sems.append(k)
            self.dq[q] = {"sems": sems, "uses": [0] * ndma, "next": 0}
        self.waited = {e: {} for e in self.ENG}
        self.same_win = 3
        self.ninstr = 0

    def _wait(self, e, k, v):
        if self.waited[e].get(k, 0) >= v:
            return
        self.eng[e].wait_ge(self.semh[k], v)
        self.waited[e][k] = v
        self.ninstr += 1

    def _deps(self, e, reads, writes):
        need = {}
        for b in reads:
            for k, v in b.w.items():
                if need.get(k, 0) < v:
                    need[k] = v
        for b in writes:
            for k, v in b.w.items():
                if need.get(k, 0) < v:
                    need[k] = v
            for k, v in b.r.items():
                if need.get(k, 0) < v:
                    need[k] = v
        for k, v in need.items():
            if k[0] != "d":
                if k[1] != self.epoch:
                    continue
                if k[0] == e:
                    if e == "pe":
                        continue
                    if v <= self.cnt[e] - self.same_win:
                        continue
            self._wait(e, k, v)

    def _mark(self, tok, reads, writes, partial):
        k, v = tok
        for b in reads:
            if b.r.get(k, 0) < v:
                b.r[k] = v
        for b in writes:
            if partial:
                if b.w.get(k, 0) < v:
                    b.w[k] = v
            else:
                b.w = {k: v}
                b.r = {}

    def op(self, e, reads, writes, emit, partial=False):
        self._deps(e, reads, writes)
        ins = emit(self.eng[e])
        self.cnt[e] += 1
        self.ninstr += 1
        k = (e, self.epoch)
        ins.then_inc(self.semh[k], 1)
        self._mark((k, self.cnt[e]), reads, writes, partial)
        return ins

    def dma(self, q, out, in_, reads, writes, partial=False):
        self._deps(q, reads, writes)
        d = self.dq[q]
        i = d["next"]
        d["next"] = (i + 1) % len(d["sems"])
        k = d["sems"][i]
        if d["uses"][i] > 0:
            self._wait(q, k, 16 * d["uses"][i])
        ins = self.eng[q].dma_start(out=out, in_=in_)
        ins.then_inc(self.semh[k], 16)
        self.ninstr += 1
        d["uses"][i] += 1
        self._mark((k, 16 * d["uses"][i]), reads, writes, partial)
        return ins

    def barrier(self, engines):
        for e in engines:
            for k in ("pe", "act", "dve", "pool"):
                if k == e:
                    continue
                if self.cnt[k] > 0:
                    self._wait(e, (k, self.epoch), self.cnt[k])

    def wait_all_dma(self, e):
        for q, d in self.dq.items():
            for i, k in enumerate(d["sems"]):
                if d["uses"][i] > 0:
                    self._wait(e, k, 16 * d["uses"][i])

    def new_epoch(self):
        self.barrier(self.ENG)
        for e in self.ENG:
            self.wait_all_dma(e)
        self.epoch += 1
        for e in self.ENG:
            self.cnt[e] = 0


class Builder:
    def __init__(self, npj=4, nsj=2, nl=4):
        self.npj, self.nsj, self.nl = npj, nsj, nl
        nc = self.nc = bass.Bass("TRN2", target_bir_lowering=False)
        dt = nc.dram_tensor
        I, O, N = "ExternalInput", "ExternalOutput", "Internal"
        nsj_ = max(nsj, 1)
        npj_ = max(npj, 1)
        self.xp = dt("xp", [npj_, SEQ, D], F32, kind=I)
        self.xs = dt("xs", [nsj_, 64, D], F32, kind=I)
        self.cak = dt("cak", [NL_FULL, nsj_, 512, 512], F32, kind=I)
        self.cav = dt("cav", [NL_FULL, nsj_, 512, 512], F32, kind=I)
        self.cbk = dt("cbk", [NL_FULL, nsj_, SEQ, 512], F32, kind=I)
        self.cbv = dt("cbv", [NL_FULL, nsj_, SEQ, 512], F32, kind=I)
        self.cbi = dt("cbi", [NL_FULL, nsj_, SEQ, 64], F32, kind=I)
        self.w_in = dt("w_in", [NL_FULL, D, EW], F32, kind=I)
        self.w_out = dt("w_out", [NL_FULL, D, D], F32, kind=I)
        self.w_up = dt("w_up", [NL_FULL, D, DFF], F32, kind=I)
        self.w_down = dt("w_down", [NL_FULL, DFF, D], F32, kind=I)
        self.relb = dt("relb", [NL_FULL * 8, 257], F32, kind=I)
        self.lnp_d = dt("lnp", [NL_FULL, 4, D], F32, kind=I)
        self.cs_d = dt("cs_tab", [NHIST, 128, 64], F32, kind=I)
        self.yp = dt("yp", [npj_, SEQ, D], F32, kind=O)
        self.ys = dt("ys", [nsj_, 64, D], F32, kind=O)
        self.akp = dt("akp", [NL_FULL, npj_, 512, 512], F32, kind=O)
        self.avp = dt("avp", [NL_FULL, npj_, 512, 512], F32, kind=O)
        self.bkp = dt("bkp", [NL_FULL, npj_, SEQ, 512], F32, kind=O)
        self.bvp = dt("bvp", [NL_FULL, npj_, SEQ, 512], F32, kind=O)
        self.bip = dt("bip", [NL_FULL, npj_, SEQ, 64], F32, kind=O)
        self.aks = dt("aks", [NL_FULL, nsj_, 64, 512], F32, kind=O)
        self.avs = dt("avs", [NL_FULL, nsj_, 64, 512], F32, kind=O)
        self.bks = dt("bks", [NL_FULL, nsj_, 64, 512], F32, kind=O)
        self.bvs = dt("bvs", [NL_FULL, nsj_, 64, 512], F32, kind=O)
        self.bis = dt("bis", [NL_FULL, nsj_, 64, 64], F32, kind=O)
        self.w_in16 = dt("w_in16", [NL_FULL, D, EW], BF, kind=N)
        self.w_out16 = dt("w_out16", [NL_FULL, D, D], BF, kind=N)
        self.w_up16 = dt("w_up16", [NL_FULL, D, DFF], BF, kind=N)
        self.w_down16 = dt("w_down16", [NL_FULL, DFF, D], BF, kind=N)
        self.ext_d = dt("ext_tab", [NL_FULL * 8, 768], F32, kind=N)

    def T(self, name, shape, dtype):
        return self.es.enter_context(self.nc.sbuf_tensor(name, shape, dtype))

    def build(self):
        nc = self.nc
        with ExitStack() as es:
            self.es = es
            S = self.S = Sch(nc, es)
            T = self.T
            self.x = T("x", [128, NT, D], F32)
            self.xB = [Buf(f"x{t}") for t in range(NT)]
            self.kbT = T("kbT", [128, 4, NHIST * 128], BF)
            self.vbA = T("vbA", [128, NHIST, 8, 65], BF)
            self.kiT = T("kiT", [128, NHIST * 128], BF)
            self.hB = [Buf(f"hist{t}") for t in range(NHIST)]
            self.kaT = T("kaT", [128, 4, 8 * 128], BF)
            self.vaA = T("vaA", [128, 8, 8, 65], BF)
            self.aB = [Buf(f"band{t}") for t in range(8)]
            self.BT = T("BT", [128, 8, 5, 128], BF)
            self.BTb = Buf("BT")
            self.cs = T("cs", [128, NHIST, 64], F32)
            self.csb = Buf("cs")
            self.ident = T("ident", [128, 128], BF)
            self.flipJ = T("flipJ", [128, 128], BF)
            self.negI = T("negI", [128, 128], BF)
            self.constb = Buf("const")
            self.pow2 = T("pow2", [128, NIT], F32)
            self.cbias = T("cbias", [128, 2], F32)
            self.wbuf = [T(f"wbuf{i}", [128, 8, WCH], BF) for i in range(2)]
            self.wB = [Buf(f"wbuf{i}") for i in range(2)]
            self.wuse = 0
            self.T8 = T("T8", [128, 8, 512], BF)
            self.T8b = Buf("T8")
            self.lng = None
            self.lngB = Buf("lng")
            self.stage = [T(f"stage{i}", [128, CW], F32) for i in range(2)]
            self.stB = [Buf(f"stage{i}") for i in range(2)]
            self.stuse = 0
            self.hank = [T(f"hank{i}", [128, 5, 128], BF) for i in range(1)]
            self.hkB = [Buf(f"hank{i}") for i in range(1)]
            self.xb16 = T("xb16", [128, D], BF)
            self.xb16B = Buf("xb16")
            self.small = T("small", [128, 64], F32)
            self.smB = Buf("small")
            P = lambda n, sh, d_: es.enter_context(nc.psum_tensor(n, sh, d_))
            self.psA = [P(f"psA{i}", [128, 512], F32) for i in range(2)]
            self.psT = [P(f"psT{i}", [128, 1024], BF) for i in range(2)]
            self.psS = [P(f"psS{i}", [128, 512], F32) for i in range(2)]
            self.psO = [P(f"psO{i}", [128, 512], F32) for i in range(2)]
            self.pAB = [Buf(f"psA{i}") for i in range(2)]
            self.pTB = [Buf(f"psT{i}") for i in range(2)]
            self.pSB = [Buf(f"psS{i}") for i in range(2)]
            self.pOB = [Buf(f"psO{i}") for i in range(2)]
            self.ua = self.ut = self.us = 0
            self.uid = 0
            self.njob = 0
            self.wcB = {(w, l): Buf(f"wc_{w}{l}") for w in ("in", "out", "up", "down") for l in range(NL_FULL)}
            self.extB = Buf("ext")
            self.outB = Buf("out")

            try:
                self.prologue()
                self.stopflag = False
                for j in range(self.npj):
                    if not self.stopflag:
                        self.job("p", j)
                for j in range(self.nsj):
                    if not self.stopflag:
                        self.job("s", j)
            except StopBuild:
                pass
            S.barrier(["sp"])
            S.wait_all_dma("sp")
        return nc

    def prologue(self):
        S, nc = self.S, self.nc
        cb = self.constb
        tmpf, tb = self.stage[0][:, 0:128], self.stB[0]
        t32, t32b = self.x[0:32, 1, 0:257], self.xB[1]
        e32, e32b = self.x[0:32, 0, 0:768], self.xB[0]
        S.op("pool", [], [tb], lambda e: e.memset(tmpf, 1.0))
        S.op("pool", [tb], [tb], lambda e: e.affine_select(out=tmpf, in_=tmpf, pattern=[[-1, 128]],
                                                           compare_op=ALU.is_equal, fill=0.0, base=0,
                                                           channel_multiplier=1))
        S.op("dve", [tb], [cb], lambda e: e.tensor_copy(self.ident[:], tmpf), partial=True)
        S.op("dve", [tb], [cb], lambda e: e.tensor_scalar(out=self.negI[:], in0=tmpf, scalar1=NEG, scalar2=None,
                                                          op0=ALU.mult), partial=True)
        S.op("pool", [tb], [tb], lambda e: e.memset(tmpf, 1.0))
        S.op("pool", [tb], [tb], lambda e: e.affine_select(out=tmpf, in_=tmpf, pattern=[[1, 128]],
                                                           compare_op=ALU.is_equal, fill=0.0, base=-127,
                                                           channel_multiplier=1))
        S.op("dve", [tb], [cb], lambda e: e.tensor_copy(self.flipJ[:], tmpf), partial=True)
        for i in range(NIT):
            S.op("pool", [], [cb], lambda e, i=i: e.memset(self.pow2[:, i:i + 1], float(2.0 ** -(i + 1))),
                 partial=True)
        S.op("pool", [], [cb], lambda e: e.memset(self.cbias[:, 0:1], EPS), partial=True)
        S.dma("sp", self.cs[:], self.cs_d.ap().rearrange("t p c -> p t c"), [], [self.csb])
        S.dma("sp", t32, self.relb.ap(), [], [t32b])
        S.op("dve", [t32b], [e32b], lambda e: e.tensor_copy(e32[:, 0:256], t32[:, 1:257]), partial=True)
        S.op("dve", [t32b], [e32b], lambda e: e.tensor_copy(e32[:, 256:768],
                                                            t32[:, 256:257].to_broadcast([32, 512])), partial=True)
        S.dma("sp", self.ext_d.ap(), e32, [e32b], [self.extB])
        for l in range(self.nl):
            for name, src, dst, rows in (("in", self.w_in, self.w_in16, D), ("out", self.w_out, self.w_out16, D),
                                         ("up", self.w_up, self.w_up16, D),
                                         ("down", self.w_down, self.w_down16, DFF)):
                for r0 in range(0, rows, 128):
                    S.dma("pool", dst.ap()[l, r0:r0 + 128, :], src.ap()[l, r0:r0 + 128, :], [],
                          [self.wcB[(name, l)]], partial=True)

    def wchunk(self, name, l, a, b_=0, width=WCH):
        i = self.wuse % 2
        self.wuse += 1
        wt, wb = self.wbuf[i], self.wB[i]
        if name == "in":
            src = bass.AP(self.w_in16, l * D * EW + a * WCH, [[EW, 128], [128 * EW, 8], [1, width]])
        elif name == "out":
            src = bass.AP(self.w_out16, l * D * D + a * WCH, [[D, 128], [128 * D, 8], [1, width]])
        elif name == "up":
            src = bass.AP(self.w_up16, l * D * DFF + a * WCH, [[DFF, 128], [128 * DFF, 8], [1, width]])
        else:
            src = bass.AP(self.w_down16, l * DFF * D + b_ * 8 * 128 * D + a * WCH, [[D, 128], [128 * D, 8], [1, width]])
        self.S.dma("sp", wt[:, :, 0:width], src, [self.wcB[(name, l)]], [wb])
        return wt, wb

    def nxt(self, which):
        if which == "A":
            i = self.ua % 2; self.ua += 1
            return self.psA[i], self.pAB[i]
        if which == "T":
            i = self.ut % 2; self.ut += 1
            return self.psT[i], self.pTB[i]
        i = self.us % 2; self.us += 1
        return self.psS[i], self.pSB[i]

    def nstage(self):
        i = self.stuse % 2
        self.stuse += 1
        return self.stage[i], self.stB[i]

    def job(self, kind, j):
        S, nc = self.S, self.nc
        prompt = kind == "p"
        if self.njob > 0:
            S.new_epoch()
        self.njob += 1
        ntile = NT if prompt else 1
        if prompt:
            for t in range(NT):
                S.dma("sp", self.x[:, t, :], self.xp.ap()[j, t * 128:(t + 1) * 128, :], [], [self.xB[t]])
        else:
            S.op("pool", [], [self.xB[0]], lambda e: e.memset(self.x[:, 0, :], 0.0))
            S.dma("sp", self.x[0:64, 0, :], self.xs.ap()[j, :, :], [], [self.xB[0]], partial=True)
        for l in range(self.nl):
            if STOP <= 1:
                self.stopflag = True
                return
            self.build_bias(l)
            if STOP <= 2:
                self.stopflag = True
                return
            if not prompt:
                self.load_caches(l, j)
            groups = [list(range(g * 4, g * 4 + 4)) for g in range(4)] if prompt else [[0]]
            for tiles in groups:
                self.phaseA(kind, j, l, tiles)
                S.barrier(["pe", "act", "dve", "pool"])
                if self.stopflag:
                    return
                self.phaseB(kind, j, l, tiles)
                S.barrier(["pe", "act", "dve", "pool"])
                if self.stopflag:
                    return
        if prompt:
            for t in range(NT):
                S.dma("sp", self.yp.ap()[j, t * 128:(t + 1) * 128, :], self.x[:, t, :], [self.xB[t]], [self.outB],
                      partial=True)
        else:
            S.dma("sp", self.ys.ap()[j, :, :], self.x[0:64, 0, :], [self.xB[0]], [self.outB], partial=True)

    def build_bias(self, l):
        S = self.S
        BB = int(_os.environ.get("KBB", "9"))
        for h in range(8):
            hk, hb = self.hank[0], self.hkB[0]
            src = bass.AP(self.ext_d, (l * 8 + h) * 768, [[1, 128], [128, 5], [1, 128]])
            S.dma("pool", hk[:], src, [self.extB], [hb])
            if BB <= 1:
                continue
            ps, pb = self.nxt("A")
            for rp in range(4):
                S.op("pe", [hb, self.constb], [pb],
                     lambda e, rp=rp: e.matmul(ps[:, rp * 128:(rp + 1) * 128], self.flipJ[:], hk[:, rp, :],
                                               start=True, stop=True))
            ps2, pb2 = self.nxt("A")
            S.op("pe", [hb, self.constb], [pb2],
                 lambda e: e.matmul(ps2[:, 0:128], self.flipJ[:], hk[:, 4, :], start=True, stop=True))
            if BB <= 2:
                continue
            for rp in range(4):
                S.op("act", [pb], [self.BTb],
                     lambda e, rp=rp: e.activation(out=self.BT[:, h, 4 - rp, :], in_=ps[:, rp * 128:(rp + 1) * 128],
                                                   func=AF.Exp), partial=True)
            S.op("act", [pb2], [self.BTb],
                 lambda e: e.activation(out=self.BT[:, h, 0, :], in_=ps2[:, 0:128], func=AF.Exp), partial=True)
        if BB <= 3:
            return
        S.op("dve", [self.BTb], [self.BTb], lambda e: e.memset(self.BT[64:128, :, 4, 0:64], 0.0), partial=True)
        S.op("dve", [self.BTb], [self.BTb], lambda e: e.memset(self.BT[0:64, :, 0, 64:128], 0.0), partial=True)

    def transposes_to(self, src_ap_fn, nblk, src_bufs, dst_fn, dst_bufs, evac_eng="act"):
        S = self.S
        i = 0
        while i < nblk:
            n = min(8, nblk - i)
            ps, pb = self.nxt("T")
            for q in range(n):
                S.op("pe", src_bufs + [self.constb], [pb],
                     lambda e, q=q, i=i: e.transpose(ps[:, q * 128:(q + 1) * 128], src_ap_fn(i + q), self.ident[:]))
            dst_fn(i, n, ps, pb)
            i += n

    def load_caches(self, l, s):
        S = self.S
        self.uid += 1
        with self.nc.sbuf_tensor(f"cst_{self.uid}", [128, 2, 512], BF) as cst, \
                self.nc.sbuf_tensor(f"cst2_{self.uid}", [128, 2, 128], BF) as cst2:
            cB = [Buf("cst0"), Buf("cst1")]
            c2B = [Buf("cst20"), Buf("cst21")]
            u = 0
            for t in range(16):
                rows = slice(t * 128, (t + 1) * 128)
                i = u % 2; u += 1
                hb = self.hB[t]
                S.dma("pool", cst[:, i, :], self.cbk.ap()[l, s, rows, :], [], [cB[i]])

                def dst(i0, n, ps, pb, t=t, hb=hb):
                    S.op("act", [pb], [hb], lambda e: e.activation(
                        out=self.kbT[:, :, t * 128:(t + 1) * 128],
                        in_=ps[:, 0:512].rearrange("p (a b) -> p a b", a=4), func=AF.Copy), partial=True)
                self.transposes_to(lambda q, i=i: cst[:, i, q * 128:(q + 1) * 128], 4, [cB[i]], dst, [hb])
                S.dma("pool", self.vbA[:, t, :, 0:64], self.cbv.ap()[l, s, rows, :].rearrange("p (h d) -> p h d", h=8),
                      [], [hb], partial=True)
                S.op("pool", [], [hb], lambda e, t=t: e.memset(self.vbA[:, t, :, 64:65], 1.0), partial=True)
                S.dma("pool", cst2[:, i, 0:64], self.cbi.ap()[l, s, rows, :], [], [c2B[i]], partial=True)
                S.dma("pool", cst2[:, i, 64:128], self.cbi.ap()[l, s, rows, :], [], [c2B[i]], partial=True)

                def dst2(i0, n, ps, pb, t=t, hb=hb):
                    S.op("act", [pb], [hb], lambda e: e.activation(
                        out=self.kiT[:, t * 128:(t + 1) * 128], in_=ps[:, 0:128], func=AF.Copy), partial=True)
                self.transposes_to(lambda q, i=i: cst2[:, i, :], 1, [c2B[i]], dst2, [hb])
            for t in range(4):
                rows = slice(t * 128, (t + 1) * 128)
                i = u % 2; u += 1
                ab = self.aB[t]
                S.dma("pool", cst[:, i, :], self.cak.ap()[l, s, rows, :], [], [cB[i]])

                def dst3(i0, n, ps, pb, t=t, ab=ab):
                    S.op("act", [pb], [ab], lambda e: e.activation(
                        out=self.kaT[:, :, t * 128:(t + 1) * 128],
                        in_=ps[:, 0:512].rearrange("p (a b) -> p a b", a=4), func=AF.Copy), partial=True)
                self.transposes_to(lambda q, i=i: cst[:, i, q * 128:(q + 1) * 128], 4, [cB[i]], dst3, [ab])
                S.dma("pool", self.vaA[:, t, :, 0:64], self.cav.ap()[l, s, rows, :].rearrange("p (h d) -> p h d", h=8),
                      [], [ab], partial=True)
                S.op("pool", [], [ab], lambda e, t=t: e.memset(self.vaA[:, t, :, 64:65], 1.0), partial=True)
            S.barrier(["pe", "act", "dve", "pool"])
            S.wait_all_dma("pool")

    def make_xT(self, tiles):
        S = self.S
        for ti, t in enumerate(tiles):
            S.op("act", [self.xB[t]], [self.xb16B], lambda e, t=t: e.activation(out=self.xb16[:], in_=self.x[:, t, :],
                                                                                 func=AF.Copy))

            def dst(i0, n, ps, pb, ti=ti):
                S.op("dve", [pb], [self.T8b], lambda e: e.tensor_copy(
                    self.T8[:, :, ti * 128:(ti + 1) * 128], ps[:, 0:1024].rearrange("p (a b) -> p a b", a=8)),
                    partial=True)
            self.transposes_to(lambda q: self.xb16[:, q * 128:(q + 1) * 128], 8, [self.xb16B], dst, [self.T8b])

    def phaseA(self, kind, j, l, tiles):
        S, nc = self.S, self.nc
        prompt = kind == "p"
        ng = len(tiles)
        ntok = ng * 128
        with ExitStack() as es:
            self.uid += 1
            A = lambda n, sh, d_: es.enter_context(nc.sbuf_tensor(f"{n}_{self.uid}", sh, d_))
            qaT = A("qaT", [128, 4, 512], BF); qbT = A("qbT", [128, 4, 512], BF); qiT = A("qiT", [128, 4, 512], BF)
            qB = Buf("q")
            wsb = A("wsb", [128, 4, 8], F32); wsbB = Buf("wsb")
            hb16 = [A(f"hb16_{i}", [128, CW], BF) for i in range(4)]
            hbB = [Buf(f"hb16_{i}") for i in range(4)]
            t1 = A("t1", [128, CW], F32); t2 = A("t2", [128, CW], F32); t12B = Buf("t12")
            score = A("score", [128, NHIST * 128], F32); scB = Buf("score")
            rsb = [A(f"rsb{i}", [128, 512], F32) for i in range(2)]
            rB = [Buf(f"rsb{i}") for i in range(2)]
            maskb = A("maskb", [128, NHIST * 128], BF); mkB = Buf("maskb")
            mbT = A("mbT", [128, NHIST, 128], BF); mtB = Buf("mbT")
            expT = [A(f"expT{i}", [128, 640], BF) for i in range(2)]
            eB = [Buf(f"expT{i}") for i in range(2)]
            otok = A("otok", [128, 512], BF); otB = Buf("otok")
            st = A("st", [128, 48], F32); stB_ = Buf("st")
            steps = A("steps", [128, NIT], F32)

            self.make_xT(tiles)
            T8 = self.T8
            if STOP <= 3:
                self.stopflag = True
                return

            hu = 0
            deferq = []
            for cc in range(8):
              width = WCH if cc < 7 else 72
              wt, wb = self.wchunk("in", l, cc, width=width)
              seg = cc
              for ti, t in enumerate(tiles):
                psF, pb = self.nxt("A")
                for k in range(8):
                    S.op("pe", [self.T8b, wb], [pb],
                         lambda e, k=k, ti=ti: e.matmul(psF[:, 0:width], T8[:, k, ti * 128:(ti + 1) * 128],
                                                        wt[:, k, 0:width], start=(k == 0), stop=(k == 7)))
                for fn_ in deferq:
                    fn_()
                deferq.clear()
                for hh in range(2 if cc < 7 else 1):
                    ps = psF[:, hh * CW:(hh + 1) * CW] if cc < 7 else psF[:, 0:CW]
                    hidx = t if prompt else 16
                    aslot = (t % 8) if prompt else 4
                    cs_c = self.cs[:, hidx, 0:32]
                    cs_s = self.cs[:, hidx, 32:64]
                    nrow = 128 if prompt else 64

                    def rope(src, srcB, nh):
                        v = lambda a: a.rearrange("p (h two d) -> p h two d", h=nh, two=2)
                        bc = lambda a: a.unsqueeze(1).to_broadcast([128, nh, 32])
                        S.op("pool", [srcB, self.csb], [t12B], lambda e: e.tensor_tensor(
                            out=v(t1[:, 0:nh * 64]), in0=v(src),
                            in1=cs_c.unsqueeze(1).unsqueeze(1).to_broadcast([128, nh, 2, 32]), op=ALU.mult))
                        S.op("pool", [srcB, self.csb], [t12B], lambda e: e.tensor_tensor(
                            out=v(t2[:, 0:nh * 64])[:, :, 0, :], in0=v(src)[:, :, 1, :], in1=bc(cs_s), op=ALU.mult),
                            partial=True)
                        S.op("pool", [srcB, self.csb], [t12B], lambda e: e.tensor_tensor(
                            out=v(t2[:, 0:nh * 64])[:, :, 1, :], in0=v(src)[:, :, 0, :], in1=bc(cs_s), op=ALU.mult),
                            partial=True)
                        S.op("pool", [t12B], [t12B], lambda e: e.tensor_tensor(
                            out=v(t1[:, 0:nh * 64])[:, :, 0, :], in0=v(t1[:, 0:nh * 64])[:, :, 0, :],
                            in1=v(t2[:, 0:nh * 64])[:, :, 0, :], op=ALU.subtract), partial=True)
                        S.op("pool", [t12B], [t12B], lambda e: e.tensor_tensor(
                            out=v(t1[:, 0:nh * 64])[:, :, 1, :], in0=v(t1[:, 0:nh * 64])[:, :, 1, :],
                            in1=v(t2[:, 0:nh * 64])[:, :, 1, :], op=ALU.add), partial=True)

                    def out_dma(dram, col0, ncol, src_t, src_b):
                        S.dma("sp", dram[0:nrow, col0:col0 + ncol], src_t[0:nrow, 0:ncol], [src_b], [self.outB],
                              partial=True)

                    def tr_to(dstT, hbt, hbb, ti=ti, hh=hh):
                        def dst(i0, n, ps_, pb_):
                            S.op("act", [pb_], [qB], lambda e: e.activation(
                                out=dstT[:, 2 * hh:2 * hh + 2, ti * 128:(ti + 1) * 128],
                                in_=ps_[:, 0:256].rearrange("p (a b) -> p a b", a=2), func=AF.Copy), partial=True)
                        self.transposes_to(lambda q: hbt[:, q * 128:(q + 1) * 128], 2, [hbb], dst, [qB])

                    hi_ = hu % 4
                    if seg == 0:
                        hu += 1
                        S.op("act", [pb], [hbB[hi_]], lambda e: e.activation(out=hb16[hi_][:], in_=ps[:, 0:CW],
                                                                             func=AF.Copy, scale=0.125))
                        deferq.append(_partial(tr_to, qaT, hb16[hi_], hbB[hi_]))
                    elif seg == 1:
                        hu += 1
                        ab = self.aB[aslot]
                        S.op("act", [pb], [hbB[hi_]], lambda e: e.activation(out=hb16[hi_][:], in_=ps[:, 0:CW],
                                                                             func=AF.Copy))
                        if (prompt and t >= 12) or not prompt:
                            sg, sgb = self.nstage()
                            S.op("act", [pb], [sgb], lambda e: e.activation(out=sg[:, 0:CW], in_=ps[:, 0:CW], func=AF.Copy))
                            dram = (self.akp.ap()[l, j, (t - 12) * 128:(t - 11) * 128, :] if prompt
                                    else self.aks.ap()[l, j, :, :])
                            out_dma(dram, hh * CW, CW, sg, sgb)

                        def dstk(i0, n, ps_, pb_, aslot=aslot, ab=ab, hh=hh):
                            S.op("act", [pb_], [ab], lambda e: e.activation(
                                out=self.kaT[:, 2 * hh:2 * hh + 2, aslot * 128:(aslot + 1) * 128],
                                in_=ps_[:, 0:256].rearrange("p (a b) -> p a b", a=2), func=AF.Copy), partial=True)
                        deferq.append(_partial(self.transposes_to, lambda q, hi_=hi_: hb16[hi_][:, q * 128:(q + 1) * 128],
                                               2, [hbB[hi_]], dstk, [ab]))
                    elif seg == 2:
                        ab = self.aB[aslot]
                        S.op("act", [pb], [ab], lambda e: e.activation(
                            out=self.vaA[:, aslot, 4 * hh:4 * hh + 4, 0:64],
                            in_=ps[:, 0:CW].rearrange("p (h d) -> p h d", h=4), func=AF.Copy), partial=True)
                        if hh == 0:
                            S.op("pool", [], [ab], lambda e: e.memset(self.vaA[:, aslot, :, 64:65], 1.0), partial=True)
                        if (prompt and t >= 12) or not prompt:
                            sg, sgb = self.nstage()
                            S.op("act", [pb], [sgb], lambda e: e.activation(out=sg[:, 0:CW], in_=ps[:, 0:CW], func=AF.Copy))
                            dram = (self.avp.ap()[l, j, (t - 12) * 128:(t - 11) * 128, :] if prompt
                                    else self.avs.ap()[l, j, :, :])
                            out_dma(dram, hh * CW, CW, sg, sgb)
                    elif seg in (3, 4, 6):
                        hu += 1
                        sg, sgb = self.nstage()
                        S.op("act", [pb], [sgb], lambda e: e.activation(out=sg[:, 0:CW], in_=ps[:, 0:CW], func=AF.Copy))
                        rope(sg[:, 0:CW], sgb, 4)
                        if seg == 4:
                            S.op("pool", [t12B], [sgb], lambda e: e.tensor_copy(sg[:, 0:CW], t1[:, 0:CW]))
                            dram = (self.bkp.ap()[l, j, t * 128:(t + 1) * 128, :] if prompt else self.bks.ap()[l, j, :, :])
                            out_dma(dram, hh * CW, CW, sg, sgb)
                            S.op("pool", [sgb], [hbB[hi_]], lambda e: e.tensor_copy(hb16[hi_][:], sg[:, 0:CW]))
                            hb_ = self.hB[hidx]

                            def dstk(i0, n, ps_, pb_, hidx=hidx, hb_=hb_, hh=hh):
                                S.op("act", [pb_], [hb_], lambda e: e.activation(
                                    out=self.kbT[:, 2 * hh:2 * hh + 2, hidx * 128:(hidx + 1) * 128],
                                    in_=ps_[:, 0:256].rearrange("p (a b) -> p a b", a=2), func=AF.Copy), partial=True)
                            deferq.append(_partial(self.transposes_to,
                                                   lambda q, hi_=hi_: hb16[hi_][:, q * 128:(q + 1) * 128], 2,
                                                   [hbB[hi_]], dstk, [hb_]))
                        else:
                            sc = 0.125 if seg == 3 else 1.0
                            S.op("pool", [t12B], [hbB[hi_]], lambda e: e.tensor_scalar(
                                out=hb16[hi_][:], in0=t1[:, 0:CW], scalar1=sc, scalar2=None, op0=ALU.mult))
                            deferq.append(_partial(tr_to, qbT if seg == 3 else qiT, hb16[hi_], hbB[hi_]))
                    elif seg == 5:
                        hb_ = self.hB[hidx]
                        S.op("act", [pb], [hb_], lambda e: e.activation(
                            out=self.vbA[:, hidx, 4 * hh:4 * hh + 4, 0:64],
                            in_=ps[:, 0:CW].rearrange("p (h d) -> p h d", h=4), func=AF.Copy), partial=True)
                        if hh == 0:
                            S.op("pool", [], [hb_], lambda e: e.memset(self.vbA[:, hidx, :, 64:65], 1.0), partial=True)
                        sg, sgb = self.nstage()
                        S.op("act", [pb], [sgb], lambda e: e.activation(out=sg[:, 0:CW], in_=ps[:, 0:CW], func=AF.Copy))
                        dram = (self.bvp.ap()[l, j, t * 128:(t + 1) * 128, :] if prompt else self.bvs.ap()[l, j, :, :])
                        out_dma(dram, hh * CW, CW, sg, sgb)
                    else:
                        hu += 1
                        sg, sgb = self.nstage()
                        S.op("act", [pb], [sgb], lambda e: e.activation(out=sg[:, 0:72], in_=ps[:, 0:72], func=AF.Copy))
                        rope(sg[:, 0:64], sgb, 1)
                        S.op("pool", [sgb], [wsbB], lambda e: e.tensor_scalar(
                            out=wsb[:, ti, :], in0=sg[:, 64:72], scalar1=WI_SCALE, scalar2=None, op0=ALU.mult),
                            partial=True)
                        S.op("pool", [t12B], [sgb], lambda e: e.tensor_copy(sg[:, 0:64], t1[:, 0:64]))
                        dram = (self.bip.ap()[l, j, t * 128:(t + 1) * 128, :] if prompt else self.bis.ap()[l, j, :, :])
                        out_dma(dram, 0, 64, sg, sgb)
                        S.op("pool", [sgb], [hbB[hi_]], lambda e: e.tensor_copy(hb16[hi_][:, 0:64], sg[:, 0:64]))
                        S.op("pool", [sgb], [hbB[hi_]], lambda e: e.tensor_copy(hb16[hi_][:, 64:128], sg[:, 0:64]),
                             partial=True)
                        hb_ = self.hB[hidx]

                        def dstk(i0, n, ps_, pb_, hidx=hidx, hb_=hb_):
                            S.op("act", [pb_], [hb_], lambda e: e.activation(
                                out=self.kiT[:, hidx * 128:(hidx + 1) * 128], in_=ps_[:, 0:128], func=AF.Copy),
                                partial=True)
                        deferq.append(_partial(self.transposes_to, lambda q, hi_=hi_: hb16[hi_][:, 0:128], 1,
                                               [hbB[hi_]], dstk, [hb_]))

            for fn_ in deferq:
                fn_()
            deferq.clear()
            if STOP <= 4:
                self.stopflag = True
                return
            def tinfo(t):
                nkt = (t + 1) if prompt else NHIST
                return nkt, nkt * 128

            def band(ti, t):
                qc = slice(ti * 128, (ti + 1) * 128)
                if prompt:
                    win = [(tt, tt - (t - 4), tt % 8) for tt in range(max(0, t - 4), t + 1)]
                else:
                    win = [(r, r, r) for r in range(5)]
                prev_pv = None
                for h in range(8):
                    pr, hf = h // 2, (h % 2) * 64
                    psS, pSb = self.nxt("S")
                    psX, pXb = self.psA[h % 2], self.pAB[h % 2]
                    for wi_, (tt, r, slot) in enumerate(win):
                        o_ps, o_b = (psS, pSb) if wi_ < 4 else (psX, pXb)
                        oc = slice((wi_ % 4) * 128, (wi_ % 4 + 1) * 128)
                        S.op("pe", [self.aB[slot], qB], [o_b], lambda e, o_ps=o_ps, oc=oc, slot=slot: e.matmul(
                            o_ps[:, oc], self.kaT[hf:hf + 64, pr, slot * 128:(slot + 1) * 128],
                            qaT[hf:hf + 64, pr, qc], start=True, stop=True))
                    ei = h % 2
                    n1 = min(4, len(win))
                    S.op("act", [pSb], [eB[ei]], lambda e, n1=n1, ei=ei: e.activation(
                        out=expT[ei][:, 0:n1 * 128], in_=psS[:, 0:n1 * 128], func=AF.Exp))
                    if len(win) == 5:
                        S.op("act", [pXb], [eB[ei]], lambda e, ei=ei: e.activation(
                            out=expT[ei][:, 512:640], in_=psX[:, 0:128], func=AF.Exp), partial=True)
                    r0 = win[0][1]
                    nw = len(win)
                    S.op("pool", [eB[ei], self.BTb], [eB[ei]], lambda e, ei=ei, r0=r0, nw=nw, h=h: e.tensor_tensor(
                        out=expT[ei][:, 0:nw * 128], in0=expT[ei][:, 0:nw * 128],
                        in1=self.BT[:, h, r0:r0 + nw, :].rearrange("p a b -> p (a b)"), op=ALU.mult))
                    def pv(h=h, ei=ei):
                        po, pob = self.psO[h // 4], self.pOB[h // 4]
                        for wi_, (tt, r, slot) in enumerate(win):
                            S.op("pe", [eB[ei], self.aB[slot]], [pob], lambda e, wi_=wi_, slot=slot: e.matmul(
                                po[:, (h % 4) * 65:(h % 4) * 65 + 65], expT[ei][:, wi_ * 128:(wi_ + 1) * 128],
                                self.vaA[:, slot, h, :], start=(wi_ == 0), stop=(wi_ == len(win) - 1)))
                    if prev_pv is not None:
                        prev_pv()
                    prev_pv = pv
                prev_pv()
                self.finish_attn(st, stB_, otok, otB, 0, ti)

            def idx(ti, t):
                qc = slice(ti * 128, (ti + 1) * 128)
                nkt, N = tinfo(t)
                for h in range(8):
                    pr, hf = h // 2, (h % 2) * 64
                    for k0 in range(0, N, 512):
                        kn = min(512, N - k0)
                        psS, pSb = self.nxt("S")
                        S.op("pe", [qB] + [self.hB[x_] for x_ in range(k0 // 128, (k0 + kn) // 128)], [pSb],
                             lambda e, k0=k0, kn=kn, psS=psS: e.matmul(psS[:, 0:kn], qiT[hf:hf + 64, pr, qc],
                                                                       self.kiT[hf:hf + 64, k0:k0 + kn],
                                                                       start=True, stop=True))
                        ri = self.us % 2
                        S.op("act", [pSb], [rB[ri]], lambda e, kn=kn, ri=ri, psS=psS: e.activation(
                            out=rsb[ri][:, 0:kn], in_=psS[:, 0:kn], func=AF.Relu))
                        if h == 0:
                            S.op("dve", [rB[ri], wsbB], [scB], lambda e, k0=k0, kn=kn, ri=ri: e.tensor_scalar(
                                out=score[:, k0:k0 + kn], in0=rsb[ri][:, 0:kn], scalar1=wsb[:, ti, 0:1], scalar2=None,
                                op0=ALU.mult), partial=True)
                        else:
                            S.op("dve", [rB[ri], wsbB, scB], [scB],
                                 lambda e, k0=k0, kn=kn, ri=ri, h=h: e.scalar_tensor_tensor(
                                     out=score[:, k0:k0 + kn], in0=rsb[ri][:, 0:kn], scalar=wsb[:, ti, h:h + 1],
                                     in1=score[:, k0:k0 + kn], op0=ALU.mult, op1=ALU.add), partial=True)
                if prompt:
                    S.op("dve", [scB], [scB], lambda e: e.memset(score[0:64, N - 64:N], NEGBIG), partial=True)
                else:
                    S.op("dve", [scB], [scB], lambda e: e.memset(score[:, N - 64:N], NEGBIG), partial=True)

            def bis(ti, t):
                nkt, N = tinfo(t)
                need_topk = (not prompt) or t >= 2
                if need_topk:
                    S.op("dve", [scB], [stB_], lambda e: e.tensor_reduce(out=st[:, 0:1], in_=score[:, 0:N], axis=AX,
                                                                         op=ALU.max), partial=True)
                    if prompt:
                        S.op("dve", [scB], [stB_], lambda e: e.tensor_reduce(out=st[0:64, 1:2], in_=score[0:64, 0:N - 64],
                                                                             axis=AX, op=ALU.min), partial=True)
                        S.op("dve", [scB], [stB_], lambda e: e.tensor_reduce(out=st[64:128, 1:2], in_=score[64:128, 0:N],
                                                                             axis=AX, op=ALU.min), partial=True)
                    else:
                        S.op("dve", [scB], [stB_], lambda e: e.tensor_reduce(out=st[:, 1:2], in_=score[:, 0:N - 64],
                                                                             axis=AX, op=ALU.min), partial=True)
                    S.op("dve", [stB_], [stB_], lambda e: e.tensor_tensor(out=st[:, 2:3], in0=st[:, 0:1], in1=st[:, 1:2],
                                                                          op=ALU.subtract))
                    S.op("dve", [stB_, self.constb], [stB_], lambda e: e.tensor_scalar(
                        out=steps[:], in0=self.pow2[:], scalar1=st[:, 2:3], scalar2=None, op0=ALU.mult))
                    S.op("dve", [stB_], [stB_], lambda e: e.tensor_tensor(out=st[:, 4:5], in0=st[:, 1:2],
                                                                          in1=steps[:, 0:1], op=ALU.add))
                    for it in range(NIT):
                        S.op("dve", [stB_, scB], [stB_, mkB], lambda e: e.tensor_scalar(
                            out=maskb[:, 0:N], in0=score[:, 0:N], scalar1=st[:, 4:5], scalar2=None, op0=ALU.is_ge,
                            op1=ALU.add, accum_out=st[:, 5:6]))
                        S.op("dve", [stB_], [stB_], lambda e: e.tensor_scalar(
                            out=st[:, 6:7], in0=st[:, 5:6], scalar1=TOPK, scalar2=0.5, op0=ALU.is_ge,
                            op1=ALU.subtract))
                        S.op("dve", [stB_], [stB_], lambda e, it=it: e.scalar_tensor_tensor(
                            out=st[:, 4:5], in0=st[:, 6:7], scalar=steps[:, it:it + 1], in1=st[:, 4:5],
                            op0=ALU.mult, op1=ALU.add))
                    S.op("dve", [stB_], [stB_], lambda e: e.scalar_tensor_tensor(
                        out=st[:, 3:4], in0=steps[:, NIT - 1:NIT], scalar=-0.5, in1=st[:, 4:5], op0=ALU.mult,
                        op1=ALU.add))
                else:
                    S.op("dve", [stB_], [stB_], lambda e: e.memset(st[:, 3:4], -1.0e29))
                S.op("dve", [stB_, scB], [mkB], lambda e: e.tensor_scalar(
                    out=maskb[:, 0:N], in0=score[:, 0:N], scalar1=st[:, 3:4], scalar2=None, op0=ALU.is_ge))

            def maskT(ti, t):
                nkt, N = tinfo(t)

                def dstm(i0, n, ps_, pb_):
                    S.op("act", [pb_], [mtB], lambda e: e.activation(
                        out=mbT[:, i0:i0 + n, :], in_=ps_[:, 0:n * 128].rearrange("p (a b) -> p a b", a=n),
                        func=AF.Copy), partial=True)
                self.transposes_to(lambda q: maskb[:, q * 128:(q + 1) * 128], nkt, [mkB], dstm, [mtB])

            def sparse_main(ti, t):
                qc = slice(ti * 128, (ti + 1) * 128)
                nkt, N = tinfo(t)
                prev_pv = None
                for h in range(8):
                    pr, hf = h // 2, (h % 2) * 64
                    po, pob = self.psO[h // 4], self.pOB[h // 4]
                    for k0 in range(0, nkt, 4):
                        kn = min(4, nkt - k0)
                        psS, pSb = self.nxt("S")
                        for q in range(kn):
                            kt = k0 + q
                            S.op("pe", [self.hB[kt], qB], [pSb], lambda e, q=q, kt=kt, psS=psS: e.matmul(
                                psS[:, q * 128:(q + 1) * 128], self.kbT[hf:hf + 64, pr, kt * 128:(kt + 1) * 128],
                                qbT[hf:hf + 64, pr, qc], start=True, stop=True))
                        ei = self.us % 2
                        S.op("act", [pSb], [eB[ei]], lambda e, kn=kn, ei=ei, psS=psS: e.activation(
                            out=expT[ei][:, 0:kn * 128], in_=psS[:, 0:kn * 128], func=AF.Exp))
                        S.op("dve", [eB[ei], mtB], [eB[ei]], lambda e, kn=kn, ei=ei, k0=k0: e.tensor_tensor(
                            out=expT[ei][:, 0:kn * 128], in0=expT[ei][:, 0:kn * 128],
                            in1=mbT[:, k0:k0 + kn, :].rearrange("p a b -> p (a b)"), op=ALU.mult))
                        def pv(h=h, ei=ei, k0=k0, kn=kn, po=po, pob=pob):
                            for q in range(kn):
                                kt = k0 + q
                                S.op("pe", [eB[ei], self.hB[kt]], [pob], lambda e, q=q, kt=kt: e.matmul(
                                    po[:, (h % 4) * 65:(h % 4) * 65 + 65], expT[ei][:, q * 128:(q + 1) * 128],
                                    self.vbA[:, kt, h, :], start=(kt == 0), stop=(kt == nkt - 1)))
                        if prev_pv is not None:
                            prev_pv()
                        prev_pv = pv
                prev_pv()

            idx(0, tiles[0])
            bis(0, tiles[0])
            for ti, t in enumerate(tiles):
                band(ti, t)
                if self.chk(5):
                    return
                nxt_ = ti + 1 < len(tiles)
                if nxt_:
                    idx(ti + 1, tiles[ti + 1])
                maskT(ti, t)
                sparse_main(ti, t)
                if nxt_:
                    bis(ti + 1, tiles[ti + 1])
                self.finish_attn(st, stB_, otok, otB, 4, ti)
                if self.chk(6):
                    return

    def chk(self, lvl):
        if STOP <= lvl:
            self.stopflag = True
            return True
        return False

    def finish_attn(self, st, stB_, otok, otB, e0, ti):
        S = self.S
        for half in range(2):
            po, pob = self.psO[half], self.pOB[half]
            pv = po[:, 0:260].rearrange("p (h d) -> p h d", h=4)
            S.op("dve", [pob], [stB_], lambda e, pv=pv, half=half: e.reciprocal(
                st[:, 8 + half * 4:12 + half * 4].unsqueeze(2), pv[:, :, 64:65]), partial=True)
            S.op("dve", [pob, stB_], [otB], lambda e, pv=pv, half=half: e.tensor_tensor(
                out=otok[:, half * 256:(half + 1) * 256].rearrange("p (h d) -> p h d", h=4), in0=pv[:, :, 0:64],
                in1=st[:, 8 + half * 4:12 + half * 4].unsqueeze(2).to_broadcast([128, 4, 64]), op=ALU.mult),
                partial=True)

        def dst(i0, n, ps_, pb_):
            S.op("act", [pb_], [self.T8b], lambda e: e.activation(
                out=self.T8[:, e0:e0 + 4, ti * 128:(ti + 1) * 128],
                in_=ps_[:, 0:512].rearrange("p (a b) -> p a b", a=4), func=AF.Copy), partial=True)
        self.transposes_to(lambda q: otok[:, q * 128:(q + 1) * 128], 4, [otB], dst, [self.T8b])

    def layer_norm(self, t, which, ytmp_unused=None):
        S = self.S
        xb = self.xB[t]
        sm, smB = self.small, self.smB
        xt = self.x[:, t, :]
        S.op("dve", [xb], [smB], lambda e: e.bn_stats(sm[:, 0:6], self.x[:, t, 0:512]), partial=True)
        S.op("dve", [xb], [smB], lambda e: e.bn_stats(sm[:, 6:12], self.x[:, t, 512:1024]), partial=True)
        S.op("dve", [smB], [smB], lambda e: e.bn_aggr(sm[:, 12:14], sm[:, 0:12]))
        S.op("act", [smB, self.constb], [smB], lambda e: e.activation(out=sm[:, 14:15], in_=sm[:, 13:14], func=AF.Sqrt,
                                                                      bias=self.cbias[:, 0:1], scale=1.0))
        S.op("dve", [smB], [smB], lambda e: e.reciprocal(sm[:, 15:16], sm[:, 14:15]))
        S.op("dve", [smB, xb], [xb], lambda e: e.tensor_scalar(out=xt, in0=xt, scalar1=sm[:, 12:13], scalar2=sm[:, 15:16],
                                                              op0=ALU.subtract, op1=ALU.mult))
        S.op("pool", [xb, self.lngB], [xb], lambda e: e.tensor_tensor(out=xt, in0=xt, in1=self.lng[:, 0, :], op=ALU.mult))
        S.op("pool", [xb, self.lngB], [xb], lambda e: e.tensor_tensor(out=xt, in0=xt, in1=self.lng[:, 1, :], op=ALU.add))

    def load_ln(self, l, which):
        src = bass.AP(self.lnp_d, (l * 4 + which * 2) * D, [[0, 128], [D, 2], [1, D]])
        self.S.dma("sp", self.lng[:], src, [], [self.lngB])

    def phaseB(self, kind, j, l, tiles):
        S, nc = self.S, self.nc
        ng = len(tiles)
        ntok = ng * 128
        T8 = self.T8
        with ExitStack() as es:
            self.uid += 1
            A = lambda n, sh, d_: es.enter_context(nc.sbuf_tensor(f"{n}_{self.uid}", sh, d_))
            hidT = A("hidT", [128, 32, 512], BF)
            hidB = Buf("hidT")
            rsb = [A(f"rf{i}", [128, 512], F32) for i in range(2)]
            rB = [Buf(f"rf{i}") for i in range(2)]
            self.lng = A("lng", [128, 2, D], F32)
            self.lngB = Buf("lng")
            first = True
            for b in range(2):
                wt, wb = self.wchunk("out", l, b)
                if first:
                    S.barrier(["sp"])
                    self.load_ln(l, 0)
                    first = False
                for ti, t in enumerate(tiles):
                    ps, pb = self.nxt("A")
                    for k in range(8):
                        S.op("pe", [self.T8b, wb], [pb], lambda e, k=k, ti=ti: e.matmul(
                            ps[:, 0:WCH], T8[:, k, ti * 128:(ti + 1) * 128], wt[:, k, :], start=(k == 0), stop=(k == 7)))
                    xs_ = self.x[:, t, b * WCH:(b + 1) * WCH]
                    S.op("dve", [pb, self.xB[t]], [self.xB[t]], lambda e, xs_=xs_: e.scalar_tensor_tensor(
                        out=xs_, in0=xs_, scalar=ALPHA, in1=ps[:, 0:WCH], op0=ALU.mult, op1=ALU.add), partial=True)
            for t in tiles:
                self.layer_norm(t, 0)
            if self.chk(7):
                return
            self.make_xT(tiles)
            self.load_ln(l, 1)
            ri = 0
            for c in range(8):
                wt, wb = self.wchunk("up", l, c)
                for fc in range(4):
                    ps, pb = self.nxt("A")
                    for k in range(8):
                        S.op("pe", [self.T8b, wb], [pb], lambda e, k=k, fc=fc: e.matmul(
                            ps[:, 0:ntok], wt[:, k, fc * 128:(fc + 1) * 128], T8[:, k, 0:ntok], start=(k == 0),
                            stop=(k == 7)))
                    r_, rb_ = rsb[ri % 2], rB[ri % 2]
                    ri += 1
                    S.op("act", [pb], [rb_], lambda e, r_=r_: e.activation(out=r_[:, 0:ntok], in_=ps[:, 0:ntok],
                                                                           func=AF.Relu))
                    S.op("pool", [rb_], [hidB], lambda e, r_=r_, c=c, fc=fc: e.tensor_tensor(
                        out=hidT[:, c * 4 + fc, 0:ntok], in0=r_[:, 0:ntok], in1=r_[:, 0:ntok], op=ALU.mult),
                        partial=True)
            accs = [(self.psO[0], self.pOB[0]), (self.psO[1], self.pOB[1]), (self.psS[0], self.pSB[0]),
                    (self.psS[1], self.pSB[1])]
            for b in range(2):
                for fg in range(4):
                    wt, wb = self.wchunk("down", l, b, fg)
                    for jj in range(8):
                        f = fg * 8 + jj
                        for ti, t in enumerate(tiles):
                            ps, pb = accs[ti]
                            S.op("pe", [hidB, wb], [pb], lambda e, ps=ps, jj=jj, f=f, ti=ti: e.matmul(
                                ps[:, 0:WCH], hidT[:, f, ti * 128:(ti + 1) * 128], wt[:, jj, :], start=(f == 0),
                                stop=(f == 31)))
                for ti, t in enumerate(tiles):
                    ps, pb = accs[ti]
                    xs_ = self.x[:, t, b * WCH:(b + 1) * WCH]
                    S.op("dve", [pb, self.xB[t]], [self.xB[t]], lambda e, xs_=xs_, ps=ps: e.scalar_tensor_tensor(
                        out=xs_, in0=xs_, scalar=ALPHA, in1=ps[:, 0:WCH], op0=ALU.mult, op1=ALU.add), partial=True)
            for t in tiles:
                self.layer_norm(t, 1)
            S.wait_all_dma("pool") if False else None


def rope_tables():
    half = 32
    inv = (np.float32(10000.0) ** (-np.arange(half, dtype=np.float32) / np.float32(half))).astype(np.float32)
    tab = np.zeros((NHIST, 128, 64), np.float32)
    for t in range(NHIST):
        if t < 16:
            pos = (t * 128 + np.arange(128)).astype(np.float32)
        else:
            pos = (2048 + np.arange(128)).astype(np.float32)
        ang = pos[:, None] * inv[None, :]
        tab[t, :, 0:32] = np.cos(ang)
        tab[t, :, 32:64] = np.sin(ang)
    return tab


_NC_CACHE = {}


def get_nc(npj, nsj, nl):
    key = (npj, nsj, nl)
    if key not in _NC_CACHE:
        _NC_CACHE[key] = Builder(npj, nsj, nl).build()
    return _NC_CACHE[key]


def make_in_maps(inp, ncores, npj, nsj):
    cs = rope_tables()
    lnp = np.ascontiguousarray(np.stack([inp["ln1_g"], inp["ln1_b"], inp["ln2_g"], inp["ln2_b"]], axis=1))
    relb = np.ascontiguousarray(inp["rel_bias"].reshape(32, 257))
    maps = []
    f = lambda a: np.ascontiguousarray(a, dtype=np.float32)
    for c in range(ncores):
        ps = slice(c * npj, (c + 1) * npj) if npj else slice(0, 1)
        ss = slice(c * nsj, (c + 1) * nsj) if nsj else slice(0, 1)
        maps.append({
            "xp": f(inp["x_prompt"][ps]), "xs": f(inp["x_sample"][ss]),
            "cak": f(inp["cache_a_k"][:, ss].reshape(4, -1, 512, 512)),
            "cav": f(inp["cache_a_v"][:, ss].reshape(4, -1, 512, 512)),
            "cbk": f(inp["cache_b_k"][:, ss].reshape(4, -1, 2048, 512)),
            "cbv": f(inp["cache_b_v"][:, ss].reshape(4, -1, 2048, 512)),
            "cbi": f(inp["cache_b_kidx"][:, ss]),
            "w_in": f(inp["w_in"]), "w_out": f(inp["w_out"]), "w_up": f(inp["w_up"]), "w_down": f(inp["w_down"]),
            "relb": relb, "lnp": lnp, "cs_tab": cs,
        })
    return maps


def kernel(x_prompt, x_sample, cache_a_k, cache_a_v, cache_b_k, cache_b_v, cache_b_kidx,
           w_in, rel_bias, w_out, ln1_g, ln1_b, w_up, w_down, ln2_g, ln2_b):
    inp = dict(x_prompt=np.asarray(x_prompt), x_sample=np.asarray(x_sample), cache_a_k=np.asarray(cache_a_k),
               cache_a_v=np.asarray(cache_a_v), cache_b_k=np.asarray(cache_b_k), cache_b_v=np.asarray(cache_b_v),
               cache_b_kidx=np.asarray(cache_b_kidx), w_in=np.asarray(w_in), rel_bias=np.asarray(rel_bias),
               w_out=np.asarray(w_out), ln1_g=np.asarray(ln1_g), ln1_b=np.asarray(ln1_b), w_up=np.asarray(w_up),
               w_down=np.asarray(w_down), ln2_g=np.asarray(ln2_g), ln2_b=np.asarray(ln2_b))
    ncores, npj, nsj = 8, 4, 2
    nc = get_nc(npj, nsj, 4)
    maps = make_in_maps(inp, ncores, npj, nsj)
    res = run_bass_kernel_spmd(nc, maps, core_ids=list(range(ncores)))
    R = res.results
    cat = lambda name, axis: np.concatenate([np.asarray(r[name]) for r in R], axis=axis)
    y_p = cat("yp", 0)
    y_s = cat("ys", 0)
    akp = cat("akp", 1).reshape(4, 32, 512, 8, 64)
    avp = cat("avp", 1).reshape(4, 32, 512, 8, 64)
    bkp = cat("bkp", 1).reshape(4, 32, 2048, 8, 64)
    bvp = cat("bvp", 1).reshape(4, 32, 2048, 8, 64)
    bip = cat("bip", 1)
    aks = cat("aks", 1).reshape(4, 16, 64, 8, 64)
    avs = cat("avs", 1).reshape(4, 16, 64, 8, 64)
    bks = cat("bks", 1).reshape(4, 16, 64, 8, 64)
    bvs = cat("bvs", 1).reshape(4, 16, 64, 8, 64)
    bis = cat("bis", 1)
    return (y_p, y_s, akp, avp, bkp, bvp, bip, aks, avs, bks, bvs, bis)
```

```python
import numpy as np
import concourse.bass as bass
import concourse.mybir as mybir
from concourse.bass_utils import run_bass_kernel_spmd
from contextlib import ExitStack
from functools import partial as _partial

F32 = mybir.dt.float32
BF = mybir.dt.bfloat16
ALU = mybir.AluOpType
AF = mybir.ActivationFunctionType
AX = mybir.AxisListType.X

D = 1024
NL_FULL = 4
EW = 3656
DFF = 4096
SEQ = 2048
NT = 16
NHIST = 17
ALPHA = float(8 ** 0.25)
EPS = 1e-5
NEG = -30000.0
NEGBIG = -1.0e30
NIT = 12
TOPK = 256.0
CW = 256
WCH = 512
WI_SCALE = float(512 ** -0.5)


import os as _os
STOP = int(_os.environ.get("KSTOP", "99"))


class StopBuild(Exception):
    pass


class Buf:
    __slots__ = ("name", "w", "r")

    def __init__(self, name=""):
        self.name = name
        self.w = {}
        self.r = {}


class Sch:
    ENG = ("pe", "act", "dve", "pool", "sp")

    def __init__(self, nc, es, ndma=12, nepoch=8):
        self.nc = nc
        self.eng = {"pe": nc.tensor, "act": nc.scalar, "dve": nc.vector, "pool": nc.gpsimd, "sp": nc.sync}
        self.cnt = {e: 0 for e in self.ENG}
        self.semh = {}
        self.epoch = 0
        for ep in range(nepoch):
            for e in self.ENG:
                if e == "sp":
                    continue
                self.semh[(e, ep)] = es.enter_context(nc.semaphore(f"s_{e}_{ep}"))
        self.dq = {}
        for q in ("sp", "pool"):
            sems = []
            for i in range(ndma):
                k = ("d", q, i)
                self.semh[k] = es.enter_context(nc.semaphore(f"d_{q}_{i}"))
                sems.append(k)
            self.dq[q] = {"sems": sems, "uses": [0] * ndma, "next": 0}
        self.waited = {e: {} for e in self.ENG}
        self.same_win = 3
        self.ninstr = 0

    def _wait(self, e, k, v):
        if self.waited[e].get(k, 0) >= v:
            return
        self.eng[e].wait_ge(self.semh[k], v)
        self.waited[e][k] = v
        self.ninstr += 1

    def _deps(self, e, reads, writes):
        need = {}
        for b in reads:
            for k, v in b.w.items():
                if need.get(k, 0) < v:
                    need[k] = v
        for b in writes:
            for k, v in b.w.items():
                if need.get(k, 0) < v:
                    need[k] = v
            for k, v in b.r.items():
                if need.get(k, 0) < v:
                    need[k] = v
        for k, v in need.items():
            if k[0] != "d":
                if k[1] != self.epoch:
                    continue
                if k[0] == e:
                    if e == "pe":
                        continue
                    if v <= self.cnt[e] - self.same_win:
                        continue
            self._wait(e, k, v)

    def _mark(self, tok, reads, writes, partial):
        k, v = tok
        for b in reads:
            if b.r.get(k, 0) < v:
                b.r[k] = v
        for b in writes:
            if partial:
                if b.w.get(k, 0) < v:
                    b.w[k] = v
            else:
                b.w = {k: v}
                b.r = {}

    def op(self, e, reads, writes, emit, partial=False):
        self._deps(e, reads, writes)
        ins = emit(self.eng[e])
        self.cnt[e] += 1
        self.ninstr += 1
        k = (e, self.epoch)
        ins.then_inc(self.semh[k], 1)
        self._mark((k, self.cnt[e]), reads, writes, partial)
        return ins

    def dma(self, q, out, in_, reads, writes, partial=False):
        self._deps(q, reads, writes)
        d = self.dq[q]
        i = d["next"]
        d["next"] = (i + 1) % len(d["sems"])
        k = d["sems"][i]
        if d["uses"][i] > 0:
            self._wait(q, k, 16 * d["uses"][i])
        ins = self.eng[q].dma_start(out=out, in_=in_)
        ins.then_inc(self.semh[k], 16)
        self.ninstr += 1
        d["uses"][i] += 1
        self._mark((k, 16 * d["uses"][i]), reads, writes, partial)
        return ins

    def barrier(self, engines):
        for e in engines:
            for k in ("pe", "act", "dve", "pool"):
                if k == e:
                    continue
                if self.cnt[k] > 0:
                    self._wait(e, (k, self.epoch), self.cnt[k])

    def wait_all_dma(self, e):
        for q, d in self.dq.items():
            for i, k in enumerate(d["sems"]):
                if d["uses"][i] > 0:
                    self._wait(e, k, 16 * d["uses"][i])

    def new_epoch(self):
        self.barrier(self.ENG)
        for e in self.ENG:
            self.wait_all_dma(e)
        self.epoch += 1
        for e in self.ENG:
            self.cnt[e] = 0


class Builder:
    def __init__(self, npj=4, nsj=2, nl=4):
        self.npj, self.nsj, self.nl = npj, nsj, nl
        nc = self.nc = bass.Bass("TRN2", target_bir_lowering=False)
        dt = nc.dram_tensor
        I, O, N = "ExternalInput", "ExternalOutput", "Internal"
        nsj_ = max(nsj, 1)
        npj_ = max(npj, 1)
        self.xp = dt("xp", [npj_, SEQ, D], F32, kind=I)
        self.xs = dt("xs", [nsj_, 64, D], F32, kind=I)
        self.cak = dt("cak", [NL_FULL, nsj_, 512, 512], F32, kind=I)
        self.cav = dt("cav", [NL_FULL, nsj_, 512, 512], F32, kind=I)
        self.cbk = dt("cbk", [NL_FULL, nsj_, SEQ, 512], F32, kind=I)
        self.cbv = dt("cbv", [NL_FULL, nsj_, SEQ, 512], F32, kind=I)
        self.cbi = dt("cbi", [NL_FULL, nsj_, SEQ, 64], F32, kind=I)
        self.w_in = dt("w_in", [NL_FULL, D, EW], F32, kind=I)
        self.w_out = dt("w_out", [NL_FULL, D, D], F32, kind=I)
        self.w_up = dt("w_up", [NL_FULL, D, DFF], F32, kind=I)
        self.w_down = dt("w_down", [NL_FULL, DFF, D], F32, kind=I)
        self.relb = dt("relb", [NL_FULL * 8, 257], F32, kind=I)
        self.lnp_d = dt("lnp", [NL_FULL, 4, D], F32, kind=I)
        self.cs_d = dt("cs_tab", [NHIST, 128, 64], F32, kind=I)
        self.yp = dt("yp", [npj_, SEQ, D], F32, kind=O)
        self.ys = dt("ys", [nsj_, 64, D], F32, kind=O)
        self.akp = dt("akp", [NL_FULL, npj_, 512, 512], F32, kind=O)
        self.avp = dt("avp", [NL_FULL, npj_, 512, 512], F32, kind=O)
        self.bkp = dt("bkp", [NL_FULL, npj_, SEQ, 512], F32, kind=O)
        self.bvp = dt("bvp", [NL_FULL, npj_, SEQ, 512], F32, kind=O)
        self.bip = dt("bip", [NL_FULL, npj_, SEQ, 64], F32, kind=O)
        self.aks = dt("aks", [NL_FULL, nsj_, 64, 512], F32, kind=O)
        self.avs = dt("avs", [NL_FULL, nsj_, 64, 512], F32, kind=O)
        self.bks = dt("bks", [NL_FULL, nsj_, 64, 512], F32, kind=O)
        self.bvs = dt("bvs", [NL_FULL, nsj_, 64, 512], F32, kind=O)
        self.bis = dt("bis", [NL_FULL, nsj_, 64, 64], F32, kind=O)
        self.w_in16 = dt("w_in16", [NL_FULL, 8, 128, 8 * WCH], BF, kind=N)
        self.w_out16 = dt("w_out16", [NL_FULL, 2, 128, 8 * WCH], BF, kind=N)
        self.w_up16 = dt("w_up16", [NL_FULL, 8, 128, 8 * WCH], BF, kind=N)
        self.w_down16 = dt("w_down16", [NL_FULL, 8, 128, 8 * WCH], BF, kind=N)
        self.ext_d = dt("ext_tab", [NL_FULL * 8, 768], F32, kind=N)

    def T(self, name, shape, dtype):
        return self.es.enter_context(self.nc.sbuf_tensor(name, shape, dtype))

    def build(self):
        nc = self.nc
        with ExitStack() as es:
            self.es = es
            S = self.S = Sch(nc, es)
            T = self.T
            self.x = T("x", [128, NT, D], F32)
            self.xB = [Buf(f"x{t}") for t in range(NT)]
            self.kbT = T("kbT", [128, 4, NHIST * 128], BF)
            self.vbA = T("vbA", [128, NHIST, 8, 65], BF)
            self.kiT = T("kiT", [128, NHIST * 128], BF)
            self.hB = [Buf(f"hist{t}") for t in range(NHIST)]
            self.kaT = T("kaT", [128, 4, 8 * 128], BF)
            self.vaA = T("vaA", [128, 8, 8, 65], BF)
            self.aB = [Buf(f"band{t}") for t in range(8)]
            self.BT = T("BT", [128, 8, 5, 128], BF)
            self.BTb = Buf("BT")
            self.cs = T("cs", [128, NHIST, 64], F32)
            self.csb = Buf("cs")
            self.ident = T("ident", [128, 128], BF)
            self.flipJ = T("flipJ", [128, 128], BF)
            self.negI = T("negI", [128, 128], BF)
            self.constb = Buf("const")
            self.pow2 = T("pow2", [128, NIT], F32)
            self.cbias = T("cbias", [128, 2], F32)
            self.wbuf = [T(f"wbuf{i}", [128, 8, WCH], BF) for i in range(2)]
            self.wB = [Buf(f"wbuf{i}") for i in range(2)]
            self.wuse = 0
            self.T8 = T("T8", [128, 8, 512], BF)
            self.T8b = Buf("T8")
            self.lng = None
            self.lngB = Buf("lng")
            self.stage = [T(f"stage{i}", [128, CW], F32) for i in range(2)]
            self.stB = [Buf(f"stage{i}") for i in range(2)]
            self.stuse = 0
            self.hank = [T(f"hank{i}", [128, 5, 128], BF) for i in range(1)]
            self.hkB = [Buf(f"hank{i}") for i in range(1)]
            self.xb16 = T("xb16", [128, D], BF)
            self.xb16B = Buf("xb16")
            self.small = T("small", [128, 64], F32)
            self.smB = Buf("small")
            self.smBs = [Buf(f"small{i}") for i in range(4)]
            P = lambda n, sh, d_: es.enter_context(nc.psum_tensor(n, sh, d_))
            self.psA = [P(f"psA{i}", [128, 512], F32) for i in range(2)]
            self.psT = [P(f"psT{i}", [128, 1024], BF) for i in range(2)]
            self.psS = [P(f"psS{i}", [128, 512], F32) for i in range(2)]
            self.psO = [P(f"psO{i}", [128, 512], F32) for i in range(2)]
            self.pAB = [Buf(f"psA{i}") for i in range(2)]
            self.pTB = [Buf(f"psT{i}") for i in range(2)]
            self.pSB = [Buf(f"psS{i}") for i in range(2)]
            self.pOB = [Buf(f"psO{i}") for i in range(2)]
            self.ua = self.ut = self.us = 0
            self.uid = 0
            self.njob = 0
            self.wcB = {(w, l): Buf(f"wc_{w}{l}") for w in ("in", "out", "up", "down") for l in range(NL_FULL)}
            self.extB = Buf("ext")
            self.outB = Buf("out")

            try:
                self.prologue()
                self.stopflag = False
                for j in range(self.npj):
                    if not self.stopflag:
                        self.job("p", j)
                for j in range(self.nsj):
                    if not self.stopflag:
                        self.job("s", j)
            except StopBuild:
                pass
            S.barrier(["sp"])
            S.wait_all_dma("sp")
        return nc

    def prologue(self):
        S, nc = self.S, self.nc
        cb = self.constb
        tmpf, tb = self.stage[0][:, 0:128], self.stB[0]
        t32, t32b = self.x[0:32, 1, 0:257], self.xB[1]
        e32, e32b = self.x[0:32, 0, 0:768], self.xB[0]
        S.op("pool", [], [tb], lambda e: e.memset(tmpf, 1.0))
        S.op("pool", [tb], [tb], lambda e: e.affine_select(out=tmpf, in_=tmpf, pattern=[[-1, 128]],
                                                           compare_op=ALU.is_equal, fill=0.0, base=0,
                                                           channel_multiplier=1))
        S.op("dve", [tb], [cb], lambda e: e.tensor_copy(self.ident[:], tmpf), partial=True)
        S.op("dve", [tb], [cb], lambda e: e.tensor_scalar(out=self.negI[:], in0=tmpf, scalar1=NEG, scalar2=None,
                                                          op0=ALU.mult), partial=True)
        S.op("pool", [tb], [tb], lambda e: e.memset(tmpf, 1.0))
        S.op("pool", [tb], [tb], lambda e: e.affine_select(out=tmpf, in_=tmpf, pattern=[[1, 128]],
                                                           compare_op=ALU.is_equal, fill=0.0, base=-127,
                                                           channel_multiplier=1))
        S.op("dve", [tb], [cb], lambda e: e.tensor_copy(self.flipJ[:], tmpf), partial=True)
        for i in range(NIT):
            S.op("pool", [], [cb], lambda e, i=i: e.memset(self.pow2[:, i:i + 1], float(2.0 ** -(i + 1))),
                 partial=True)
        S.op("pool", [], [cb], lambda e: e.memset(self.cbias[:, 0:1], EPS), partial=True)
        S.dma("sp", self.cs[:], self.cs_d.ap().rearrange("t p c -> p t c"), [], [self.csb])
        S.dma("sp", t32, self.relb.ap(), [], [t32b])
        S.op("dve", [t32b], [e32b], lambda e: e.tensor_copy(e32[:, 0:256], t32[:, 1:257]), partial=True)
        S.op("dve", [t32b], [e32b], lambda e: e.tensor_copy(e32[:, 256:768],
                                                            t32[:, 256:257].to_broadcast([32, 512])), partial=True)
        S.dma("sp", self.ext_d.ap(), e32, [e32b], [self.extB])
        CH = 128 * 8 * WCH
        for l in range(self.nl):
            for k in range(8):
                rows = slice(k * 128, (k + 1) * 128)
                S.dma("pool", bass.AP(self.w_in16, l * 8 * CH + k * WCH, [[8 * WCH, 128], [CH, 7], [1, WCH]]),
                      self.w_in.ap()[l, rows, 0:7 * WCH].rearrange("p (c j) -> p c j", c=7), [],
                      [self.wcB[("in", l)]], partial=True)
                S.dma("pool", bass.AP(self.w_in16, l * 8 * CH + 7 * CH + k * WCH, [[8 * WCH, 128], [1, 72]]),
                      self.w_in.ap()[l, rows, 7 * WCH:EW], [], [self.wcB[("in", l)]], partial=True)
                S.dma("pool", bass.AP(self.w_out16, l * 2 * CH + k * WCH, [[8 * WCH, 128], [CH, 2], [1, WCH]]),
                      self.w_out.ap()[l, rows, :].rearrange("p (c j) -> p c j", c=2), [],
                      [self.wcB[("out", l)]], partial=True)
                S.dma("pool", bass.AP(self.w_up16, l * 8 * CH + k * WCH, [[8 * WCH, 128], [CH, 8], [1, WCH]]),
                      self.w_up.ap()[l, rows, :].rearrange("p (c j) -> p c j", c=8), [],
                      [self.wcB[("up", l)]], partial=True)
            for rb in range(32):
                fg, k = rb // 8, rb % 8
                S.dma("pool", bass.AP(self.w_down16, l * 8 * CH + fg * CH + k * WCH, [[8 * WCH, 128], [4 * CH, 2], [1, WCH]]),
                      self.w_down.ap()[l, rb * 128:(rb + 1) * 128, :].rearrange("p (c j) -> p c j", c=2), [],
                      [self.wcB[("down", l)]], partial=True)

    def wchunk(self, name, l, a, b_=0, width=WCH):
        i = self.wuse % 2
        self.wuse += 1
        wt, wb = self.wbuf[i], self.wB[i]
        CH = 128 * 8 * WCH
        if name == "in":
            tens, ci = self.w_in16, l * 8 + a
        elif name == "out":
            tens, ci = self.w_out16, l * 2 + a
        elif name == "up":
            tens, ci = self.w_up16, l * 8 + a
        else:
            tens, ci = self.w_down16, l * 8 + a * 4 + b_
        if width == WCH:
            src = bass.AP(tens, ci * CH, [[8 * WCH, 128], [1, 8 * WCH]])
            self.S.dma("sp", wt[:].rearrange("p k j -> p (k j)"), src, [self.wcB[(name, l)]], [wb])
            return wt, wb
        src = bass.AP(tens, ci * CH, [[8 * WCH, 128], [WCH, 8], [1, width]])
        self.S.dma("sp", wt[:, :, 0:width], src, [self.wcB[(name, l)]], [wb])
        return wt, wb

    def nxt(self, which):
        if which == "A":
            i = self.ua % 2; self.ua += 1
            return self.psA[i], self.pAB[i]
        if which == "T":
            i = self.ut % 2; self.ut += 1
            return self.psT[i], self.pTB[i]
        i = self.us % 2; self.us += 1
        return self.psS[i], self.pSB[i]

    def nstage(self):
        i = self.stuse % 2
        self.stuse += 1
        return self.stage[i], self.stB[i]

    def job(self, kind, j):
        S, nc = self.S, self.nc
        prompt = kind == "p"
        if self.njob > 0:
            S.new_epoch()
        self.njob += 1
        ntile = NT if prompt else 1
        if prompt:
            for t in range(NT):
                S.dma("sp", self.x[:, t, :], self.xp.ap()[j, t * 128:(t + 1) * 128, :], [], [self.xB[t]])
        else:
            S.op("pool", [], [self.xB[0]], lambda e: e.memset(self.x[:, 0, :], 0.0))
            S.dma("sp", self.x[0:64, 0, :], self.xs.ap()[j, :, :], [], [self.xB[0]], partial=True)
        for l in range(self.nl):
            if STOP <= 1:
                self.stopflag = True
                return
            self.build_bias(l)
            if STOP <= 2:
                self.stopflag = True
                return
            if not prompt:
                self.load_caches(l, j)
            groups = [list(range(g * 4, g * 4 + 4)) for g in range(4)] if prompt else [[0]]
            for tiles in groups:
                self.phaseA(kind, j, l, tiles)
                S.barrier(["pe", "act", "dve", "pool"])
                if self.stopflag:
                    return
                self.phaseB(kind, j, l, tiles)
                S.barrier(["pe", "act", "dve", "pool"])
                if self.stopflag:
                    return
        if prompt:
            for t in range(NT):
                S.dma("sp", self.yp.ap()[j, t * 128:(t + 1) * 128, :], self.x[:, t, :], [self.xB[t]], [self.outB],
                      partial=True)
        else:
            S.dma("sp", self.ys.ap()[j, :, :], self.x[0:64, 0, :], [self.xB[0]], [self.outB], partial=True)

    def build_bias(self, l):
        S = self.S
        BB = int(_os.environ.get("KBB", "9"))
        for h in range(8):
            hk, hb = self.hank[0], self.hkB[0]
            src = bass.AP(self.ext_d, (l * 8 + h) * 768, [[1, 128], [128, 5], [1, 128]])
            S.dma("pool", hk[:], src, [self.extB], [hb])
            if BB <= 1:
                continue
            ps, pb = self.nxt("A")
            for rp in range(4):
                S.op("pe", [hb, self.constb], [pb],
                     lambda e, rp=rp: e.matmul(ps[:, rp * 128:(rp + 1) * 128], self.flipJ[:], hk[:, rp, :],
                                               start=True, stop=True))
            ps2, pb2 = self.nxt("A")
            S.op("pe", [hb, self.constb], [pb2],
                 lambda e: e.matmul(ps2[:, 0:128], self.flipJ[:], hk[:, 4, :], start=True, stop=True))
            if BB <= 2:
                continue
            for rp in range(4):
                S.op("act", [pb], [self.BTb],
                     lambda e, rp=rp: e.activation(out=self.BT[:, h, 4 - rp, :], in_=ps[:, rp * 128:(rp + 1) * 128],
                                                   func=AF.Exp), partial=True)
            S.op("act", [pb2], [self.BTb],
                 lambda e: e.activation(out=self.BT[:, h, 0, :], in_=ps2[:, 0:128], func=AF.Exp), partial=True)
        if BB <= 3:
            return
        S.op("dve", [self.BTb], [self.BTb], lambda e: e.memset(self.BT[64:128, :, 4, 0:64], 0.0), partial=True)
        S.op("dve", [self.BTb], [self.BTb], lambda e: e.memset(self.BT[0:64, :, 0, 64:128], 0.0), partial=True)

    def transposes_to(self, src_ap_fn, nblk, src_bufs, dst_fn, dst_bufs, evac_eng="act"):
        S = self.S
        i = 0
        while i < nblk:
            n = min(8, nblk - i)
            ps, pb = self.nxt("T")
            for q in range(n):
                S.op("pe", src_bufs + [self.constb], [pb],
                     lambda e, q=q, i=i: e.transpose(ps[:, q * 128:(q + 1) * 128], src_ap_fn(i + q), self.ident[:]))
            dst_fn(i, n, ps, pb)
            i += n

    def load_caches(self, l, s):
        S = self.S
        self.uid += 1
        with self.nc.sbuf_tensor(f"cst_{self.uid}", [128, 2, 512], BF) as cst, \
                self.nc.sbuf_tensor(f"cst2_{self.uid}", [128, 2, 128], BF) as cst2:
            cB = [Buf("cst0"), Buf("cst1")]
            c2B = [Buf("cst20"), Buf("cst21")]
            u = 0
            for t in range(16):
                rows = slice(t * 128, (t + 1) * 128)
                i = u % 2; u += 1
                hb = self.hB[t]
                S.dma("pool", cst[:, i, :], self.cbk.ap()[l, s, rows, :], [], [cB[i]])

                def dst(i0, n, ps, pb, t=t, hb=hb):
                    S.op("act", [pb], [hb], lambda e: e.activation(
                        out=self.kbT[:, :, t * 128:(t + 1) * 128],
                        in_=ps[:, 0:512].rearrange("p (a b) -> p a b", a=4), func=AF.Copy), partial=True)
                self.transposes_to(lambda q, i=i: cst[:, i, q * 128:(q + 1) * 128], 4, [cB[i]], dst, [hb])
                S.dma("pool", self.vbA[:, t, :, 0:64], self.cbv.ap()[l, s, rows, :].rearrange("p (h d) -> p h d", h=8),
                      [], [hb], partial=True)
                S.op("pool", [], [hb], lambda e, t=t: e.memset(self.vbA[:, t, :, 64:65], 1.0), partial=True)
                S.dma("pool", cst2[:, i, 0:64], self.cbi.ap()[l, s, rows, :], [], [c2B[i]], partial=True)
                S.dma("pool", cst2[:, i, 64:128], self.cbi.ap()[l, s, rows, :], [], [c2B[i]], partial=True)

                def dst2(i0, n, ps, pb, t=t, hb=hb):
                    S.op("act", [pb], [hb], lambda e: e.activation(
                        out=self.kiT[:, t * 128:(t + 1) * 128], in_=ps[:, 0:128], func=AF.Copy), partial=True)
                self.transposes_to(lambda q, i=i: cst2[:, i, :], 1, [c2B[i]], dst2, [hb])
            for t in range(4):
                rows = slice(t * 128, (t + 1) * 128)
                i = u % 2; u += 1
                ab = self.aB[t]
                S.dma("pool", cst[:, i, :], self.cak.ap()[l, s, rows, :], [], [cB[i]])

                def dst3(i0, n, ps, pb, t=t, ab=ab):
                    S.op("act", [pb], [ab], lambda e: e.activation(
                        out=self.kaT[:, :, t * 128:(t + 1) * 128],
                        in_=ps[:, 0:512].rearrange("p (a b) -> p a b", a=4), func=AF.Copy), partial=True)
                self.transposes_to(lambda q, i=i: cst[:, i, q * 128:(q + 1) * 128], 4, [cB[i]], dst3, [ab])
                S.dma("pool", self.vaA[:, t, :, 0:64], self.cav.ap()[l, s, rows, :].rearrange("p (h d) -> p h d", h=8),
                      [], [ab], partial=True)
                S.op("pool", [], [ab], lambda e, t=t: e.memset(self.vaA[:, t, :, 64:65], 1.0), partial=True)
            S.barrier(["pe", "act", "dve", "pool"])
            S.wait_all_dma("pool")

    def make_xT(self, tiles):
        S = self.S
        for ti, t in enumerate(tiles):
            S.op("act", [self.xB[t]], [self.xb16B], lambda e, t=t: e.activation(out=self.xb16[:], in_=self.x[:, t, :],
                                                                                 func=AF.Copy))

            def dst(i0, n, ps, pb, ti=ti):
                S.op("dve", [pb], [self.T8b], lambda e: e.tensor_copy(
                    self.T8[:, :, ti * 128:(ti + 1) * 128], ps[:, 0:1024].rearrange("p (a b) -> p a b", a=8)),
                    partial=True)
            self.transposes_to(lambda q: self.xb16[:, q * 128:(q + 1) * 128], 8, [self.xb16B], dst, [self.T8b])

    def phaseA(self, kind, j, l, tiles):
        S, nc = self.S, self.nc
        prompt = kind == "p"
        ng = len(tiles)
        ntok = ng * 128
        with ExitStack() as es:
            self.uid += 1
            A = lambda n, sh, d_: es.enter_context(nc.sbuf_tensor(f"{n}_{self.uid}", sh, d_))
            qaT = A("qaT", [128, 4, 512], BF); qbT = A("qbT", [128, 4, 512], BF); qiT = A("qiT", [128, 4, 512], BF)
            qB = Buf("q")
            wsb = A("wsb", [128, 4, 8], F32); wsbB = Buf("wsb")
            hb16 = [A(f"hb16_{i}", [128, CW], BF) for i in range(4)]
            hbB = [Buf(f"hb16_{i}") for i in range(4)]
            t1p = A("t1", [128, CW], F32); t2p = A("t2", [128, CW], F32); t12Bp = Buf("t12")
            t1d = A("t1d", [128, CW], F32); t2d = A("t2d", [128, CW], F32); t12Bd = Buf("t12d")
            score = A("score", [128, NHIST * 128], F32); scB = Buf("score")
            rsb = [A(f"rsb{i}", [128, 512], F32) for i in range(2)]
            rB = [Buf(f"rsb{i}") for i in range(2)]
            maskb = A("maskb", [128, NHIST * 128], BF); mkB = Buf("maskb")
            mbT = A("mbT", [128, NHIST, 128], BF); mtB = Buf("mbT")
            NE = 3
            expT = [A(f"expT{i}", [128, 640], BF) for i in range(NE)]
            eB = [Buf(f"expT{i}") for i in range(NE)]
            ue = [0]
            otok = A("otok", [128, 512], BF); otB = Buf("otok")
            st = A("st", [128, 48], F32); stB_ = Buf("st")
            steps = A("steps", [128, NIT], F32)
            stg = [(self.stage[i], self.stB[i]) for i in range(2)]
            stg += [(score[:, i * CW:(i + 1) * CW], Buf(f"xstage{i}")) for i in range(8)]
            su = [0]

            def nstage():
                i = su[0] % len(stg)
                su[0] += 1
                return stg[i]

            self.make_xT(tiles)
            T8 = self.T8
            if STOP <= 3:
                self.stopflag = True
                return

            hu = 0
            deferq = []
            for cc in range(8):
              width = WCH if cc < 7 else 72
              wt, wb = self.wchunk("in", l, cc, width=width)
              seg = cc
              for ti, t in enumerate(tiles):
                psF, pb = self.nxt("A")
                for k in range(8):
                    S.op("pe", [self.T8b, wb], [pb],
                         lambda e, k=k, ti=ti: e.matmul(psF[:, 0:width], T8[:, k, ti * 128:(ti + 1) * 128],
                                                        wt[:, k, 0:width], start=(k == 0), stop=(k == 7)))
                for fn_ in deferq:
                    fn_()
                deferq.clear()
                for hh in range(2 if cc < 7 else 1):
                    ps = psF[:, hh * CW:(hh + 1) * CW] if cc < 7 else psF[:, 0:CW]
                    RE = "pool" if (2 * ti + hh) % 3 == 2 else "dve"
                    t1, t2, t12B = (t1d, t2d, t12Bd) if RE == "dve" else (t1p, t2p, t12Bp)
                    hidx = t if prompt else 16
                    aslot = (t % 8) if prompt else 4
                    cs_c = self.cs[:, hidx, 0:32]
                    cs_s = self.cs[:, hidx, 32:64]
                    nrow = 128 if prompt else 64

                    def rope(src, srcB, nh):
                        v = lambda a: a.rearrange("p (h two d) -> p h two d", h=nh, two=2)
                        bc = lambda a: a.unsqueeze(1).to_broadcast([128, nh, 32])
                        S.op(RE, [srcB, self.csb], [t12B], lambda e: e.tensor_tensor(
                            out=v(t1[:, 0:nh * 64]), in0=v(src),
                            in1=cs_c.unsqueeze(1).unsqueeze(1).to_broadcast([128, nh, 2, 32]), op=ALU.mult))
                        S.op(RE, [srcB, self.csb], [t12B], lambda e: e.tensor_tensor(
                            out=v(t2[:, 0:nh * 64])[:, :, 0, :], in0=v(src)[:, :, 1, :], in1=bc(cs_s), op=ALU.mult),
                            partial=True)
                        S.op(RE, [srcB, self.csb], [t12B], lambda e: e.tensor_tensor(
                            out=v(t2[:, 0:nh * 64])[:, :, 1, :], in0=v(src)[:, :, 0, :], in1=bc(cs_s), op=ALU.mult),
                            partial=True)
                        S.op(RE, [t12B], [t12B], lambda e: e.tensor_tensor(
                            out=v(t1[:, 0:nh * 64])[:, :, 0, :], in0=v(t1[:, 0:nh * 64])[:, :, 0, :],
                            in1=v(t2[:, 0:nh * 64])[:, :, 0, :], op=ALU.subtract), partial=True)
                        S.op(RE, [t12B], [t12B], lambda e: e.tensor_tensor(
                            out=v(t1[:, 0:nh * 64])[:, :, 1, :], in0=v(t1[:, 0:nh * 64])[:, :, 1, :],
                            in1=v(t2[:, 0:nh * 64])[:, :, 1, :], op=ALU.add), partial=True)

                    def out_dma(dram, col0, ncol, src_t, src_b):
                        S.dma("sp", dram[0:nrow, col0:col0 + ncol], src_t[0:nrow, 0:ncol], [src_b], [self.outB],
                              partial=True)

                    def tr_to(dstT, hbt, hbb, ti=ti, hh=hh):
                        def dst(i0, n, ps_, pb_):
                            S.op("act", [pb_], [qB], lambda e: e.activation(
                                out=dstT[:, 2 * hh:2 * hh + 2, ti * 128:(ti + 1) * 128],
                                in_=ps_[:, 0:256].rearrange("p (a b) -> p a b", a=2), func=AF.Copy), partial=True)
                        self.transposes_to(lambda q: hbt[:, q * 128:(q + 1) * 128], 2, [hbb], dst, [qB])

                    hi_ = hu % 4
                    if seg == 0:
                        hu += 1
                        S.op("act", [pb], [hbB[hi_]], lambda e: e.activation(out=hb16[hi_][:], in_=ps[:, 0:CW],
                                                                             func=AF.Copy, scale=0.125))
                        deferq.append(_partial(tr_to, qaT, hb16[hi_], hbB[hi_]))
                    elif seg == 1:
                        hu += 1
                        ab = self.aB[aslot]
                        S.op("act", [pb], [hbB[hi_]], lambda e: e.activation(out=hb16[hi_][:], in_=ps[:, 0:CW],
                                                                             func=AF.Copy))
                        if (prompt and t >= 12) or not prompt:
                            sg, sgb = nstage()
                            S.op("act", [pb], [sgb], lambda e: e.activation(out=sg[:, 0:CW], in_=ps[:, 0:CW], func=AF.Copy))
                            dram = (self.akp.ap()[l, j, (t - 12) * 128:(t - 11) * 128, :] if prompt
                                    else self.aks.ap()[l, j, :, :])
                            out_dma(dram, hh * CW, CW, sg, sgb)

                        def dstk(i0, n, ps_, pb_, aslot=aslot, ab=ab, hh=hh):
                            S.op("act", [pb_], [ab], lambda e: e.activation(
                                out=self.kaT[:, 2 * hh:2 * hh + 2, aslot * 128:(aslot + 1) * 128],
                                in_=ps_[:, 0:256].rearrange("p (a b) -> p a b", a=2), func=AF.Copy), partial=True)
                        deferq.append(_partial(self.transposes_to, lambda q, hi_=hi_: hb16[hi_][:, q * 128:(q + 1) * 128],
                                               2, [hbB[hi_]], dstk, [ab]))
                    elif seg == 2:
                        ab = self.aB[aslot]
                        S.op("act", [pb], [ab], lambda e: e.activation(
                            out=self.vaA[:, aslot, 4 * hh:4 * hh + 4, 0:64],
                            in_=ps[:, 0:CW].rearrange("p (h d) -> p h d", h=4), func=AF.Copy), partial=True)
                        if hh == 0:
                            S.op("pool", [], [ab], lambda e: e.memset(self.vaA[:, aslot, :, 64:65], 1.0), partial=True)
                        if (prompt and t >= 12) or not prompt:
                            sg, sgb = nstage()
                            S.op("act", [pb], [sgb], lambda e: e.activation(out=sg[:, 0:CW], in_=ps[:, 0:CW], func=AF.Copy))
                            dram = (self.avp.ap()[l, j, (t - 12) * 128:(t - 11) * 128, :] if prompt
                                    else self.avs.ap()[l, j, :, :])
                            out_dma(dram, hh * CW, CW, sg, sgb)
                    elif seg in (3, 4, 6):
                        hu += 1
                        sg, sgb = nstage()
                        S.op("act", [pb], [sgb], lambda e: e.activation(out=sg[:, 0:CW], in_=ps[:, 0:CW], func=AF.Copy))
                        rope(sg[:, 0:CW], sgb, 4)
                        if seg == 4:
                            S.op(RE, [t12B], [sgb], lambda e: e.tensor_copy(sg[:, 0:CW], t1[:, 0:CW]))
                            dram = (self.bkp.ap()[l, j, t * 128:(t + 1) * 128, :] if prompt else self.bks.ap()[l, j, :, :])
                            out_dma(dram, hh * CW, CW, sg, sgb)
                            S.op(RE, [sgb], [hbB[hi_]], lambda e: e.tensor_copy(hb16[hi_][:], sg[:, 0:CW]))
                            hb_ = self.hB[hidx]

                            def dstk(i0, n, ps_, pb_, hidx=hidx, hb_=hb_, hh=hh):
                                S.op("act", [pb_], [hb_], lambda e: e.activation(
                                    out=self.kbT[:, 2 * hh:2 * hh + 2, hidx * 128:(hidx + 1) * 128],
                                    in_=ps_[:, 0:256].rearrange("p (a b) -> p a b", a=2), func=AF.Copy), partial=True)
                            deferq.append(_partial(self.transposes_to,
                                                   lambda q, hi_=hi_: hb16[hi_][:, q * 128:(q + 1) * 128], 2,
                                                   [hbB[hi_]], dstk, [hb_]))
                        else:
                            sc = 0.125 if seg == 3 else 1.0
                            S.op(RE, [t12B], [hbB[hi_]], lambda e: e.tensor_scalar(
                                out=hb16[hi_][:], in0=t1[:, 0:CW], scalar1=sc, scalar2=None, op0=ALU.mult))
                            deferq.append(_partial(tr_to, qbT if seg == 3 else qiT, hb16[hi_], hbB[hi_]))
                    elif seg == 5:
                        hb_ = self.hB[hidx]
                        S.op("act", [pb], [hb_], lambda e: e.activation(
                            out=self.vbA[:, hidx, 4 * hh:4 * hh + 4, 0:64],
                            in_=ps[:, 0:CW].rearrange("p (h d) -> p h d", h=4), func=AF.Copy), partial=True)
                        if hh == 0:
                            S.op("pool", [], [hb_], lambda e: e.memset(self.vbA[:, hidx, :, 64:65], 1.0), partial=True)
                        sg, sgb = nstage()
                        S.op("act", [pb], [sgb], lambda e: e.activation(out=sg[:, 0:CW], in_=ps[:, 0:CW], func=AF.Copy))
                        dram = (self.bvp.ap()[l, j, t * 128:(t + 1) * 128, :] if prompt else self.bvs.ap()[l, j, :, :])
                        out_dma(dram, hh * CW, CW, sg, sgb)
                    else:
                        hu += 1
                        sg, sgb = nstage()
                        S.op("act", [pb], [sgb], lambda e: e.activation(out=sg[:, 0:72], in_=ps[:, 0:72], func=AF.Copy))
                        rope(sg[:, 0:64], sgb, 1)
                        S.op(RE, [sgb], [wsbB], lambda e: e.tensor_scalar(
                            out=wsb[:, ti, :], in0=sg[:, 64:72], scalar1=WI_SCALE, scalar2=None, op0=ALU.mult),
                            partial=True)
                        S.op(RE, [t12B], [sgb], lambda e: e.tensor_copy(sg[:, 0:64], t1[:, 0:64]))
                        dram = (self.bip.ap()[l, j, t * 128:(t + 1) * 128, :] if prompt else self.bis.ap()[l, j, :, :])
                        out_dma(dram, 0, 64, sg, sgb)
                        S.op(RE, [sgb], [hbB[hi_]], lambda e: e.tensor_copy(hb16[hi_][:, 0:64], sg[:, 0:64]))
                        S.op(RE, [sgb], [hbB[hi_]], lambda e: e.tensor_copy(hb16[hi_][:, 64:128], sg[:, 0:64]),
                             partial=True)
                        hb_ = self.hB[hidx]

                        def dstk(i0, n, ps_, pb_, hidx=hidx, hb_=hb_):
                            S.op("act", [pb_], [hb_], lambda e: e.activation(
                                out=self.kiT[:, hidx * 128:(hidx + 1) * 128], in_=ps_[:, 0:128], func=AF.Copy),
                                partial=True)
                        deferq.append(_partial(self.transposes_to, lambda q, hi_=hi_: hb16[hi_][:, 0:128], 1,
                                               [hbB[hi_]], dstk, [hb_]))

            for fn_ in deferq:
                fn_()
            deferq.clear()
            if STOP <= 4:
                self.stopflag = True
                return
            S.op("dve", [], [scB] + [b_ for _, b_ in stg[2:]], lambda e: e.memset(score[:, 0:1], 0.0))
            def tinfo(t):
                nkt = (t + 1) if prompt else NHIST
                return nkt, nkt * 128

            def band(ti, t):
                qc = slice(ti * 128, (ti + 1) * 128)
                if prompt:
                    win = [(tt, tt - (t - 4), tt % 8) for tt in range(max(0, t - 4), t + 1)]
                else:
                    win = [(r, r, r) for r in range(5)]
                prev_pv = None
                for h in range(8):
                    pr, hf = h // 2, (h % 2) * 64
                    psS, pSb = self.nxt("S")
                    psX, pXb = self.psA[h % 2], self.pAB[h % 2]
                    for wi_, (tt, r, slot) in enumerate(win):
                        o_ps, o_b = (psS, pSb) if wi_ < 4 else (psX, pXb)
                        oc = slice((wi_ % 4) * 128, (wi_ % 4 + 1) * 128)
                        S.op("pe", [self.aB[slot], qB], [o_b], lambda e, o_ps=o_ps, oc=oc, slot=slot: e.matmul(
                            o_ps[:, oc], self.kaT[hf:hf + 64, pr, slot * 128:(slot + 1) * 128],
                            qaT[hf:hf + 64, pr, qc], start=True, stop=True))
                    ei = ue[0] % NE
                    ue[0] += 1
                    n1 = min(4, len(win))
                    S.op("act", [pSb], [eB[ei]], lambda e, n1=n1, ei=ei: e.activation(
                        out=expT[ei][:, 0:n1 * 128], in_=psS[:, 0:n1 * 128], func=AF.Exp))
                    if len(win) == 5:
                        S.op("act", [pXb], [eB[ei]], lambda e, ei=ei: e.activation(
                            out=expT[ei][:, 512:640], in_=psX[:, 0:128], func=AF.Exp), partial=True)
                    r0 = win[0][1]
                    nw = len(win)
                    S.op("pool", [eB[ei], self.BTb], [eB[ei]], lambda e, ei=ei, r0=r0, nw=nw, h=h: e.tensor_tensor(
                        out=expT[ei][:, 0:nw * 128], in0=expT[ei][:, 0:nw * 128],
                        in1=self.BT[:, h, r0:r0 + nw, :].rearrange("p a b -> p (a b)"), op=ALU.mult))
                    def pv(h=h, ei=ei):
                        po, pob = self.psO[h // 4], self.pOB[h // 4]
                        for wi_, (tt, r, slot) in enumerate(win):
                            S.op("pe", [eB[ei], self.aB[slot]], [pob], lambda e, wi_=wi_, slot=slot: e.matmul(
                                po[:, (h % 4) * 65:(h % 4) * 65 + 65], expT[ei][:, wi_ * 128:(wi_ + 1) * 128],
                                self.vaA[:, slot, h, :], start=(wi_ == 0), stop=(wi_ == len(win) - 1)))
                    if prev_pv is not None:
                        prev_pv()
                    prev_pv = pv
                    yield
                prev_pv()
                self.finish_attn(st, stB_, otok, otB, 0, ti)

            def idx(ti, t):
                qc = slice(ti * 128, (ti + 1) * 128)
                nkt, N = tinfo(t)
                for h in range(8):
                    pr, hf = h // 2, (h % 2) * 64
                    for k0 in range(0, N, 512):
                        kn = min(512, N - k0)
                        psS, pSb = self.nxt("S")
                        S.op("pe", [qB] + [self.hB[x_] for x_ in range(k0 // 128, (k0 + kn) // 128)], [pSb],
                             lambda e, k0=k0, kn=kn, psS=psS: e.matmul(psS[:, 0:kn], qiT[hf:hf + 64, pr, qc],
                                                                       self.kiT[hf:hf + 64, k0:k0 + kn],
                                                                       start=True, stop=True))
                        ri = self.us % 2
                        S.op("act", [pSb], [rB[ri]], lambda e, kn=kn, ri=ri, psS=psS: e.activation(
                            out=rsb[ri][:, 0:kn], in_=psS[:, 0:kn], func=AF.Relu))
                        if h == 0:
                            S.op("dve", [rB[ri], wsbB], [scB], lambda e, k0=k0, kn=kn, ri=ri: e.tensor_scalar(
                                out=score[:, k0:k0 + kn], in0=rsb[ri][:, 0:kn], scalar1=wsb[:, ti, 0:1], scalar2=None,
                                op0=ALU.mult), partial=True)
                        else:
                            S.op("dve", [rB[ri], wsbB, scB], [scB],
                                 lambda e, k0=k0, kn=kn, ri=ri, h=h: e.scalar_tensor_tensor(
                                     out=score[:, k0:k0 + kn], in0=rsb[ri][:, 0:kn], scalar=wsb[:, ti, h:h + 1],
                                     in1=score[:, k0:k0 + kn], op0=ALU.mult, op1=ALU.add), partial=True)
                    yield
                if prompt:
                    S.op("dve", [scB], [scB], lambda e: e.memset(score[0:64, N - 64:N], NEGBIG), partial=True)
                else:
                    S.op("dve", [scB], [scB], lambda e: e.memset(score[:, N - 64:N], NEGBIG), partial=True)

            def bis(ti, t):
                nkt, N = tinfo(t)
                need_topk = (not prompt) or t >= 2
                if need_topk:
                    S.op("dve", [scB], [stB_], lambda e: e.tensor_reduce(out=st[:, 0:1], in_=score[:, 0:N], axis=AX,
                                                                         op=ALU.max), partial=True)
                    if prompt:
                        S.op("dve", [scB], [stB_], lambda e: e.tensor_reduce(out=st[0:64, 1:2], in_=score[0:64, 0:N - 64],
                                                                             axis=AX, op=ALU.min), partial=True)
                        S.op("dve", [scB], [stB_], lambda e: e.tensor_reduce(out=st[64:128, 1:2], in_=score[64:128, 0:N],
                                                                             axis=AX, op=ALU.min), partial=True)
                    else:
                        S.op("dve", [scB], [stB_], lambda e: e.tensor_reduce(out=st[:, 1:2], in_=score[:, 0:N - 64],
                                                                             axis=AX, op=ALU.min), partial=True)
                    S.op("dve", [stB_], [stB_], lambda e: e.tensor_tensor(out=st[:, 2:3], in0=st[:, 0:1], in1=st[:, 1:2],
                                                                          op=ALU.subtract))
                    S.op("dve", [stB_, self.constb], [stB_], lambda e: e.tensor_scalar(
                        out=steps[:], in0=self.pow2[:], scalar1=st[:, 2:3], scalar2=None, op0=ALU.mult))
                    S.op("dve", [stB_], [stB_], lambda e: e.tensor_tensor(out=st[:, 4:5], in0=st[:, 1:2],
                                                                          in1=steps[:, 0:1], op=ALU.add))
                    for it in range(NIT):
                        S.op("dve", [stB_, scB], [stB_, mkB], lambda e: e.tensor_scalar(
                            out=maskb[:, 0:N], in0=score[:, 0:N], scalar1=st[:, 4:5], scalar2=None, op0=ALU.is_ge,
                            op1=ALU.add, accum_out=st[:, 5:6]))
                        S.op("dve", [stB_], [stB_], lambda e: e.tensor_scalar(
                            out=st[:, 6:7], in0=st[:, 5:6], scalar1=TOPK, scalar2=0.5, op0=ALU.is_ge,
                            op1=ALU.subtract))
                        S.op("dve", [stB_], [stB_], lambda e, it=it: e.scalar_tensor_tensor(
                            out=st[:, 4:5], in0=st[:, 6:7], scalar=steps[:, it:it + 1], in1=st[:, 4:5],
                            op0=ALU.mult, op1=ALU.add))
                    S.op("dve", [stB_], [stB_], lambda e: e.scalar_tensor_tensor(
                        out=st[:, 3:4], in0=steps[:, NIT - 1:NIT], scalar=-0.5, in1=st[:, 4:5], op0=ALU.mult,
                        op1=ALU.add))
                else:
                    S.op("dve", [stB_], [stB_], lambda e: e.memset(st[:, 3:4], -1.0e29))
                S.op("dve", [stB_, scB], [mkB], lambda e: e.tensor_scalar(
                    out=maskb[:, 0:N], in0=score[:, 0:N], scalar1=st[:, 3:4], scalar2=None, op0=ALU.is_ge))

            def maskT(ti, t):
                nkt, N = tinfo(t)

                def dstm(i0, n, ps_, pb_):
                    S.op("act", [pb_], [mtB], lambda e: e.activation(
                        out=mbT[:, i0:i0 + n, :], in_=ps_[:, 0:n * 128].rearrange("p (a b) -> p a b", a=n),
                        func=AF.Copy), partial=True)
                self.transposes_to(lambda q: maskb[:, q * 128:(q + 1) * 128], nkt, [mkB], dstm, [mtB])

            def sparse_main(ti, t):
                qc = slice(ti * 128, (ti + 1) * 128)
                nkt, N = tinfo(t)
                prev_pv = None
                for h in range(8):
                    pr, hf = h // 2, (h % 2) * 64
                    po, pob = self.psO[h // 4], self.pOB[h // 4]
                    for k0 in range(0, nkt, 4):
                        kn = min(4, nkt - k0)
                        psS, pSb = self.nxt("S")
                        for q in range(kn):
                            kt = k0 + q
                            S.op("pe", [self.hB[kt], qB], [pSb], lambda e, q=q, kt=kt, psS=psS: e.matmul(
                                psS[:, q * 128:(q + 1) * 128], self.kbT[hf:hf + 64, pr, kt * 128:(kt + 1) * 128],
                                qbT[hf:hf + 64, pr, qc], start=True, stop=True))
                        ei = ue[0] % NE
                        ue[0] += 1
                        S.op("act", [pSb], [eB[ei]], lambda e, kn=kn, ei=ei, psS=psS: e.activation(
                            out=expT[ei][:, 0:kn * 128], in_=psS[:, 0:kn * 128], func=AF.Exp))
                        S.op("pool", [eB[ei], mtB], [eB[ei]], lambda e, kn=kn, ei=ei, k0=k0: e.tensor_tensor(
                            out=expT[ei][:, 0:kn * 128], in0=expT[ei][:, 0:kn * 128],
                            in1=mbT[:, k0:k0 + kn, :].rearrange("p a b -> p (a b)"), op=ALU.mult))
                        def pv(h=h, ei=ei, k0=k0, kn=kn, po=po, pob=pob):
                            for q in range(kn):
                                kt = k0 + q
                                S.op("pe", [eB[ei], self.hB[kt]], [pob], lambda e, q=q, kt=kt: e.matmul(
                                    po[:, (h % 4) * 65:(h % 4) * 65 + 65], expT[ei][:, q * 128:(q + 1) * 128],
                                    self.vbA[:, kt, h, :], start=(kt == 0), stop=(kt == nkt - 1)))
                        if prev_pv is not None:
                            prev_pv()
                        prev_pv = pv
                prev_pv()

            def run(*gens):
                gens = list(gens)
                while gens:
                    for g in list(gens):
                        try:
                            next(g)
                        except StopIteration:
                            gens.remove(g)

            run(idx(0, tiles[0]))
            bis(0, tiles[0])
            for ti, t in enumerate(tiles):
                nxt_ = ti + 1 < len(tiles)
                if nxt_:
                    run(band(ti, t), idx(ti + 1, tiles[ti + 1]))
                else:
                    run(band(ti, t))
                if self.chk(5):
                    return
                maskT(ti, t)
                if nxt_:
                    bis(ti + 1, tiles[ti + 1])
                sparse_main(ti, t)
                self.finish_attn(st, stB_, otok, otB, 4, ti)
                if self.chk(6):
                    return

    def chk(self, lvl):
        if STOP <= lvl:
            self.stopflag = True
            return True
        return False

    def finish_attn(self, st, stB_, otok, otB, e0, ti):
        S = self.S
        for half in range(2):
            po, pob = self.psO[half], self.pOB[half]
            pv = po[:, 0:260].rearrange("p (h d) -> p h d", h=4)
            S.op("dve", [pob], [stB_], lambda e, pv=pv, half=half: e.reciprocal(
                st[:, 8 + half * 4:12 + half * 4].unsqueeze(2), pv[:, :, 64:65]), partial=True)
            S.op("dve", [pob, stB_], [otB], lambda e, pv=pv, half=half: e.tensor_tensor(
                out=otok[:, half * 256:(half + 1) * 256].rearrange("p (h d) -> p h d", h=4), in0=pv[:, :, 0:64],
                in1=st[:, 8 + half * 4:12 + half * 4].unsqueeze(2).to_broadcast([128, 4, 64]), op=ALU.mult),
                partial=True)

        def dst(i0, n, ps_, pb_):
            S.op("act", [pb_], [self.T8b], lambda e: e.activation(
                out=self.T8[:, e0:e0 + 4, ti * 128:(ti + 1) * 128],
                in_=ps_[:, 0:512].rearrange("p (a b) -> p a b", a=4), func=AF.Copy), partial=True)
        self.transposes_to(lambda q: otok[:, q * 128:(q + 1) * 128], 4, [otB], dst, [self.T8b])

    def layer_norm_group(self, tiles):
        S = self.S
        sl = lambda ti: self.small[:, ti * 16:(ti + 1) * 16]
        for ti, t in enumerate(tiles):
            sm, smB, xb = sl(ti), self.smBs[ti], self.xB[t]
            S.op("dve", [xb], [smB], lambda e, sm=sm, t=t: e.bn_stats(sm[:, 0:6], self.x[:, t, 0:512]), partial=True)
            S.op("dve", [xb], [smB], lambda e, sm=sm, t=t: e.bn_stats(sm[:, 6:12], self.x[:, t, 512:1024]), partial=True)
            S.op("dve", [smB], [smB], lambda e, sm=sm: e.bn_aggr(sm[:, 12:14], sm[:, 0:12]))
        for ti, t in enumerate(tiles):
            sm, smB = sl(ti), self.smBs[ti]
            S.op("act", [smB, self.constb], [smB], lambda e, sm=sm: e.activation(
                out=sm[:, 14:15], in_=sm[:, 13:14], func=AF.Sqrt, bias=self.cbias[:, 0:1], scale=1.0))
        for ti, t in enumerate(tiles):
            sm, smB = sl(ti), self.smBs[ti]
            S.op("dve", [smB], [smB], lambda e, sm=sm: e.reciprocal(sm[:, 15:16], sm[:, 14:15]))
        for ti, t in enumerate(tiles):
            sm, smB, xb = sl(ti), self.smBs[ti], self.xB[t]
            xt = self.x[:, t, :]
            S.op("dve", [smB, xb, self.lngB], [xb], lambda e, sm=sm, xt=xt: e.scalar_tensor_tensor(
                out=xt, in0=xt, scalar=sm[:, 12:13], in1=self.lng[:, 0, :], op0=ALU.subtract, op1=ALU.mult))
            S.op("dve", [smB, xb, self.lngB], [xb], lambda e, sm=sm, xt=xt: e.scalar_tensor_tensor(
                out=xt, in0=xt, scalar=sm[:, 15:16], in1=self.lng[:, 1, :], op0=ALU.mult, op1=ALU.add))

    def load_ln(self, l, which):
        src = bass.AP(self.lnp_d, (l * 4 + which * 2) * D, [[0, 128], [D, 2], [1, D]])
        self.S.dma("sp", self.lng[:], src, [], [self.lngB])

    def phaseB(self, kind, j, l, tiles):
        S, nc = self.S, self.nc
        ng = len(tiles)
        ntok = ng * 128
        T8 = self.T8
        with ExitStack() as es:
            self.uid += 1
            A = lambda n, sh, d_: es.enter_context(nc.sbuf_tensor(f"{n}_{self.uid}", sh, d_))
            hidT = A("hidT", [128, 32, 512], BF)
            hidB = Buf("hidT")
            rsb = [A(f"rf{i}", [128, 512], F32) for i in range(2)]
            rB = [Buf(f"rf{i}") for i in range(2)]
            self.lng = A("lng", [128, 2, D], F32)
            self.lngB = Buf("lng")
            first = True
            for b in range(2):
                wt, wb = self.wchunk("out", l, b)
                if first:
                    S.barrier(["sp"])
                    self.load_ln(l, 0)
                    first = False
                for ti, t in enumerate(tiles):
                    ps, pb = self.nxt("A")
                    for k in range(8):
                        S.op("pe", [self.T8b, wb], [pb], lambda e, k=k, ti=ti: e.matmul(
                            ps[:, 0:WCH], T8[:, k, ti * 128:(ti + 1) * 128], wt[:, k, :], start=(k == 0), stop=(k == 7)))
                    xs_ = self.x[:, t, b * WCH:(b + 1) * WCH]
                    S.op("dve", [pb, self.xB[t]], [self.xB[t]], lambda e, xs_=xs_: e.scalar_tensor_tensor(
                        out=xs_, in0=xs_, scalar=ALPHA, in1=ps[:, 0:WCH], op0=ALU.mult, op1=ALU.add), partial=True)
            self.layer_norm_group(tiles)
            if self.chk(7):
                return
            self.make_xT(tiles)
            self.load_ln(l, 1)
            ri = 0
            for c in range(8):
                wt, wb = self.wchunk("up", l, c)
                for fc in range(4):
                    ps, pb = self.nxt("A")
                    for k in range(8):
                        S.op("pe", [self.T8b, wb], [pb], lambda e, k=k, fc=fc: e.matmul(
                            ps[:, 0:ntok], wt[:, k, fc * 128:(fc + 1) * 128], T8[:, k, 0:ntok], start=(k == 0),
                            stop=(k == 7)))
                    r_, rb_ = rsb[ri % 2], rB[ri % 2]
                    ri += 1
                    S.op("act", [pb], [rb_], lambda e, r_=r_: e.activation(out=r_[:, 0:ntok], in_=ps[:, 0:ntok],
                                                                           func=AF.Relu))
                    S.op("pool", [rb_], [hidB], lambda e, r_=r_, c=c, fc=fc: e.tensor_tensor(
                        out=hidT[:, c * 4 + fc, 0:ntok], in0=r_[:, 0:ntok], in1=r_[:, 0:ntok], op=ALU.mult),
                        partial=True)
            accs = [(self.psO[0], self.pOB[0]), (self.psO[1], self.pOB[1]), (self.psS[0], self.pSB[0]),
                    (self.psS[1], self.pSB[1])]
            for b in range(2):
                for fg in range(4):
                    wt, wb = self.wchunk("down", l, b, fg)
                    for jj in range(8):
                        f = fg * 8 + jj
                        for ti, t in enumerate(tiles):
                            ps, pb = accs[ti]
                            S.op("pe", [hidB, wb], [pb], lambda e, ps=ps, jj=jj, f=f, ti=ti: e.matmul(
                                ps[:, 0:WCH], hidT[:, f, ti * 128:(ti + 1) * 128], wt[:, jj, :], start=(f == 0),
                                stop=(f == 31)))
                for ti, t in enumerate(tiles):
                    ps, pb = accs[ti]
                    xs_ = self.x[:, t, b * WCH:(b + 1) * WCH]
                    S.op("dve", [pb, self.xB[t]], [self.xB[t]], lambda e, xs_=xs_, ps=ps: e.scalar_tensor_tensor(
                        out=xs_, in0=xs_, scalar=ALPHA, in1=ps[:, 0:WCH], op0=ALU.mult, op1=ALU.add), partial=True)
            self.layer_norm_group(tiles)
            S.wait_all_dma("pool") if False else None


def rope_tables():
    half = 32
    inv = (np.float32(10000.0) ** (-np.arange(half, dtype=np.float32) / np.float32(half))).astype(np.float32)
    tab = np.zeros((NHIST, 128, 64), np.float32)
    for t in range(NHIST):
        if t < 16:
            pos = (t * 128 + np.arange(128)).astype(np.float32)
        else:
            pos = (2048 + np.arange(128)).astype(np.float32)
        ang = pos[:, None] * inv[None, :]
        tab[t, :, 0:32] = np.cos(ang)
        tab[t, :, 32:64] = np.sin(ang)
    return tab


_NC_CACHE = {}


def get_nc(npj, nsj, nl):
    key = (npj, nsj, nl)
    if key not in _NC_CACHE:
        _NC_CACHE[key] = Builder(npj, nsj, nl).build()
    return _NC_CACHE[key]


def make_in_maps(inp, ncores, npj, nsj):
    cs = rope_tables()
    lnp = np.ascontiguousarray(np.stack([inp["ln1_g"], inp["ln1_b"], inp["ln2_g"], inp["ln2_b"]], axis=1))
    relb = np.ascontiguousarray(inp["rel_bias"].reshape(32, 257))
    maps = []
    f = lambda a: np.ascontiguousarray(a, dtype=np.float32)
    for c in range(ncores):
        ps = slice(c * npj, (c + 1) * npj) if npj else slice(0, 1)
        ss = slice(c * nsj, (c + 1) * nsj) if nsj else slice(0, 1)
        maps.append({
            "xp": f(inp["x_prompt"][ps]), "xs": f(inp["x_sample"][ss]),
            "cak": f(inp["cache_a_k"][:, ss].reshape(4, -1, 512, 512)),
            "cav": f(inp["cache_a_v"][:, ss].reshape(4, -1, 512, 512)),
            "cbk": f(inp["cache_b_k"][:, ss].reshape(4, -1, 2048, 512)),
            "cbv": f(inp["cache_b_v"][:, ss].reshape(4, -1, 2048, 512)),
            "cbi": f(inp["cache_b_kidx"][:, ss]),
            "w_in": f(inp["w_in"]), "w_out": f(inp["w_out"]), "w_up": f(inp["w_up"]), "w_down": f(inp["w_down"]),
            "relb": relb, "lnp": lnp, "cs_tab": cs,
        })
    return maps


def kernel(x_prompt, x_sample, cache_a_k, cache_a_v, cache_b_k, cache_b_v, cache_b_kidx,
           w_in, rel_bias, w_out, ln1_g, ln1_b, w_up, w_down, ln2_g, ln2_b):
    inp = dict(x_prompt=np.asarray(x_prompt), x_sample=np.asarray(x_sample), cache_a_k=np.asarray(cache_a_k),
               cache_a_v=np.asarray(cache_a_v), cache_b_k=np.asarray(cache_b_k), cache_b_v=np.asarray(cache_b_v),
               cache_b_kidx=np.asarray(cache_b_kidx), w_in=np.asarray(w_in), rel_bias=np.asarray(rel_bias),
               w_out=np.asarray(w_out), ln1_g=np.asarray(ln1_g), ln1_b=np.asarray(ln1_b), w_up=np.asarray(w_up),
               w_down=np.asarray(w_down), ln2_g=np.asarray(ln2_g), ln2_b=np.asarray(ln2_b))
    ncores, npj, nsj = 8, 4, 2
    nc = get_nc(npj, nsj, 4)
    maps = make_in_maps(inp, ncores, npj, nsj)
    res = run_bass_kernel_spmd(nc, maps, core_ids=list(range(ncores)))
    R = res.results
    cat = lambda name, axis: np.concatenate([np.asarray(r[name]) for r in R], axis=axis)
    y_p = cat("yp", 0)
    y_s = cat("ys", 0)
    akp = cat("akp", 1).reshape(4, 32, 512, 8, 64)
    avp = cat("avp", 1).reshape(4, 32, 512, 8, 64)
    bkp = cat("bkp", 1).reshape(4, 32, 2048, 8, 64)
    bvp = cat("bvp", 1).reshape(4, 32, 2048, 8, 64)
    bip = cat("bip", 1)
    aks = cat("aks", 1).reshape(4, 16, 64, 8, 64)
    avs = cat("avs", 1).reshape(4, 16, 64, 8, 64)
    bks = cat("bks", 1).reshape(4, 16, 64, 8, 64)
    bvs = cat("bvs", 1).reshape(4, 16, 64, 8, 64)
    bis = cat("bis", 1)
    return (y_p, y_s, akp, avp, bkp, bvp, bip, aks, avs, bks, bvs, bis)
```

```python
import numpy as np
import concourse.bass as bass
import concourse.mybir as mybir
from concourse.bass_utils import run_bass_kernel_spmd
from contextlib import ExitStack
from functools import partial as _partial

F32 = mybir.dt.float32
BF = mybir.dt.bfloat16
ALU = mybir.AluOpType
AF = mybir.ActivationFunctionType
AX = mybir.AxisListType.X

D = 1024
NL_FULL = 4
EW = 3656
DFF = 4096
SEQ = 2048
NT = 16
NHIST = 17
ALPHA = float(8 ** 0.25)
EPS = 1e-5
NEG = -30000.0
NEGBIG = -1.0e30
NIT = 12
TOPK = 256.0
CW = 256
WCH = 512
WI_SCALE = float(512 ** -0.5)


import os as _os
STOP = int(_os.environ.get("KSTOP", "99"))


class StopBuild(Exception):
    pass


class Buf:
    __slots__ = ("name", "w", "r")

    def __init__(self, name=""):
        self.name = name
        self.w = {}
        self.r = {}


class Sch:
    ENG = ("pe", "act", "dve", "pool", "sp")

    def __init__(self, nc, es, ndma=12, nepoch=8):
        self.nc = nc
        self.eng = {"pe": nc.tensor, "act": nc.scalar, "dve": nc.vector, "pool": nc.gpsimd, "sp": nc.sync}
        self.cnt = {e: 0 for e in self.ENG}
        self.semh = {}
        self.epoch = 0
        for ep in range(nepoch):
            for e in self.ENG:
                if e == "sp":
                    continue
                self.semh[(e, ep)] = es.enter_context(nc.semaphore(f"s_{e}_{ep}"))
        self.dq = {}
        for q in ("sp", "pool"):
            sems = []
            for i in range(ndma):
                k = ("d", q, i)
                self.semh[k] = es.enter_context(nc.semaphore(f"d_{q}_{i}"))
                sems.append(k)
            self.dq[q] = {"sems": sems, "uses": [0] * ndma, "next": 0}
        self.waited = {e: {} for e in self.ENG}
        self.same_win = 3
        self.ninstr = 0

    def _wait(self, e, k, v):
        if self.waited[e].get(k, 0) >= v:
            return
        self.eng[e].wait_ge(self.semh[k], v)
        self.waited[e][k] = v
        self.ninstr += 1

    def _deps(self, e, reads, writes):
        need = {}
        for b in reads:
            for k, v in b.w.items():
                if need.get(k, 0) < v:
                    need[k] = v
        for b in writes:
            for k, v in b.w.items():
                if need.get(k, 0) < v:
                    need[k] = v
            for k, v in b.r.items():
                if need.get(k, 0) < v:
                    need[k] = v
        for k, v in need.items():
            if k[0] != "d":
                if k[1] != self.epoch:
                    continue
                if k[0] == e:
                    if e == "pe":
                        continue
                    if v <= self.cnt[e] - self.same_win:
                        continue
            self._wait(e, k, v)

    def _mark(self, tok, reads, writes, partial):
        k, v = tok
        for b in reads:
            if b.r.get(k, 0) < v:
                b.r[k] = v
        for b in writes:
            if partial:
                if b.w.get(k, 0) < v:
                    b.w[k] = v
            else:
                b.w = {k: v}
                b.r = {}

    def op(self, e, reads, writes, emit, partial=False):
        self._deps(e, reads, writes)
        ins = emit(self.eng[e])
        self.cnt[e] += 1
        self.ninstr += 1
        k = (e, self.epoch)
        ins.then_inc(self.semh[k], 1)
        self._mark((k, self.cnt[e]), reads, writes, partial)
        return ins

    def dma(self, q, out, in_, reads, writes, partial=False):
        self._deps(q, reads, writes)
        d = self.dq[q]
        i = d["next"]
        d["next"] = (i + 1) % len(d["sems"])
        k = d["sems"][i]
        if d["uses"][i] > 0:
            self._wait(q, k, 16 * d["uses"][i])
        ins = self.eng[q].dma_start(out=out, in_=in_)
        ins.then_inc(self.semh[k], 16)
        self.ninstr += 1
        d["uses"][i] += 1
        self._mark((k, 16 * d["uses"][i]), reads, writes, partial)
        return ins

    def barrier(self, engines):
        for e in engines:
            for k in ("pe", "act", "dve", "pool"):
                if k == e:
                    continue
                if self.cnt[k] > 0:
                    self._wait(e, (k, self.epoch), self.cnt[k])

    def wait_all_dma(self, e):
        for q, d in self.dq.items():
            for i, k in enumerate(d["sems"]):
                if d["uses"][i] > 0:
                    self._wait(e, k, 16 * d["uses"][i])

    def new_epoch(self):
        self.barrier(self.ENG)
        for e in self.ENG:
            self.wait_all_dma(e)
        self.epoch += 1
        for e in self.ENG:
            self.cnt[e] = 0


class Builder:
    def __init__(self, npj=4, nsj=2, nl=4):
        self.npj, self.nsj, self.nl = npj, nsj, nl
        nc = self.nc = bass.Bass("TRN2", target_bir_lowering=False)
        dt = nc.dram_tensor
        I, O, N = "ExternalInput", "ExternalOutput", "Internal"
        nsj_ = max(nsj, 1)
        npj_ = max(npj, 1)
        self.xp = dt("xp", [npj_, SEQ, D], F32, kind=I)
        self.xs = dt("xs", [nsj_, 64, D], F32, kind=I)
        self.cak = dt("cak", [NL_FULL, nsj_, 512, 512], F32, kind=I)
        self.cav = dt("cav", [NL_FULL, nsj_, 512, 512], F32, kind=I)
        self.cbk = dt("cbk", [NL_FULL, nsj_, SEQ, 512], F32, kind=I)
        self.cbv = dt("cbv", [NL_FULL, nsj_, SEQ, 512], F32, kind=I)
        self.cbi = dt("cbi", [NL_FULL, nsj_, SEQ, 64], F32, kind=I)
        self.w_in = dt("w_in", [NL_FULL, D, EW], F32, kind=I)
        self.w_out = dt("w_out", [NL_FULL, D, D], F32, kind=I)
        self.w_up = dt("w_up", [NL_FULL, D, DFF], F32, kind=I)
        self.w_down = dt("w_down", [NL_FULL, DFF, D], F32, kind=I)
        self.relb = dt("relb", [NL_FULL * 8, 257], F32, kind=I)
        self.lnp_d = dt("lnp", [NL_FULL, 4, D], F32, kind=I)
        self.cs_d = dt("cs_tab", [NHIST, 128, 96], F32, kind=I)
        self.yp = dt("yp", [npj_, SEQ, D], F32, kind=O)
        self.ys = dt("ys", [nsj_, 64, D], F32, kind=O)
        self.akp = dt("akp", [NL_FULL, npj_, 512, 512], F32, kind=O)
        self.avp = dt("avp", [NL_FULL, npj_, 512, 512], F32, kind=O)
        self.bkp = dt("bkp", [NL_FULL, npj_, SEQ, 512], F32, kind=O)
        self.bvp = dt("bvp", [NL_FULL, npj_, SEQ, 512], F32, kind=O)
        self.bip = dt("bip", [NL_FULL, npj_, SEQ, 64], F32, kind=O)
        self.aks = dt("aks", [NL_FULL, nsj_, 64, 512], F32, kind=O)
        self.avs = dt("avs", [NL_FULL, nsj_, 64, 512], F32, kind=O)
        self.bks = dt("bks", [NL_FULL, nsj_, 64, 512], F32, kind=O)
        self.bvs = dt("bvs", [NL_FULL, nsj_, 64, 512], F32, kind=O)
        self.bis = dt("bis", [NL_FULL, nsj_, 64, 64], F32, kind=O)
        self.w_in16 = dt("w_in16", [NL_FULL, 8, 128, 8 * WCH], BF, kind=N)
        self.w_out16 = dt("w_out16", [NL_FULL, 2, 128, 8 * WCH], BF, kind=N)
        self.w_up16 = dt("w_up16", [NL_FULL, 8, 128, 8 * WCH], BF, kind=N)
        self.w_down16 = dt("w_down16", [NL_FULL, 8, 128, 8 * WCH], BF, kind=N)
        self.ext_d = dt("ext_tab", [NL_FULL * 8, 768], F32, kind=N)

    def T(self, name, shape, dtype):
        return self.es.enter_context(self.nc.sbuf_tensor(name, shape, dtype))

    def build(self):
        nc = self.nc
        with ExitStack() as es:
            self.es = es
            S = self.S = Sch(nc, es)
            T = self.T
            self.x = T("x", [128, NT, D], F32)
            self.xB = [Buf(f"x{t}") for t in range(NT)]
            self.kbT = T("kbT", [128, 4, NHIST * 128], BF)
            self.vbA = T("vbA", [128, NHIST, 8, 65], BF)
            self.kiT = T("kiT", [128, NHIST * 128], BF)
            self.hB = [Buf(f"hist{t}") for t in range(NHIST)]
            self.kaT = T("kaT", [128, 4, 8 * 128], BF)
            self.vaA = T("vaA", [128, 8, 8, 65], BF)
            self.aB = [Buf(f"band{t}") for t in range(8)]
            self.BT = T("BT", [128, 8, 5, 128], BF)
            self.BTb = Buf("BT")
            self.cs = T("cs", [128, NHIST, 96], F32)
            self.csb = Buf("cs")
            self.ident = T("ident", [128, 128], BF)
            self.flipJ = T("flipJ", [128, 128], BF)
            self.negI = T("negI", [128, 128], BF)
            self.constb = Buf("const")
            self.pow2 = T("pow2", [128, NIT], F32)
            self.cbias = T("cbias", [128, 2], F32)
            self.wbuf = [T(f"wbuf{i}", [128, 8, WCH], BF) for i in range(2)]
            self.wB = [Buf(f"wbuf{i}") for i in range(2)]
            self.wuse = 0
            self.T8 = T("T8", [128, 8, 512], BF)
            self.T8b = Buf("T8")
            self.lng = None
            self.lngB = Buf("lng")
            self.hank = [T(f"hank{i}", [128, 5, 128], BF) for i in range(1)]
            self.hkB = [Buf(f"hank{i}") for i in range(1)]
            self.xb16 = T("xb16", [128, D], BF)
            self.xb16B = Buf("xb16")
            self.small = T("small", [128, 64], F32)
            self.smB = Buf("small")
            self.smBs = [Buf(f"small{i}") for i in range(4)]
            P = lambda n, sh, d_: es.enter_context(nc.psum_tensor(n, sh, d_))
            self.psA = [P(f"psA{i}", [128, 512], F32) for i in range(2)]
            self.psT = [P(f"psT{i}", [128, 1024], BF) for i in range(2)]
            self.psS = [P(f"psS{i}", [128, 512], F32) for i in range(2)]
            self.psO = [P(f"psO{i}", [128, 512], F32) for i in range(2)]
            self.pAB = [Buf(f"psA{i}") for i in range(2)]
            self.pTB = [Buf(f"psT{i}") for i in range(2)]
            self.pSB = [Buf(f"psS{i}") for i in range(2)]
            self.pOB = [Buf(f"psO{i}") for i in range(2)]
            self.ua = self.ut = self.us = 0
            self.uid = 0
            self.njob = 0
            self.wcB = {(w, l): Buf(f"wc_{w}{l}") for w in ("in", "out", "up", "down") for l in range(NL_FULL)}
            self.extB = Buf("ext")
            self.outB = Buf("out")

            try:
                self.prologue()
                self.stopflag = False
                for j in range(self.npj):
                    if not self.stopflag:
                        self.job("p", j)
                for j in range(self.nsj):
                    if not self.stopflag:
                        self.job("s", j)
            except StopBuild:
                pass
            S.barrier(["sp"])
            S.wait_all_dma("sp")
        return nc

    def prologue(self):
        S, nc = self.S, self.nc
        cb = self.constb
        tmpf, tb = self.x[:, 2, 0:128], self.xB[2]
        t32, t32b = self.x[0:32, 1, 0:257], self.xB[1]
        e32, e32b = self.x[0:32, 0, 0:768], self.xB[0]
        S.op("pool", [], [tb], lambda e: e.memset(tmpf, 1.0))
        S.op("pool", [tb], [tb], lambda e: e.affine_select(out=tmpf, in_=tmpf, pattern=[[-1, 128]],
                                                           compare_op=ALU.is_equal, fill=0.0, base=0,
                                                           channel_multiplier=1))
        S.op("dve", [tb], [cb], lambda e: e.tensor_copy(self.ident[:], tmpf), partial=True)
        S.op("dve", [tb], [cb], lambda e: e.tensor_scalar(out=self.negI[:], in0=tmpf, scalar1=NEG, scalar2=None,
                                                          op0=ALU.mult), partial=True)
        S.op("pool", [tb], [tb], lambda e: e.memset(tmpf, 1.0))
        S.op("pool", [tb], [tb], lambda e: e.affine_select(out=tmpf, in_=tmpf, pattern=[[1, 128]],
                                                           compare_op=ALU.is_equal, fill=0.0, base=-127,
                                                           channel_multiplier=1))
        S.op("dve", [tb], [cb], lambda e: e.tensor_copy(self.flipJ[:], tmpf), partial=True)
        for i in range(NIT):
            S.op("pool", [], [cb], lambda e, i=i: e.memset(self.pow2[:, i:i + 1], float(2.0 ** -(i + 1))),
                 partial=True)
        S.op("pool", [], [cb], lambda e: e.memset(self.cbias[:, 0:1], EPS), partial=True)
        S.dma("sp", self.cs[:], self.cs_d.ap().rearrange("t p c -> p t c"), [], [self.csb])
        S.dma("sp", t32, self.relb.ap(), [], [t32b])
        S.op("dve", [t32b], [e32b], lambda e: e.tensor_copy(e32[:, 0:256], t32[:, 1:257]), partial=True)
        S.op("dve", [t32b], [e32b], lambda e: e.tensor_copy(e32[:, 256:768],
                                                            t32[:, 256:257].to_broadcast([32, 512])), partial=True)
        S.dma("sp", self.ext_d.ap(), e32, [e32b], [self.extB])
        CH = 128 * 8 * WCH
        for l in range(self.nl):
            for k in range(8):
                rows = slice(k * 128, (k + 1) * 128)
                S.dma("pool", bass.AP(self.w_in16, l * 8 * CH + k * WCH, [[8 * WCH, 128], [CH, 7], [1, WCH]]),
                      self.w_in.ap()[l, rows, 0:7 * WCH].rearrange("p (c j) -> p c j", c=7), [],
                      [self.wcB[("in", l)]], partial=True)
                S.dma("pool", bass.AP(self.w_in16, l * 8 * CH + 7 * CH + k * WCH, [[8 * WCH, 128], [1, 72]]),
                      self.w_in.ap()[l, rows, 7 * WCH:EW], [], [self.wcB[("in", l)]], partial=True)
                S.dma("pool", bass.AP(self.w_out16, l * 2 * CH + k * WCH, [[8 * WCH, 128], [CH, 2], [1, WCH]]),
                      self.w_out.ap()[l, rows, :].rearrange("p (c j) -> p c j", c=2), [],
                      [self.wcB[("out", l)]], partial=True)
                S.dma("pool", bass.AP(self.w_up16, l * 8 * CH + k * WCH, [[8 * WCH, 128], [CH, 8], [1, WCH]]),
                      self.w_up.ap()[l, rows, :].rearrange("p (c j) -> p c j", c=8), [],
                      [self.wcB[("up", l)]], partial=True)
            for rb in range(32):
                fg, k = rb // 8, rb % 8
                S.dma("pool", bass.AP(self.w_down16, l * 8 * CH + fg * CH + k * WCH, [[8 * WCH, 128], [4 * CH, 2], [1, WCH]]),
                      self.w_down.ap()[l, rb * 128:(rb + 1) * 128, :].rearrange("p (c j) -> p c j", c=2), [],
                      [self.wcB[("down", l)]], partial=True)

    def wchunk(self, name, l, a, b_=0, width=WCH):
        i = self.wuse % 2
        self.wuse += 1
        wt, wb = self.wbuf[i], self.wB[i]
        CH = 128 * 8 * WCH
        if name == "in":
            tens, ci = self.w_in16, l * 8 + a
        elif name == "out":
            tens, ci = self.w_out16, l * 2 + a
        elif name == "up":
            tens, ci = self.w_up16, l * 8 + a
        else:
            tens, ci = self.w_down16, l * 8 + a * 4 + b_
        if width == WCH:
            src = bass.AP(tens, ci * CH, [[8 * WCH, 128], [1, 8 * WCH]])
            self.S.dma("sp", wt[:].rearrange("p k j -> p (k j)"), src, [self.wcB[(name, l)]], [wb])
            return wt, wb
        src = bass.AP(tens, ci * CH, [[8 * WCH, 128], [WCH, 8], [1, width]])
        self.S.dma("sp", wt[:, :, 0:width], src, [self.wcB[(name, l)]], [wb])
        return wt, wb

    def nxt(self, which):
        if which == "A":
            i = self.ua % 2; self.ua += 1
            return self.psA[i], self.pAB[i]
        if which == "T":
            i = self.ut % 2; self.ut += 1
            return self.psT[i], self.pTB[i]
        i = self.us % 2; self.us += 1
        return self.psS[i], self.pSB[i]

    def job(self, kind, j):
        S, nc = self.S, self.nc
        prompt = kind == "p"
        if self.njob > 0:
            S.new_epoch()
        self.njob += 1
        ntile = NT if prompt else 1
        if prompt:
            for t in range(NT):
                S.dma("sp", self.x[:, t, :], self.xp.ap()[j, t * 128:(t + 1) * 128, :], [], [self.xB[t]])
        else:
            S.op("pool", [], [self.xB[0]], lambda e: e.memset(self.x[:, 0, :], 0.0))
            S.dma("sp", self.x[0:64, 0, :], self.xs.ap()[j, :, :], [], [self.xB[0]], partial=True)
        for l in range(self.nl):
            if STOP <= 1:
                self.stopflag = True
                return
            self.build_bias(l)
            if STOP <= 2:
                self.stopflag = True
                return
            if not prompt:
                self.load_caches(l, j)
            groups = [list(range(g * 4, g * 4 + 4)) for g in range(4)] if prompt else [[0]]
            for tiles in groups:
                self.phaseA(kind, j, l, tiles)
                S.barrier(["pe", "act", "dve", "pool"])
                if self.stopflag:
                    return
                self.phaseB(kind, j, l, tiles)
                S.barrier(["pe", "act", "dve", "pool"])
                if self.stopflag:
                    return
        if prompt:
            for t in range(NT):
                S.dma("sp", self.yp.ap()[j, t * 128:(t + 1) * 128, :], self.x[:, t, :], [self.xB[t]], [self.outB],
                      partial=True)
        else:
            S.dma("sp", self.ys.ap()[j, :, :], self.x[0:64, 0, :], [self.xB[0]], [self.outB], partial=True)

    def build_bias(self, l):
        S = self.S
        BB = int(_os.environ.get("KBB", "9"))
        for h in range(8):
            hk, hb = self.hank[0], self.hkB[0]
            src = bass.AP(self.ext_d, (l * 8 + h) * 768, [[1, 128], [128, 5], [1, 128]])
            S.dma("pool", hk[:], src, [self.extB], [hb])
            if BB <= 1:
                continue
            ps, pb = self.nxt("A")
            for rp in range(4):
                S.op("pe", [hb, self.constb], [pb],
                     lambda e, rp=rp: e.matmul(ps[:, rp * 128:(rp + 1) * 128], self.flipJ[:], hk[:, rp, :],
                                               start=True, stop=True))
            ps2, pb2 = self.nxt("A")
            S.op("pe", [hb, self.constb], [pb2],
                 lambda e: e.matmul(ps2[:, 0:128], self.flipJ[:], hk[:, 4, :], start=True, stop=True))
            if BB <= 2:
                continue
            for rp in range(4):
                S.op("act", [pb], [self.BTb],
                     lambda e, rp=rp: e.activation(out=self.BT[:, h, 4 - rp, :], in_=ps[:, rp * 128:(rp + 1) * 128],
                                                   func=AF.Exp), partial=True)
            S.op("act", [pb2], [self.BTb],
                 lambda e: e.activation(out=self.BT[:, h, 0, :], in_=ps2[:, 0:128], func=AF.Exp), partial=True)
        if BB <= 3:
            return
        S.op("dve", [self.BTb], [self.BTb], lambda e: e.memset(self.BT[64:128, :, 4, 0:64], 0.0), partial=True)
        S.op("dve", [self.BTb], [self.BTb], lambda e: e.memset(self.BT[0:64, :, 0, 64:128], 0.0), partial=True)

    def transposes_to(self, src_ap_fn, nblk, src_bufs, dst_fn, dst_bufs, evac_eng="act"):
        S = self.S
        i = 0
        while i < nblk:
            n = min(8, nblk - i)
            ps, pb = self.nxt("T")
            for q in range(n):
                S.op("pe", src_bufs + [self.constb], [pb],
                     lambda e, q=q, i=i: e.transpose(ps[:, q * 128:(q + 1) * 128], src_ap_fn(i + q), self.ident[:]))
            dst_fn(i, n, ps, pb)
            i += n

    def load_caches(self, l, s):
        S = self.S
        self.uid += 1
        with self.nc.sbuf_tensor(f"cst_{self.uid}", [128, 2, 512], BF) as cst, \
                self.nc.sbuf_tensor(f"cst2_{self.uid}", [128, 2, 128], BF) as cst2:
            cB = [Buf("cst0"), Buf("cst1")]
            c2B = [Buf("cst20"), Buf("cst21")]
            u = 0
            for t in range(16):
                rows = slice(t * 128, (t + 1) * 128)
                i = u % 2; u += 1
                hb = self.hB[t]
                S.dma("pool", cst[:, i, :], self.cbk.ap()[l, s, rows, :], [], [cB[i]])

                def dst(i0, n, ps, pb, t=t, hb=hb):
                    S.op("act", [pb], [hb], lambda e: e.activation(
                        out=self.kbT[:, :, t * 128:(t + 1) * 128],
                        in_=ps[:, 0:512].rearrange("p (a b) -> p a b", a=4), func=AF.Copy), partial=True)
                self.transposes_to(lambda q, i=i: cst[:, i, q * 128:(q + 1) * 128], 4, [cB[i]], dst, [hb])
                S.dma("pool", self.vbA[:, t, :, 0:64], self.cbv.ap()[l, s, rows, :].rearrange("p (h d) -> p h d", h=8),
                      [], [hb], partial=True)
                S.op("pool", [], [hb], lambda e, t=t: e.memset(self.vbA[:, t, :, 64:65], 1.0), partial=True)
                S.dma("pool", cst2[:, i, 0:64], self.cbi.ap()[l, s, rows, :], [], [c2B[i]], partial=True)
                S.dma("pool", cst2[:, i, 64:128], self.cbi.ap()[l, s, rows, :], [], [c2B[i]], partial=True)

                def dst2(i0, n, ps, pb, t=t, hb=hb):
                    S.op("act", [pb], [hb], lambda e: e.activation(
                        out=self.kiT[:, t * 128:(t + 1) * 128], in_=ps[:, 0:128], func=AF.Copy), partial=True)
                self.transposes_to(lambda q, i=i: cst2[:, i, :], 1, [c2B[i]], dst2, [hb])
            for t in range(4):
                rows = slice(t * 128, (t + 1) * 128)
                i = u % 2; u += 1
                ab = self.aB[t]
                S.dma("pool", cst[:, i, :], self.cak.ap()[l, s, rows, :], [], [cB[i]])

                def dst3(i0, n, ps, pb, t=t, ab=ab):
                    S.op("act", [pb], [ab], lambda e: e.activation(
                        out=self.kaT[:, :, t * 128:(t + 1) * 128],
                        in_=ps[:, 0:512].rearrange("p (a b) -> p a b", a=4), func=AF.Copy), partial=True)
                self.transposes_to(lambda q, i=i: cst[:, i, q * 128:(q + 1) * 128], 4, [cB[i]], dst3, [ab])
                S.dma("pool", self.vaA[:, t, :, 0:64], self.cav.ap()[l, s, rows, :].rearrange("p (h d) -> p h d", h=8),
                      [], [ab], partial=True)
                S.op("pool", [], [ab], lambda e, t=t: e.memset(self.vaA[:, t, :, 64:65], 1.0), partial=True)
            S.barrier(["pe", "act", "dve", "pool"])
            S.wait_all_dma("pool")

    def make_xT(self, tiles):
        S = self.S
        for ti, t in enumerate(tiles):
            S.op("act", [self.xB[t]], [self.xb16B], lambda e, t=t: e.activation(out=self.xb16[:], in_=self.x[:, t, :],
                                                                                 func=AF.Copy))

            def dst(i0, n, ps, pb, ti=ti):
                S.op("dve", [pb], [self.T8b], lambda e: e.tensor_copy(
                    self.T8[:, :, ti * 128:(ti + 1) * 128], ps[:, 0:1024].rearrange("p (a b) -> p a b", a=8)),
                    partial=True)
            self.transposes_to(lambda q: self.xb16[:, q * 128:(q + 1) * 128], 8, [self.xb16B], dst, [self.T8b])

    def phaseA(self, kind, j, l, tiles):
        S, nc = self.S, self.nc
        prompt = kind == "p"
        ng = len(tiles)
        ntok = ng * 128
        with ExitStack() as es:
            self.uid += 1
            A = lambda n, sh, d_: es.enter_context(nc.sbuf_tensor(f"{n}_{self.uid}", sh, d_))
            qaT = A("qaT", [128, 4, 512], BF); qbT = A("qbT", [128, 4, 512], BF); qiT = A("qiT", [128, 4, 512], BF)
            qB = Buf("q")
            wsb = A("wsb", [128, 4, 8], F32); wsbB = Buf("wsb")
            hb16 = [A(f"hb16_{i}", [128, CW], BF) for i in range(4)]
            hbB = [Buf(f"hb16_{i}") for i in range(4)]
            t1p = A("t1", [128, CW], F32); t2p = A("t2", [128, CW], F32); t12Bp = Buf("t12")
            t1d = A("t1d", [128, CW], F32); t2d = A("t2d", [128, CW], F32); t12Bd = Buf("t12d")
            score = A("score", [128, NHIST * 128], F32); scB = Buf("score")
            rsb = [A(f"rsb{i}", [128, 512], F32) for i in range(2)]
            rB = [Buf(f"rsb{i}") for i in range(2)]
            maskb = A("maskb", [128, NHIST * 128], BF); mkB = Buf("maskb")
            mbT = A("mbT", [128, NHIST, 128], BF); mtB = Buf("mbT")
            NE = 3
            NHB = 12
            hb16 = hb16 + [mbT[:, 2 * i:2 * i + 2, :].rearrange("p a b -> p (a b)") for i in range(8)]
            hbB = hbB + [Buf(f"hb16x_{i}") for i in range(8)]
            expT = [A(f"expT{i}", [128, 640], BF) for i in range(NE)]
            eB = [Buf(f"expT{i}") for i in range(NE)]
            ue = [0]
            otok = A("otok", [128, 512], BF); otB = Buf("otok")
            st = A("st", [128, 48], F32); stB_ = Buf("st")
            steps = A("steps", [128, NIT], F32)
            stg = [(score[:, i * CW:(i + 1) * CW], Buf(f"xstage{i}")) for i in range(8)]
            su = [0]

            def nstage():
                i = su[0] % len(stg)
                su[0] += 1
                return stg[i]

            self.make_xT(tiles)
            T8 = self.T8
            if STOP <= 3:
                self.stopflag = True
                return

            hu = 0
            deferq = []
            pend = []
            DEFER = 3
            for cc in range(8):
              width = WCH if cc < 7 else 72
              wt, wb = self.wchunk("in", l, cc, width=width)
              seg = cc
              for ti, t in enumerate(tiles):
                psF, pb = self.nxt("A")
                for k in range(8):
                    S.op("pe", [self.T8b, wb], [pb],
                         lambda e, k=k, ti=ti: e.matmul(psF[:, 0:width], T8[:, k, ti * 128:(ti + 1) * 128],
                                                        wt[:, k, 0:width], start=(k == 0), stop=(k == 7)))
                pend.append(list(deferq))
                deferq.clear()
                if len(pend) >= DEFER:
                    for fn_ in pend.pop(0):
                        fn_()
                for hh in range(2 if cc < 7 else 1):
                    ps = psF[:, hh * CW:(hh + 1) * CW] if cc < 7 else psF[:, 0:CW]
                    RE = "pool" if (2 * ti + hh) % 3 == 2 else "dve"
                    t1, t2, t12B = (t1d, t2d, t12Bd) if RE == "dve" else (t1p, t2p, t12Bp)
                    hidx = t if prompt else 16
                    aslot = (t % 8) if prompt else 4
                    cs_c = self.cs[:, hidx, 0:32]
                    cs_s = self.cs[:, hidx, 32:64]
                    nrow = 128 if prompt else 64

                    cs_n = self.cs[:, hidx, 64:96]

                    def rope(src, srcB, nh, dst, dstB):
                        v = lambda a: a.rearrange("p (h two d) -> p h two d", h=nh, two=2)
                        bc = lambda a: a.unsqueeze(1).to_broadcast([128, nh, 32])
                        S.op(RE, [srcB, self.csb], [t12B], lambda e: e.tensor_tensor(
                            out=v(t1[:, 0:nh * 64]), in0=v(src),
                            in1=cs_c.unsqueeze(1).unsqueeze(1).to_broadcast([128, nh, 2, 32]), op=ALU.mult))
                        S.op(RE, [srcB, self.csb], [t12B], lambda e: e.tensor_tensor(
                            out=v(t2[:, 0:nh * 64])[:, :, 0, :], in0=v(src)[:, :, 1, :], in1=bc(cs_n), op=ALU.mult),
                            partial=True)
                        S.op(RE, [srcB, self.csb], [t12B], lambda e: e.tensor_tensor(
                            out=v(t2[:, 0:nh * 64])[:, :, 1, :], in0=v(src)[:, :, 0, :], in1=bc(cs_s), op=ALU.mult),
                            partial=True)
                        S.op(RE, [t12B], [dstB], lambda e: e.tensor_tensor(
                            out=dst, in0=t1[:, 0:nh * 64], in1=t2[:, 0:nh * 64], op=ALU.add))

                    def out_dma(dram, col0, ncol, src_t, src_b):
                        S.dma("sp", dram[0:nrow, col0:col0 + ncol], src_t[0:nrow, 0:ncol], [src_b], [self.outB],
                              partial=True)

                    def tr_to(dstT, hbt, hbb, ti=ti, hh=hh):
                        def dst(i0, n, ps_, pb_):
                            S.op("act", [pb_], [qB], lambda e: e.activation(
                                out=dstT[:, 2 * hh:2 * hh + 2, ti * 128:(ti + 1) * 128],
                                in_=ps_[:, 0:256].rearrange("p (a b) -> p a b", a=2), func=AF.Copy), partial=True)
                        self.transposes_to(lambda q: hbt[:, q * 128:(q + 1) * 128], 2, [hbb], dst, [qB])

                    hi_ = hu % NHB
                    if seg == 0:
                        hu += 1
                        S.op("act", [pb], [hbB[hi_]], lambda e: e.activation(out=hb16[hi_][:], in_=ps[:, 0:CW],
                                                                             func=AF.Copy, scale=0.125))
                        deferq.append(_partial(tr_to, qaT, hb16[hi_], hbB[hi_]))
                    elif seg == 1:
                        hu += 1
                        ab = self.aB[aslot]
                        S.op("act", [pb], [hbB[hi_]], lambda e: e.activation(out=hb16[hi_][:], in_=ps[:, 0:CW],
                                                                             func=AF.Copy))
                        if (prompt and t >= 12) or not prompt:
                            sg, sgb = nstage()
                            S.op("act", [pb], [sgb], lambda e: e.activation(out=sg[:, 0:CW], in_=ps[:, 0:CW], func=AF.Copy))
                            dram = (self.akp.ap()[l, j, (t - 12) * 128:(t - 11) * 128, :] if prompt
                                    else self.aks.ap()[l, j, :, :])
                            out_dma(dram, hh * CW, CW, sg, sgb)

                        def dstk(i0, n, ps_, pb_, aslot=aslot, ab=ab, hh=hh):
                            S.op("act", [pb_], [ab], lambda e: e.activation(
                                out=self.kaT[:, 2 * hh:2 * hh + 2, aslot * 128:(aslot + 1) * 128],
                                in_=ps_[:, 0:256].rearrange("p (a b) -> p a b", a=2), func=AF.Copy), partial=True)
                        deferq.append(_partial(self.transposes_to, lambda q, hi_=hi_: hb16[hi_][:, q * 128:(q + 1) * 128],
                                               2, [hbB[hi_]], dstk, [ab]))
                    elif seg == 2:
                        ab = self.aB[aslot]
                        S.op("act", [pb], [ab], lambda e: e.activation(
                            out=self.vaA[:, aslot, 4 * hh:4 * hh + 4, 0:64],
                            in_=ps[:, 0:CW].rearrange("p (h d) -> p h d", h=4), func=AF.Copy), partial=True)
                        if hh == 0:
                            S.op("pool", [], [ab], lambda e: e.memset(self.vaA[:, aslot, :, 64:65], 1.0), partial=True)
                        if (prompt and t >= 12) or not prompt:
                            sg, sgb = nstage()
                            S.op("act", [pb], [sgb], lambda e: e.activation(out=sg[:, 0:CW], in_=ps[:, 0:CW], func=AF.Copy))
                            dram = (self.avp.ap()[l, j, (t - 12) * 128:(t - 11) * 128, :] if prompt
                                    else self.avs.ap()[l, j, :, :])
                            out_dma(dram, hh * CW, CW, sg, sgb)
                    elif seg in (3, 4, 6):
                        hu += 1
                        sg, sgb = nstage()
                        sc = 0.125 if seg == 3 else 1.0
                        S.op("act", [pb], [sgb], lambda e: e.activation(out=sg[:, 0:CW], in_=ps[:, 0:CW], func=AF.Copy,
                                                                          scale=sc))
                        if seg == 4:
                            rope(sg[:, 0:CW], sgb, 4, sg[:, 0:CW], sgb)
                            dram = (self.bkp.ap()[l, j, t * 128:(t + 1) * 128, :] if prompt else self.bks.ap()[l, j, :, :])
                            out_dma(dram, hh * CW, CW, sg, sgb)
                            S.op(RE, [sgb], [hbB[hi_]], lambda e: e.tensor_copy(hb16[hi_][:], sg[:, 0:CW]))
                            hb_ = self.hB[hidx]

                            def dstk(i0, n, ps_, pb_, hidx=hidx, hb_=hb_, hh=hh):
                                S.op("act", [pb_], [hb_], lambda e: e.activation(
                                    out=self.kbT[:, 2 * hh:2 * hh + 2, hidx * 128:(hidx + 1) * 128],
                                    in_=ps_[:, 0:256].rearrange("p (a b) -> p a b", a=2), func=AF.Copy), partial=True)
                            deferq.append(_partial(self.transposes_to,
                                                   lambda q, hi_=hi_: hb16[hi_][:, q * 128:(q + 1) * 128], 2,
                                                   [hbB[hi_]], dstk, [hb_]))
                        else:
                            rope(sg[:, 0:CW], sgb, 4, hb16[hi_][:], hbB[hi_])
                            deferq.append(_partial(tr_to, qbT if seg == 3 else qiT, hb16[hi_], hbB[hi_]))
                    elif seg == 5:
                        hb_ = self.hB[hidx]
                        S.op("act", [pb], [hb_], lambda e: e.activation(
                            out=self.vbA[:, hidx, 4 * hh:4 * hh + 4, 0:64],
                            in_=ps[:, 0:CW].rearrange("p (h d) -> p h d", h=4), func=AF.Copy), partial=True)
                        if hh == 0:
                            S.op("pool", [], [hb_], lambda e: e.memset(self.vbA[:, hidx, :, 64:65], 1.0), partial=True)
                        sg, sgb = nstage()
                        S.op("act", [pb], [sgb], lambda e: e.activation(out=sg[:, 0:CW], in_=ps[:, 0:CW], func=AF.Copy))
                        dram = (self.bvp.ap()[l, j, t * 128:(t + 1) * 128, :] if prompt else self.bvs.ap()[l, j, :, :])
                        out_dma(dram, hh * CW, CW, sg, sgb)
                    else:
                        hu += 1
                        sg, sgb = nstage()
                        S.op("act", [pb], [sgb], lambda e: e.activation(out=sg[:, 0:72], in_=ps[:, 0:72], func=AF.Copy))
                        S.op(RE, [sgb], [wsbB], lambda e: e.tensor_scalar(
                            out=wsb[:, ti, :], in0=sg[:, 64:72], scalar1=WI_SCALE, scalar2=None, op0=ALU.mult),
                            partial=True)
                        rope(sg[:, 0:64], sgb, 1, sg[:, 0:64], sgb)
                        dram = (self.bip.ap()[l, j, t * 128:(t + 1) * 128, :] if prompt else self.bis.ap()[l, j, :, :])
                        out_dma(dram, 0, 64, sg, sgb)
                        S.op(RE, [sgb], [hbB[hi_]], lambda e: e.tensor_copy(hb16[hi_][:, 0:64], sg[:, 0:64]))
                        S.op(RE, [sgb], [hbB[hi_]], lambda e: e.tensor_copy(hb16[hi_][:, 64:128], sg[:, 0:64]),
                             partial=True)
                        hb_ = self.hB[hidx]

                        def dstk(i0, n, ps_, pb_, hidx=hidx, hb_=hb_):
                            S.op("act", [pb_], [hb_], lambda e: e.activation(
                                out=self.kiT[:, hidx * 128:(hidx + 1) * 128], in_=ps_[:, 0:128], func=AF.Copy),
                                partial=True)
                        deferq.append(_partial(self.transposes_to, lambda q, hi_=hi_: hb16[hi_][:, 0:128], 1,
                                               [hbB[hi_]], dstk, [hb_]))

            pend.append(list(deferq))
            deferq.clear()
            for lst_ in pend:
                for fn_ in lst_:
                    fn_()
            pend.clear()
            if STOP <= 4:
                self.stopflag = True
                return
            S.op("dve", [], [scB] + [b_ for _, b_ in stg], lambda e: e.memset(score[:, 0:1], 0.0))
            def tinfo(t):
                nkt = (t + 1) if prompt else NHIST
                return nkt, nkt * 128

            def band(ti, t):
                qc = slice(ti * 128, (ti + 1) * 128)
                if prompt:
                    win = [(tt, tt - (t - 4), tt % 8) for tt in range(max(0, t - 4), t + 1)]
                else:
                    win = [(r, r, r) for r in range(5)]
                prev_pv = None
                for h in range(8):
                    pr, hf = h // 2, (h % 2) * 64
                    psS, pSb = self.nxt("S")
                    psX, pXb = self.psA[h % 2], self.pAB[h % 2]
                    for wi_, (tt, r, slot) in enumerate(win):
                        o_ps, o_b = (psS, pSb) if wi_ < 4 else (psX, pXb)
                        oc = slice((wi_ % 4) * 128, (wi_ % 4 + 1) * 128)
                        S.op("pe", [self.aB[slot], qB], [o_b], lambda e, o_ps=o_ps, oc=oc, slot=slot: e.matmul(
                            o_ps[:, oc], self.kaT[hf:hf + 64, pr, slot * 128:(slot + 1) * 128],
                            qaT[hf:hf + 64, pr, qc], start=True, stop=True))
                    ei = ue[0] % NE
                    ue[0] += 1
                    n1 = min(4, len(win))
                    S.op("act", [pSb], [eB[ei]], lambda e, n1=n1, ei=ei: e.activation(
                        out=expT[ei][:, 0:n1 * 128], in_=psS[:, 0:n1 * 128], func=AF.Exp))
                    if len(win) == 5:
                        S.op("act", [pXb], [eB[ei]], lambda e, ei=ei: e.activation(
                            out=expT[ei][:, 512:640], in_=psX[:, 0:128], func=AF.Exp), partial=True)
                    r0 = win[0][1]
                    nw = len(win)
                    S.op("pool", [eB[ei], self.BTb], [eB[ei]], lambda e, ei=ei, r0=r0, nw=nw, h=h: e.tensor_tensor(
                        out=expT[ei][:, 0:nw * 128], in0=expT[ei][:, 0:nw * 128],
                        in1=self.BT[:, h, r0:r0 + nw, :].rearrange("p a b -> p (a b)"), op=ALU.mult))
                    def pv(h=h, ei=ei):
                        po, pob = self.psO[h // 4], self.pOB[h // 4]
                        for wi_, (tt, r, slot) in enumerate(win):
                            S.op("pe", [eB[ei], self.aB[slot]], [pob], lambda e, wi_=wi_, slot=slot: e.matmul(
                                po[:, (h % 4) * 65:(h % 4) * 65 + 65], expT[ei][:, wi_ * 128:(wi_ + 1) * 128],
                                self.vaA[:, slot, h, :], start=(wi_ == 0), stop=(wi_ == len(win) - 1)))
                    if prev_pv is not None:
                        prev_pv()
                    prev_pv = pv
                    yield
                prev_pv()
                self.finish_attn(st, stB_, otok, otB, 0, ti)

            def idx(ti, t):
                qc = slice(ti * 128, (ti + 1) * 128)
                nkt, N = tinfo(t)
                for h in range(8):
                    pr, hf = h // 2, (h % 2) * 64
                    for k0 in range(0, N, 512):
                        kn = min(512, N - k0)
                        psS, pSb = self.nxt("S")
                        S.op("pe", [qB] + [self.hB[x_] for x_ in range(k0 // 128, (k0 + kn) // 128)], [pSb],
                             lambda e, k0=k0, kn=kn, psS=psS: e.matmul(psS[:, 0:kn], qiT[hf:hf + 64, pr, qc],
                                                                       self.kiT[hf:hf + 64, k0:k0 + kn],
                                                                       start=True, stop=True))
                        ri = self.us % 2
                        S.op("act", [pSb], [rB[ri]], lambda e, kn=kn, ri=ri, psS=psS: e.activation(
                            out=rsb[ri][:, 0:kn], in_=psS[:, 0:kn], func=AF.Relu))
                        if h == 0:
                            S.op("dve", [rB[ri], wsbB], [scB], lambda e, k0=k0, kn=kn, ri=ri: e.tensor_scalar(
                                out=score[:, k0:k0 + kn], in0=rsb[ri][:, 0:kn], scalar1=wsb[:, ti, 0:1], scalar2=None,
                                op0=ALU.mult), partial=True)
                        else:
                            S.op("dve", [rB[ri], wsbB, scB], [scB],
                                 lambda e, k0=k0, kn=kn, ri=ri, h=h: e.scalar_tensor_tensor(
                                     out=score[:, k0:k0 + kn], in0=rsb[ri][:, 0:kn], scalar=wsb[:, ti, h:h + 1],
                                     in1=score[:, k0:k0 + kn], op0=ALU.mult, op1=ALU.add), partial=True)
                    yield
                if prompt:
                    S.op("dve", [scB], [scB], lambda e: e.memset(score[0:64, N - 64:N], NEGBIG), partial=True)
                else:
                    S.op("dve", [scB], [scB], lambda e: e.memset(score[:, N - 64:N], NEGBIG), partial=True)

            def bis(ti, t):
                nkt, N = tinfo(t)
                need_topk = (not prompt) or t >= 2
                if need_topk:
                    S.op("dve", [scB], [stB_], lambda e: e.tensor_reduce(out=st[:, 0:1], in_=score[:, 0:N], axis=AX,
                                                                         op=ALU.max), partial=True)
                    if prompt:
                        S.op("dve", [scB], [stB_], lambda e: e.tensor_reduce(out=st[0:64, 1:2], in_=score[0:64, 0:N - 64],
                                                                             axis=AX, op=ALU.min), partial=True)
                        S.op("dve", [scB], [stB_], lambda e: e.tensor_reduce(out=st[64:128, 1:2], in_=score[64:128, 0:N],
                                                                             axis=AX, op=ALU.min), partial=True)
                    else:
                        S.op("dve", [scB], [stB_], lambda e: e.tensor_reduce(out=st[:, 1:2], in_=score[:, 0:N - 64],
                                                                             axis=AX, op=ALU.min), partial=True)
                    S.op("dve", [stB_], [stB_], lambda e: e.tensor_tensor(out=st[:, 2:3], in0=st[:, 0:1], in1=st[:, 1:2],
                                                                          op=ALU.subtract))
                    S.op("dve", [stB_, self.constb], [stB_], lambda e: e.tensor_scalar(
                        out=steps[:], in0=self.pow2[:], scalar1=st[:, 2:3], scalar2=None, op0=ALU.mult))
                    S.op("dve", [stB_], [stB_], lambda e: e.tensor_tensor(out=st[:, 4:5], in0=st[:, 1:2],
                                                                          in1=steps[:, 0:1], op=ALU.add))
                    for it in range(NIT):
                        S.op("dve", [stB_, scB], [stB_, mkB], lambda e: e.tensor_scalar(
                            out=maskb[:, 0:N], in0=score[:, 0:N], scalar1=st[:, 4:5], scalar2=None, op0=ALU.is_ge,
                            op1=ALU.add, accum_out=st[:, 5:6]))
                        S.op("dve", [stB_], [stB_], lambda e: e.tensor_scalar(
                            out=st[:, 6:7], in0=st[:, 5:6], scalar1=TOPK, scalar2=0.5, op0=ALU.is_ge,
                            op1=ALU.subtract))
                        S.op("dve", [stB_], [stB_], lambda e, it=it: e.scalar_tensor_tensor(
                            out=st[:, 4:5], in0=st[:, 6:7], scalar=steps[:, it:it + 1], in1=st[:, 4:5],
                            op0=ALU.mult, op1=ALU.add))
                    S.op("dve", [stB_], [stB_], lambda e: e.scalar_tensor_tensor(
                        out=st[:, 3:4], in0=steps[:, NIT - 1:NIT], scalar=-0.5, in1=st[:, 4:5], op0=ALU.mult,
                        op1=ALU.add))
                else:
                    S.op("dve", [stB_], [stB_], lambda e: e.memset(st[:, 3:4], -1.0e29))
                S.op("dve", [stB_, scB], [mkB], lambda e: e.tensor_scalar(
                    out=maskb[:, 0:N], in0=score[:, 0:N], scalar1=st[:, 3:4], scalar2=None, op0=ALU.is_ge))

            def maskT(ti, t):
                nkt, N = tinfo(t)

                def dstm(i0, n, ps_, pb_):
                    S.op("act", [pb_], [mtB] + hbB[4:], lambda e: e.activation(
                        out=mbT[:, i0:i0 + n, :], in_=ps_[:, 0:n * 128].rearrange("p (a b) -> p a b", a=n),
                        func=AF.Copy), partial=True)
                self.transposes_to(lambda q: maskb[:, q * 128:(q + 1) * 128], nkt, [mkB], dstm, [mtB])

            def sparse_main(ti, t):
                qc = slice(ti * 128, (ti + 1) * 128)
                nkt, N = tinfo(t)
                prev_pv = None
                for h in range(8):
                    pr, hf = h // 2, (h % 2) * 64
                    po, pob = self.psO[h // 4], self.pOB[h // 4]
                    for k0 in range(0, nkt, 4):
                        kn = min(4, nkt - k0)
                        psS, pSb = self.nxt("S")
                        for q in range(kn):
                            kt = k0 + q
                            S.op("pe", [self.hB[kt], qB], [pSb], lambda e, q=q, kt=kt, psS=psS: e.matmul(
                                psS[:, q * 128:(q + 1) * 128], self.kbT[hf:hf + 64, pr, kt * 128:(kt + 1) * 128],
                                qbT[hf:hf + 64, pr, qc], start=True, stop=True))
                        ei = ue[0] % NE
                        ue[0] += 1
                        S.op("act", [pSb], [eB[ei]], lambda e, kn=kn, ei=ei, psS=psS: e.activation(
                            out=expT[ei][:, 0:kn * 128], in_=psS[:, 0:kn * 128], func=AF.Exp))
                        S.op("pool", [eB[ei], mtB], [eB[ei]], lambda e, kn=kn, ei=ei, k0=k0: e.tensor_tensor(
                            out=expT[ei][:, 0:kn * 128], in0=expT[ei][:, 0:kn * 128],
                            in1=mbT[:, k0:k0 + kn, :].rearrange("p a b -> p (a b)"), op=ALU.mult))
                        def pv(h=h, ei=ei, k0=k0, kn=kn, po=po, pob=pob):
                            for q in range(kn):
                                kt = k0 + q
                                S.op("pe", [eB[ei], self.hB[kt]], [pob], lambda e, q=q, kt=kt: e.matmul(
                                    po[:, (h % 4) * 65:(h % 4) * 65 + 65], expT[ei][:, q * 128:(q + 1) * 128],
                                    self.vbA[:, kt, h, :], start=(kt == 0), stop=(kt == nkt - 1)))
                        if prev_pv is not None:
                            prev_pv()
                        prev_pv = pv
                prev_pv()

            def run(*gens):
                gens = list(gens)
                while gens:
                    for g in list(gens):
                        try:
                            next(g)
                        except StopIteration:
                            gens.remove(g)

            run(idx(0, tiles[0]))
            bis(0, tiles[0])
            for ti, t in enumerate(tiles):
                nxt_ = ti + 1 < len(tiles)
                if nxt_:
                    run(band(ti, t), idx(ti + 1, tiles[ti + 1]))
                else:
                    run(band(ti, t))
                if self.chk(5):
                    return
                maskT(ti, t)
                if nxt_:
                    bis(ti + 1, tiles[ti + 1])
                sparse_main(ti, t)
                self.finish_attn(st, stB_, otok, otB, 4, ti)
                if self.chk(6):
                    return

    def chk(self, lvl):
        if STOP <= lvl:
            self.stopflag = True
            return True
        return False

    def finish_attn(self, st, stB_, otok, otB, e0, ti):
        S = self.S
        for half in range(2):
            po, pob = self.psO[half], self.pOB[half]
            pv = po[:, 0:260].rearrange("p (h d) -> p h d", h=4)
            S.op("dve", [pob], [stB_], lambda e, pv=pv, half=half: e.reciprocal(
                st[:, 8 + half * 4:12 + half * 4].unsqueeze(2), pv[:, :, 64:65]), partial=True)
            S.op("dve", [pob, stB_], [otB], lambda e, pv=pv, half=half: e.tensor_tensor(
                out=otok[:, half * 256:(half + 1) * 256].rearrange("p (h d) -> p h d", h=4), in0=pv[:, :, 0:64],
                in1=st[:, 8 + half * 4:12 + half * 4].unsqueeze(2).to_broadcast([128, 4, 64]), op=ALU.mult),
                partial=True)

        def dst(i0, n, ps_, pb_):
            S.op("act", [pb_], [self.T8b], lambda e: e.activation(
                out=self.T8[:, e0:e0 + 4, ti * 128:(ti + 1) * 128],
                in_=ps_[:, 0:512].rearrange("p (a b) -> p a b", a=4), func=AF.Copy), partial=True)
        self.transposes_to(lambda q: otok[:, q * 128:(q + 1) * 128], 4, [otB], dst, [self.T8b])

    def layer_norm_group(self, tiles):
        S = self.S
        sl = lambda ti: self.small[:, ti * 16:(ti + 1) * 16]
        for ti, t in enumerate(tiles):
            sm, smB, xb = sl(ti), self.smBs[ti], self.xB[t]
            S.op("dve", [xb], [smB], lambda e, sm=sm, t=t: e.bn_stats(sm[:, 0:6], self.x[:, t, 0:512]), partial=True)
            S.op("dve", [xb], [smB], lambda e, sm=sm, t=t: e.bn_stats(sm[:, 6:12], self.x[:, t, 512:1024]), partial=True)
            S.op("dve", [smB], [smB], lambda e, sm=sm: e.bn_aggr(sm[:, 12:14], sm[:, 0:12]))
        for ti, t in enumerate(tiles):
            sm, smB = sl(ti), self.smBs[ti]
            S.op("act", [smB, self.constb], [smB], lambda e, sm=sm: e.activation(
                out=sm[:, 14:15], in_=sm[:, 13:14], func=AF.Sqrt, bias=self.cbias[:, 0:1], scale=1.0))
        for ti, t in enumerate(tiles):
            sm, smB = sl(ti), self.smBs[ti]
            S.op("dve", [smB], [smB], lambda e, sm=sm: e.reciprocal(sm[:, 15:16], sm[:, 14:15]))
        for ti, t in enumerate(tiles):
            sm, smB, xb = sl(ti), self.smBs[ti], self.xB[t]
            xt = self.x[:, t, :]
            S.op("dve", [smB, xb, self.lngB], [xb], lambda e, sm=sm, xt=xt: e.scalar_tensor_tensor(
                out=xt, in0=xt, scalar=sm[:, 12:13], in1=self.lng[:, 0, :], op0=ALU.subtract, op1=ALU.mult))
            S.op("dve", [smB, xb, self.lngB], [xb], lambda e, sm=sm, xt=xt: e.scalar_tensor_tensor(
                out=xt, in0=xt, scalar=sm[:, 15:16], in1=self.lng[:, 1, :], op0=ALU.mult, op1=ALU.add))

    def load_ln(self, l, which):
        src = bass.AP(self.lnp_d, (l * 4 + which * 2) * D, [[0, 128], [D, 2], [1, D]])
        self.S.dma("sp", self.lng[:], src, [], [self.lngB])

    def phaseB(self, kind, j, l, tiles):
        S, nc = self.S, self.nc
        ng = len(tiles)
        ntok = ng * 128
        T8 = self.T8
        with ExitStack() as es:
            self.uid += 1
            A = lambda n, sh, d_: es.enter_context(nc.sbuf_tensor(f"{n}_{self.uid}", sh, d_))
            hidT = A("hidT", [128, 32, 512], BF)
            hidB = Buf("hidT")
            rsb = [A(f"rf{i}", [128, 512], F32) for i in range(2)]
            rB = [Buf(f"rf{i}") for i in range(2)]
            self.lng = A("lng", [128, 2, D], F32)
            self.lngB = Buf("lng")
            first = True
            for b in range(2):
                wt, wb = self.wchunk("out", l, b)
                if first:
                    S.barrier(["sp"])
                    self.load_ln(l, 0)
                    first = False
                for ti, t in enumerate(tiles):
                    ps, pb = self.nxt("A")
                    for k in range(8):
                        S.op("pe", [self.T8b, wb], [pb], lambda e, k=k, ti=ti: e.matmul(
                            ps[:, 0:WCH], T8[:, k, ti * 128:(ti + 1) * 128], wt[:, k, :], start=(k == 0), stop=(k == 7)))
                    xs_ = self.x[:, t, b * WCH:(b + 1) * WCH]
                    S.op("dve", [pb, self.xB[t]], [self.xB[t]], lambda e, xs_=xs_: e.scalar_tensor_tensor(
                        out=xs_, in0=xs_, scalar=ALPHA, in1=ps[:, 0:WCH], op0=ALU.mult, op1=ALU.add), partial=True)
            self.layer_norm_group(tiles)
            if self.chk(7):
                return
            self.make_xT(tiles)
            self.load_ln(l, 1)
            ri = 0
            for c in range(8):
                wt, wb = self.wchunk("up", l, c)
                for fc in range(4):
                    ps, pb = self.nxt("A")
                    for k in range(8):
                        S.op("pe", [self.T8b, wb], [pb], lambda e, k=k, fc=fc: e.matmul(
                            ps[:, 0:ntok], wt[:, k, fc * 128:(fc + 1) * 128], T8[:, k, 0:ntok], start=(k == 0),
                            stop=(k == 7)))
                    r_, rb_ = rsb[ri % 2], rB[ri % 2]
                    ri += 1
                    S.op("act", [pb], [rb_], lambda e, r_=r_: e.activation(out=r_[:, 0:ntok], in_=ps[:, 0:ntok],
                                                                           func=AF.Relu))
                    S.op("pool", [rb_], [hidB], lambda e, r_=r_, c=c, fc=fc: e.tensor_tensor(
                        out=hidT[:, c * 4 + fc, 0:ntok], in0=r_[:, 0:ntok], in1=r_[:, 0:ntok], op=ALU.mult),
                        partial=True)
            accs = [(self.psO[0], self.pOB[0]), (self.psO[1], self.pOB[1]), (self.psS[0], self.pSB[0]),
                    (self.psS[1], self.pSB[1])]
            for b in range(2):
                for fg in range(4):
                    wt, wb = self.wchunk("down", l, b, fg)
                    for jj in range(8):
                        f = fg * 8 + jj
                        for ti, t in enumerate(tiles):
                            ps, pb = accs[ti]
                            S.op("pe", [hidB, wb], [pb], lambda e, ps=ps, jj=jj, f=f, ti=ti: e.matmul(
                                ps[:, 0:WCH], hidT[:, f, ti * 128:(ti + 1) * 128], wt[:, jj, :], start=(f == 0),
                                stop=(f == 31)))
                for ti, t in enumerate(tiles):
                    ps, pb = accs[ti]
                    xs_ = self.x[:, t, b * WCH:(b + 1) * WCH]
                    S.op("dve", [pb, self.xB[t]], [self.xB[t]], lambda e, xs_=xs_, ps=ps: e.scalar_tensor_tensor(
                        out=xs_, in0=xs_, scalar=ALPHA, in1=ps[:, 0:WCH], op0=ALU.mult, op1=ALU.add), partial=True)
            self.layer_norm_group(tiles)
            S.wait_all_dma("pool") if False else None


def rope_tables():
    half = 32
    inv = (np.float32(10000.0) ** (-np.arange(half, dtype=np.float32) / np.float32(half))).astype(np.float32)
    tab = np.zeros((NHIST, 128, 96), np.float32)
    for t in range(NHIST):
        if t < 16:
            pos = (t * 128 + np.arange(128)).astype(np.float32)
        else:
            pos = (2048 + np.arange(128)).astype(np.float32)
        ang = pos[:, None] * inv[None, :]
        tab[t, :, 0:32] = np.cos(ang)
        tab[t, :, 32:64] = np.sin(ang)
        tab[t, :, 64:96] = -np.sin(ang)
    return tab


_NC_CACHE = {}


def get_nc(npj, nsj, nl):
    key = (npj, nsj, nl)
    if key not in _NC_CACHE:
        _NC_CACHE[key] = Builder(npj, nsj, nl).build()
    return _NC_CACHE[key]


def make_in_maps(inp, ncores, npj, nsj):
    cs = rope_tables()
    lnp = np.ascontiguousarray(np.stack([inp["ln1_g"], inp["ln1_b"], inp["ln2_g"], inp["ln2_b"]], axis=1))
    relb = np.ascontiguousarray(inp["rel_bias"].reshape(32, 257))
    maps = []
    f = lambda a: np.ascontiguousarray(a, dtype=np.float32)
    for c in range(ncores):
        ps = slice(c * npj, (c + 1) * npj) if npj else slice(0, 1)
        ss = slice(c * nsj, (c + 1) * nsj) if nsj else slice(0, 1)
        maps.append({
            "xp": f(inp["x_prompt"][ps]), "xs": f(inp["x_sample"][ss]),
            "cak": f(inp["cache_a_k"][:, ss].reshape(4, -1, 512, 512)),
            "cav": f(inp["cache_a_v"][:, ss].reshape(4, -1, 512, 512)),
            "cbk": f(inp["cache_b_k"][:, ss].reshape(4, -1, 2048, 512)),
            "cbv": f(inp["cache_b_v"][:, ss].reshape(4, -1, 2048, 512)),
            "cbi": f(inp["cache_b_kidx"][:, ss]),
            "w_in": f(inp["w_in"]), "w_out": f(inp["w_out"]), "w_up": f(inp["w_up"]), "w_down": f(inp["w_down"]),
            "relb": relb, "lnp": lnp, "cs_tab": cs,
        })
    return maps


def kernel(x_prompt, x_sample, cache_a_k, cache_a_v, cache_b_k, cache_b_v, cache_b_kidx,
           w_in, rel_bias, w_out, ln1_g, ln1_b, w_up, w_down, ln2_g, ln2_b):
    inp = dict(x_prompt=np.asarray(x_prompt), x_sample=np.asarray(x_sample), cache_a_k=np.asarray(cache_a_k),
               cache_a_v=np.asarray(cache_a_v), cache_b_k=np.asarray(cache_b_k), cache_b_v=np.asarray(cache_b_v),
               cache_b_kidx=np.asarray(cache_b_kidx), w_in=np.asarray(w_in), rel_bias=np.asarray(rel_bias),
               w_out=np.asarray(w_out), ln1_g=np.asarray(ln1_g), ln1_b=np.asarray(ln1_b), w_up=np.asarray(w_up),
               w_down=np.asarray(w_down), ln2_g=np.asarray(ln2_g), ln2_b=np.asarray(ln2_b))
    ncores, npj, nsj = 8, 4, 2
    nc = get_nc(npj, nsj, 4)
    maps = make_in_maps(inp, ncores, npj, nsj)
    res = run_bass_kernel_spmd(nc, maps, core_ids=list(range(ncores)))
    R = res.results
    cat = lambda name, axis: np.concatenate([np.asarray(r[name]) for r in R], axis=axis)
    y_p = cat("yp", 0)
    y_s = cat("ys", 0)
    akp = cat("akp", 1).reshape(4, 32, 512, 8, 64)
    avp = cat("avp", 1).reshape(4, 32, 512, 8, 64)
    bkp = cat("bkp", 1).reshape(4, 32, 2048, 8, 64)
    bvp = cat("bvp", 1).reshape(4, 32, 2048, 8, 64)
    bip = cat("bip", 1)
    aks = cat("aks", 1).reshape(4, 16, 64, 8, 64)
    avs = cat("avs", 1).reshape(4, 16, 64, 8, 64)
    bks = cat("bks", 1).reshape(4, 16, 64, 8, 64)
    bvs = cat("bvs", 1).reshape(4, 16, 64, 8, 64)
    bis = cat("bis", 1)
    return (y_p, y_s, akp, avp, bkp, bvp, bip, aks, avs, bks, bvs, bis)
```
